# Optimizing a Trainium2 kernel written in Bass

```python
import math
import jax, jax.numpy as jnp
from jax import lax
import numpy as np

D_MODEL = 1024
BATCH = 16
SEQ = 2048
DEPTH = 2
DEC_BATCH = 4
DEC_SEQ = 8192
PAST_LEN = 128

N_META = 16
N_EVEN = (DEPTH + 1) // 2
N_ODD = DEPTH // 2
D_A = D_MODEL // 2
S5_GROUP = 16
G_A = D_A // S5_GROUP
S5_STATE = 64
D_B = D_MODEL // 2
H_B = 8
BW_B = D_B // H_B
LRU_C = 8.0
CONV_B = 4
CONV_B_LEFT = 2
D_IN_AB = D_A + 2 * D_B
D_MIX_AB = D_A + D_B
HEAD_DIM = 64
N_Q_HEADS = D_MODEL // HEAD_DIM
N_KV_HEADS = 4
GQ = N_Q_HEADS // N_KV_HEADS
D_QKV = (N_Q_HEADS + 2 * N_KV_HEADS) * HEAD_DIM
WINDOW = 128
BLOCK = 128
D_FF = 2816
CONV_F = 3
EPS = 1e-6
NEG = -1e30

kernel_name = 'hybrid_s5_rglru_swa_encoder'


def rmsnorm(x, g):
    xf = x.astype(jnp.float32)
    y = xf * lax.rsqrt(jnp.mean(xf * xf, axis=-1, keepdims=True) + EPS) * g.astype(jnp.float32)
    return y.astype(x.dtype)


def dwconv(x, w, b, left):
    k_width = w.shape[0]
    length = x.shape[1]
    xp = jnp.pad(x, ((0, 0), (left, k_width - 1 - left), (0, 0)))
    out = xp[:, 0:length] * w[0]
    for k in range(1, k_width):
        out = out + xp[:, k:k + length] * w[k]
    return out + b


def _linear_combine(e1, e2):
    a1, b1 = e1
    a2, b2 = e2
    return (a1 * a2, a2 * b1 + b2)


def s5_mixer(u, lam_re, lam_im, log_dt, b_re, b_im, c_re, c_im, d_skip, w_glu, b_glu):
    bsz, length, _ = u.shape
    uf = u.astype(jnp.float32).reshape(bsz, length, G_A, S5_GROUP)
    uc = uf.astype(jnp.complex64)
    y = uf * d_skip.astype(jnp.float32).reshape(G_A, S5_GROUP)
    for direction in range(2):
        lam = lax.complex(jnp.minimum(lam_re[direction].astype(jnp.float32), -1e-4),
                          lam_im[direction].astype(jnp.float32))
        dt = jnp.exp(log_dt[direction].astype(jnp.float32))[:, None]
        lam_bar = jnp.exp(lam * dt)
        b_mat = lax.complex(b_re[direction].astype(jnp.float32), b_im[direction].astype(jnp.float32))
        b_bar = ((lam_bar - 1.0) / lam)[:, :, None] * b_mat
        bu = jnp.einsum('blgc,gnc->blgn', uc, b_bar)
        a = jnp.broadcast_to(lam_bar, bu.shape)
        _, states = lax.associative_scan(_linear_combine, (a, bu), axis=1, reverse=(direction == 1))
        c_mat = lax.complex(c_re[direction].astype(jnp.float32), c_im[direction].astype(jnp.float32))
        y = y + jnp.real(jnp.einsum('blgn,gcn->blgc', states, c_mat))
    y = jax.nn.gelu(y.reshape(bsz, length, D_A))
    y = y * jax.nn.sigmoid(y @ w_glu.astype(jnp.float32) + b_glu.astype(jnp.float32))
    return y.astype(u.dtype)


def rglru_direction(x, w_r, b_r, w_i, b_i, lam, reverse):
    bsz, length, _ = x.shape
    xh = x.reshape(bsz, length, H_B, BW_B)
    r = jax.nn.sigmoid(jnp.einsum('blhi,hij->blhj', xh, w_r.astype(jnp.float32)).reshape(bsz, length, D_B) + b_r.astype(jnp.float32))
    i = jax.nn.sigmoid(jnp.einsum('blhi,hij->blhj', xh, w_i.astype(jnp.float32)).reshape(bsz, length, D_B) + b_i.astype(jnp.float32))
    log_a = -LRU_C * r * jax.nn.softplus(-lam.astype(jnp.float32))
    a = jnp.exp(log_a)
    inp = jnp.sqrt(-jnp.expm1(2.0 * log_a)) * (i * x)
    _, h = lax.associative_scan(_linear_combine, (a, inp), axis=1, reverse=reverse)
    return h


def rglru_mixer(xb, gate, conv_w, conv_b, w_r, b_r, w_i, b_i, lam):
    xc = dwconv(xb, conv_w, conv_b, CONV_B_LEFT).astype(jnp.float32)
    h = (rglru_direction(xc, w_r[0], b_r[0], w_i[0], b_i[0], lam[0], False)
         + rglru_direction(xc, w_r[1], b_r[1], w_i[1], b_i[1], lam[1], True))
    return (h * jax.nn.gelu(gate.astype(jnp.float32))).astype(xb.dtype)


def ab_layer(h, w_in, lam_re, lam_im, log_dt, b_re, b_im, c_re, c_im, d_skip, w_glu, b_glu,
             conv_w, conv_b, w_r, b_r, w_i, b_i, lam, w_out):
    z = h @ w_in
    u_a = z[..., :D_A]
    x_b = z[..., D_A:D_A + D_B]
    g_b = z[..., D_A + D_B:]
    y_a = s5_mixer(u_a, lam_re, lam_im, log_dt, b_re, b_im, c_re, c_im, d_skip, w_glu, b_glu)
    y_b = rglru_mixer(x_b, g_b, conv_w, conv_b, w_r, b_r, w_i, b_i, lam)
    return jnp.concatenate([y_a, y_b], axis=-1) @ w_out


def alibi_slopes():
    return 2.0 ** (-8.0 * jnp.arange(1, N_Q_HEADS + 1, dtype=jnp.float32) / N_Q_HEADS)


def windowed_gqa(h, w_qkv, w_o, sink):
    bsz, length, _ = h.shape
    qkv = h @ w_qkv
    q = qkv[..., :N_Q_HEADS * HEAD_DIM].reshape(bsz, length, N_KV_HEADS, GQ, HEAD_DIM)
    k = qkv[..., N_Q_HEADS * HEAD_DIM:(N_Q_HEADS + N_KV_HEADS) * HEAD_DIM].reshape(bsz, length, N_KV_HEADS, HEAD_DIM)
    v = qkv[..., (N_Q_HEADS + N_KV_HEADS) * HEAD_DIM:].reshape(bsz, length, N_KV_HEADS, HEAD_DIM)
    front = BLOCK - N_META
    lp = length + front
    nb = lp // BLOCK
    qp = jnp.pad(q, ((0, 0), (front, 0), (0, 0), (0, 0), (0, 0))).reshape(bsz, nb, BLOCK, N_KV_HEADS, GQ, HEAD_DIM)
    kp = jnp.pad(k, ((0, 0), (front + BLOCK, BLOCK), (0, 0), (0, 0))).reshape(bsz, nb + 2, BLOCK, N_KV_HEADS, HEAD_DIM)
    vp = jnp.pad(v, ((0, 0), (front + BLOCK, BLOCK), (0, 0), (0, 0))).reshape(bsz, nb + 2, BLOCK, N_KV_HEADS, HEAD_DIM)
    kb = jnp.concatenate([kp[:, :-2], kp[:, 1:-1], kp[:, 2:]], axis=2)
    vb = jnp.concatenate([vp[:, :-2], vp[:, 1:-1], vp[:, 2:]], axis=2)
    s = jnp.einsum('bnqhgd,bnkhd->bnhgqk', qp, kb).astype(jnp.float32) * (1.0 / math.sqrt(HEAD_DIM))
    qi = jnp.arange(BLOCK)
    ki = jnp.arange(3 * BLOCK)
    dist = jnp.abs(qi[:, None] + BLOCK - ki[None, :])
    key_pos = (jnp.arange(nb)[:, None] - 1) * BLOCK + ki[None, :]
    key_ok = (key_pos >= front) & (key_pos < lp)
    mask = (dist <= WINDOW)[None] & key_ok[:, None, :]
    slopes = alibi_slopes().reshape(N_KV_HEADS, GQ)
    s = s - slopes[:, :, None, None] * dist.astype(jnp.float32)
    s = jnp.where(mask[None, :, None, None], s, NEG)
    sink_l = sink.astype(jnp.float32).reshape(N_KV_HEADS, GQ)[None, None, :, :, None]
    m = jnp.maximum(jnp.max(s, axis=-1), sink_l)
    p = jnp.exp(s - m[..., None])
    denom = jnp.sum(p, axis=-1) + jnp.exp(sink_l - m)
    o = jnp.einsum('bnhgqk,bnkhd->bnqhgd', p.astype(vb.dtype), vb)
    o = o / jnp.moveaxis(denom, -1, 2)[..., None].astype(o.dtype)
    o = o.reshape(bsz, lp, N_Q_HEADS * HEAD_DIM)[:, front:]
    return o @ w_o


def conv_ffn(h, w_up, conv_w, conv_b, w_down):
    u = dwconv(h @ w_up, conv_w, conv_b, CONV_F // 2)
    a = u[..., :D_FF]
    g = u[..., D_FF:]
    return (jax.nn.gelu(g) * a) @ w_down


def trunk(x, meta_tokens, norm_mix_g, norm_ffn_g, final_norm_g, w_in_ab, s5_lambda_re, s5_lambda_im,
          s5_log_dt, s5_b_re, s5_b_im, s5_c_re, s5_c_im, s5_d, w_glu, b_glu, lru_conv_w, lru_conv_b,
          lru_w_r, lru_b_r, lru_w_i, lru_b_i, lru_lambda, w_out_ab, w_qkv, w_o, attn_sink,
          w_up, ffn_conv_w, ffn_conv_b, w_down):
    bsz = x.shape[0]
    meta = jnp.broadcast_to(meta_tokens[None].astype(x.dtype), (bsz, N_META, D_MODEL))
    h = jnp.concatenate([meta, x], axis=1)
    for layer in range(DEPTH):
        j = layer // 2
        hn = rmsnorm(h, norm_mix_g[layer])
        if layer % 2 == 0:
            h = h + ab_layer(hn, w_in_ab[j], s5_lambda_re[j], s5_lambda_im[j], s5_log_dt[j], s5_b_re[j],
                             s5_b_im[j], s5_c_re[j], s5_c_im[j], s5_d[j], w_glu[j], b_glu[j],
                             lru_conv_w[j], lru_conv_b[j], lru_w_r[j], lru_b_r[j], lru_w_i[j], lru_b_i[j],
                             lru_lambda[j], w_out_ab[j])
        else:
            h = h + windowed_gqa(hn, w_qkv[j], w_o[j], attn_sink[j])
        h = h + conv_ffn(rmsnorm(h, norm_ffn_g[layer]), w_up[layer], ffn_conv_w[layer], ffn_conv_b[layer], w_down[layer])
    h = rmsnorm(h, final_norm_g)
    return h[:, N_META:]


def setup_inputs(seed: int = 0) -> dict:
    key = jax.random.key(seed)
    ks = iter(jax.random.split(key, 40))
    f32 = jnp.float32

    def nrm(shape, scale):
        return jax.random.normal(next(ks), shape, f32) * scale

    x_prompt = nrm((BATCH, SEQ, D_MODEL), 1.0)
    x_sample = nrm((DEC_BATCH, DEC_SEQ, D_MODEL), 1.0)
    meta_tokens = nrm((N_META, D_MODEL), 1.0)
    norm_mix_g = 1.0 + nrm((DEPTH, D_MODEL), 0.02)
    norm_ffn_g = 1.0 + nrm((DEPTH, D_MODEL), 0.02)
    final_norm_g = 1.0 + nrm((D_MODEL,), 0.02)
    w_in_ab = nrm((N_EVEN, D_MODEL, D_IN_AB), D_MODEL ** -0.5)
    s5_lambda_re = -0.5 + nrm((N_EVEN, 2, G_A, S5_STATE), 0.01)
    s5_lambda_im = math.pi * jnp.arange(S5_STATE, dtype=f32) + nrm((N_EVEN, 2, G_A, S5_STATE), 0.01)
    s5_log_dt = jax.random.uniform(next(ks), (N_EVEN, 2, G_A), f32, math.log(1e-3), math.log(1e-1))
    s5_b_re = nrm((N_EVEN, 2, G_A, S5_STATE, S5_GROUP), (2 * S5_GROUP) ** -0.5)
    s5_b_im = nrm((N_EVEN, 2, G_A, S5_STATE, S5_GROUP), (2 * S5_GROUP) ** -0.5)
    s5_c_re = nrm((N_EVEN, 2, G_A, S5_GROUP, S5_STATE), S5_STATE ** -0.5)
    s5_c_im = nrm((N_EVEN, 2, G_A, S5_GROUP, S5_STATE), S5_STATE ** -0.5)
    s5_d = nrm((N_EVEN, D_A), 1.0)
    w_glu = nrm((N_EVEN, D_A, D_A), D_A ** -0.5)
    b_glu = nrm((N_EVEN, D_A), 0.01)
    lru_conv_w = nrm((N_EVEN, CONV_B, D_B), CONV_B ** -0.5)
    lru_conv_b = nrm((N_EVEN, D_B), 0.01)
    lru_w_r = nrm((N_EVEN, 2, H_B, BW_B, BW_B), BW_B ** -0.5)
    lru_b_r = nrm((N_EVEN, 2, D_B), 0.01)
    lru_w_i = nrm((N_EVEN, 2, H_B, BW_B, BW_B), BW_B ** -0.5)
    lru_b_i = nrm((N_EVEN, 2, D_B), 0.01)
    a_c = jax.random.uniform(next(ks), (N_EVEN, 2, D_B), f32, 0.9, 0.999)
    sig = a_c ** (1.0 / LRU_C)
    lru_lambda = jnp.log(sig) - jnp.log1p(-sig)
    w_out_ab = nrm((N_EVEN, D_MIX_AB, D_MODEL), D_MIX_AB ** -0.5)
    w_qkv = nrm((N_ODD, D_MODEL, D_QKV), D_MODEL ** -0.5)
    w_o = nrm((N_ODD, N_Q_HEADS * HEAD_DIM, D_MODEL), (N_Q_HEADS * HEAD_DIM) ** -0.5)
    attn_sink = nrm((N_ODD, N_Q_HEADS), 0.5)
    w_up = nrm((DEPTH, D_MODEL, 2 * D_FF), D_MODEL ** -0.5)
    ffn_conv_w = nrm((DEPTH, CONV_F, 2 * D_FF), CONV_F ** -0.5)
    ffn_conv_b = nrm((DEPTH, 2 * D_FF), 0.01)
    w_down = nrm((DEPTH, D_FF, D_MODEL), D_FF ** -0.5)
    return {'x_prompt': x_prompt, 'x_sample': x_sample, 'meta_tokens': meta_tokens,
            'norm_mix_g': norm_mix_g, 'norm_ffn_g': norm_ffn_g, 'final_norm_g': final_norm_g,
            'w_in_ab': w_in_ab, 's5_lambda_re': s5_lambda_re, 's5_lambda_im': s5_lambda_im,
            's5_log_dt': s5_log_dt, 's5_b_re': s5_b_re, 's5_b_im': s5_b_im, 's5_c_re': s5_c_re,
            's5_c_im': s5_c_im, 's5_d': s5_d, 'w_glu': w_glu, 'b_glu': b_glu,
            'lru_conv_w': lru_conv_w, 'lru_conv_b': lru_conv_b, 'lru_w_r': lru_w_r, 'lru_b_r': lru_b_r,
            'lru_w_i': lru_w_i, 'lru_b_i': lru_b_i, 'lru_lambda': lru_lambda, 'w_out_ab': w_out_ab,
            'w_qkv': w_qkv, 'w_o': w_o, 'attn_sink': attn_sink, 'w_up': w_up,
            'ffn_conv_w': ffn_conv_w, 'ffn_conv_b': ffn_conv_b, 'w_down': w_down}


def reference(x_prompt, x_sample, meta_tokens, norm_mix_g, norm_ffn_g, final_norm_g, w_in_ab,
              s5_lambda_re, s5_lambda_im, s5_log_dt, s5_b_re, s5_b_im, s5_c_re, s5_c_im, s5_d,
              w_glu, b_glu, lru_conv_w, lru_conv_b, lru_w_r, lru_b_r, lru_w_i, lru_b_i, lru_lambda,
              w_out_ab, w_qkv, w_o, attn_sink, w_up, ffn_conv_w, ffn_conv_b, w_down):
    params = (meta_tokens, norm_mix_g, norm_ffn_g, final_norm_g, w_in_ab, s5_lambda_re, s5_lambda_im,
              s5_log_dt, s5_b_re, s5_b_im, s5_c_re, s5_c_im, s5_d, w_glu, b_glu, lru_conv_w, lru_conv_b,
              lru_w_r, lru_b_r, lru_w_i, lru_b_i, lru_lambda, w_out_ab, w_qkv, w_o, attn_sink,
              w_up, ffn_conv_w, ffn_conv_b, w_down)
    y_prompt = trunk(x_prompt, *params)
    y_sample = trunk(x_sample, *params)
    return (y_prompt, y_sample)
```

```python
import math
import os
import numpy as np
from contextlib import ExitStack
import concourse.bass as bass
import concourse.mybir as mybir
from concourse.bass_utils import run_bass_kernel_spmd

F32 = mybir.dt.float32
BF16 = mybir.dt.bfloat16
I32 = mybir.dt.int32
AF = mybir.ActivationFunctionType
ALU = mybir.AluOpType
AX = mybir.AxisListType

D = 1024
NSEG = 4
SEG = 2064
T = NSEG * SEG
TT = 344
NTILE = T // TT
CH = 1032
NCH = T // CH
DFF = 2816
NJ = DFF // 128
NEG = -1e30
EPS = 1e-6
PADLO = 8208
NDMASEM = 24
SEMCAP = 30000


class Buf:
    __slots__ = ("name", "w", "r")

    def __init__(self, name=""):
        self.name = name
        self.w = None
        self.r = []


class Rec:
    def __init__(self):
        self.calls = []

    def __getattr__(self, name):
        def f(*a, **k):
            self.calls.append((name, a, k))
            return self
        return f


class Plan:
    ENGS = ("pe", "dve", "act", "pool", "sp")

    def __init__(self, nc, es):
        self.nc = nc
        self.es = es
        self.items = {e: [] for e in self.ENGS}
        self.sems = {}
        self.epoch = {e: 0 for e in self.ENGS}
        self.count = {}
        for e in self.ENGS:
            self._newsem((e, 0))
        for i in range(NDMASEM):
            self._newsem(("dma", i))
        self.known = {e: {} for e in self.ENGS}
        self.ndma = 0
        self.nwaits = 0
        self.nops = 0

    def _newsem(self, key):
        self.sems[key] = self.es.enter_context(self.nc.semaphore("s%d" % len(self.sems)))
        self.count[key] = 0

    def _deps(self, eng, reads, writes):
        need = {}
        for b in reads:
            if b.w is not None:
                k, v = b.w
                if need.get(k, 0) < v:
                    need[k] = v
        for b in writes:
            if b.w is not None:
                k, v = b.w
                if need.get(k, 0) < v:
                    need[k] = v
            for k, v in b.r:
                if need.get(k, 0) < v:
                    need[k] = v
        waits = []
        kn = self.known[eng]
        for k, v in need.items():
            if kn.get(k, 0) >= v:
                continue
            kn[k] = v
            waits.append((k, v))
        self.nwaits += len(waits)
        return waits

    def _commit(self, sig, reads, writes):
        for b in reads:
            b.r.append(sig)
        for b in writes:
            b.w = sig
            b.r = []

    def op(self, eng, fn, reads=(), writes=()):
        waits = self._deps(eng, reads, writes)
        key = (eng, self.epoch[eng])
        if self.count[key] >= SEMCAP:
            self.epoch[eng] += 1
            key = (eng, self.epoch[eng])
            self._newsem(key)
        self.count[key] += 1
        val = self.count[key]
        rec = Rec()
        fn(rec)
        assert rec.calls
        self.items[eng].append((waits, rec.calls, (key, 1)))
        self._commit((key, val), reads, writes)
        self.nops += 1

    def dma(self, out, in_, reads=(), writes=(), eng="sp"):
        i = self.ndma % NDMASEM
        self.ndma += 1
        key = ("dma", i)
        waits = self._deps(eng, reads, writes)
        prev = self.count[key]
        if prev > 0 and self.known[eng].get(key, 0) < prev:
            self.known[eng][key] = prev
            waits.append((key, prev))
        self.count[key] += 16
        val = self.count[key]
        self.items[eng].append((waits, [("dma_start", (), {"out": out, "in_": in_})], (key, 16)))
        self._commit((key, val), reads, writes)

    def barrier(self):
        for eng in self.ENGS:
            waits = []
            for k, v in self.count.items():
                if v > 0 and self.known[eng].get(k, 0) < v:
                    self.known[eng][k] = v
                    waits.append((k, v))
            self.items[eng].append((waits, None, None))

    def emit(self, block):
        plan = self

        def run(engname):
            def f(e):
                for waits, fn, inc in plan.items[engname]:
                    for k, v in waits:
                        e.wait_ge(plan.sems[k], v)
                    if fn is None:
                        continue
                    for name, a, k in fn:
                        ins = getattr(e, name)(*a, **k)
                    ins.then_inc(plan.sems[inc[0]], inc[1])
            return f

        block.tensor(run("pe"))
        block.vector(run("dve"))
        block.scalar(run("act"))
        block.gpsimd(run("pool"))
        block.sync(run("sp"))


class Tl:
    def __init__(self, t, nb=1, name=""):
        self.t = t
        self.bs = [Buf(name + str(i)) for i in range(nb)]
        self.b = self.bs[0]

    def __getitem__(self, k):
        return self.t[k]


class Ctx:
    def __init__(self, nc, P):
        self.nc = nc
        self.P = P
        self.n = 0

    def sb(self, es, shape, dt=F32, nb=1):
        self.n += 1
        return Tl(es.enter_context(self.nc.sbuf_tensor("sb%d" % self.n, list(shape), dt)), nb, "sb%d_" % self.n)

    def ps(self, es, shape, dt=F32, nb=1):
        self.n += 1
        return Tl(es.enter_context(self.nc.psum_tensor("ps%d" % self.n, list(shape), dt)), nb, "ps%d_" % self.n)


def bc(ap, shape):
    return ap.to_broadcast(list(shape))


def pipeline(n, stage_fns):
    ns = len(stage_fns)
    for step in range(n + ns - 1):
        for si, f in enumerate(stage_fns):
            it = step - si
            if 0 <= it < n:
                f(it)


def build(debug=False):
    nc = bass.Bass("TRN2", target_bir_lowering=False)
    ein = lambda n, s, d=F32: nc.dram_tensor(n, list(s), d, kind="ExternalInput").ap()
    xs = ein("xs", [T, D])
    flags_d = ein("flags", [128, 4])
    g_mix = ein("norm_mix_g", [2, D]); g_ffn = ein("norm_ffn_g", [2, D]); g_fin = ein("final_norm_g", [D])
    w_in = ein("w_in_ab", [1, D, 1536])
    lam_re = ein("s5_lambda_re", [1, 2, 32, 64]); lam_im = ein("s5_lambda_im", [1, 2, 32, 64])
    log_dt = ein("s5_log_dt", [1, 2, 32])
    b_re = ein("s5_b_re", [1, 2, 32, 64, 16]); b_im = ein("s5_b_im", [1, 2, 32, 64, 16])
    c_re = ein("s5_c_re", [1, 2, 32, 16, 64]); c_im = ein("s5_c_im", [1, 2, 32, 16, 64])
    s5_d = ein("s5_d", [1, 512]); w_glu = ein("w_glu", [1, 512, 512]); b_glu = ein("b_glu", [1, 512])
    cwB = ein("lru_conv_w", [1, 4, 512]); cbB = ein("lru_conv_b", [1, 512])
    w_r = ein("lru_w_r", [1, 2, 8, 64, 64]); b_r = ein("lru_b_r", [1, 2, 512])
    w_i = ein("lru_w_i", [1, 2, 8, 64, 64]); b_i = ein("lru_b_i", [1, 2, 512])
    lru_lam = ein("lru_lambda", [1, 2, 512])
    w_out = ein("w_out_ab", [1, D, D]); w_qkv = ein("w_qkv", [1, D, 1536]); w_o = ein("w_o", [1, D, D])
    sink_d = ein("attn_sink", [1, 16])
    w_up = ein("w_up", [2, D, 2 * DFF]); cwF = ein("ffn_conv_w", [2, 3, 2 * DFF]); cbF = ein("ffn_conv_b", [2, 2 * DFF])
    w_down = ein("w_down", [2, DFF, D])
    ys = nc.dram_tensor("ys", [T, D], F32, kind="ExternalOutput").ap()
    skind = "ExternalOutput" if debug else "Internal"
    scr = lambda n, s, d=F32: nc.dram_tensor(n, list(s), d, kind=skind).ap()
    hs0 = scr("hs0", [8, 128, T]); zs = scr("zs", [12, 128, T]); mixs = scr("mixs", [8, 128, T], BF16)
    hs2 = scr("hs2", [8, 128, T]); hs3 = scr("hs3", [8, 128, T])
    hs0B = Buf("hs0"); zsB = Buf("zs"); mixB = Buf("mixs"); hs2B = Buf("hs2"); hs3B = Buf("hs3"); ysB = Buf("ys")

    with ExitStack() as top:
        P = Plan(nc, top)
        C = Ctx(nc, P)
        op = P.op
        nc_allow = top.enter_context(nc.allow_non_contiguous_dma(reason="small parameter layouts"))

        flg = C.sb(top, [128, 4])
        ident = C.sb(top, [128, 128]); identb = C.sb(top, [128, 128], BF16); onesb = C.sb(top, [128, 128], BF16)
        gmix = C.sb(top, [128, 2, 8]); gffn = C.sb(top, [128, 2, 8]); gfin = C.sb(top, [128, 8])
        P.dma(flg[:], flags_d[:, :], writes=[flg.b])
        P.dma(gmix[:], g_mix.rearrange("l (k p) -> p l k", p=128), writes=[gmix.b])
        P.dma(gffn[:], g_ffn.rearrange("l (k p) -> p l k", p=128), writes=[gffn.b])
        P.dma(gfin[:], g_fin.rearrange("(k p) -> p k", p=128), writes=[gfin.b])
        fcol = flg[:, 0:1]; nfcol = flg[:, 1:2]; nbcol = flg[:, 2:3]; npcol = flg[:, 3:4]
        with ExitStack() as es:
            it = C.sb(es, [128, 128], I32); ip = C.sb(es, [128, 1], I32); itf = C.sb(es, [128, 128]); ipf = C.sb(es, [128, 1])
            op("pool", lambda e: e.iota(it[:], pattern=[[1, 128]], base=0, channel_multiplier=0), writes=[it.b])
            op("pool", lambda e: e.iota(ip[:], pattern=[[0, 1]], base=0, channel_multiplier=1), writes=[ip.b])
            op("dve", lambda e: e.tensor_copy(out=itf[:], in_=it[:]), [it.b], [itf.b])
            op("dve", lambda e: e.tensor_copy(out=ipf[:], in_=ip[:]), [ip.b], [ipf.b])
            op("dve", lambda e: e.tensor_scalar(out=ident[:], in0=itf[:], scalar1=ipf[:, 0:1], scalar2=None, op0=ALU.is_equal),
               [itf.b, ipf.b], [ident.b])
            op("dve", lambda e: e.tensor_copy(out=identb[:], in_=ident[:]), [ident.b], [identb.b])
            op("dve", lambda e: e.memset(onesb[:], 1.0), [], [onesb.b])
            P.barrier()

        def load_cast(es_, dst, dstbuf, src_ap, ncols, scale_ap=None, scale_buf=None, eng_cycle=("dve", "act", "pool"), stg=None, k=[0]):
            s = stg[k[0] % len(stg)]
            eng = eng_cycle[k[0] % len(eng_cycle)]
            k[0] += 1
            P.dma(s[:, 0:ncols], src_ap, writes=[s.b])
            rd = [s.b] + ([scale_buf] if scale_buf is not None else [])
            if scale_ap is None:
                if eng == "act":
                    op("act", lambda e: e.copy(out=dst, in_=s[:, 0:ncols]), rd, [dstbuf])
                else:
                    op(eng, lambda e: e.tensor_copy(out=dst, in_=s[:, 0:ncols]), rd, [dstbuf])
            else:
                if eng == "act":
                    op("act", lambda e: e.activation(out=dst, in_=s[:, 0:ncols], func=AF.Copy, scale=scale_ap), rd, [dstbuf])
                else:
                    op(eng, lambda e: e.tensor_scalar(out=dst, in0=s[:, 0:ncols], scalar1=scale_ap, scalar2=None, op0=ALU.mult), rd, [dstbuf])

        def rmsnorm(es_, h, hbufs, hn, hnbuf, n, pss, sq, rstd, gscale=None):
            for k in range(8):
                eng = "act" if k % 2 == 0 else "pool"
                if eng == "act":
                    op("act", lambda e, k=k: e.activation(out=sq[:, k, 0:n], in_=h[:, k, 0:n], func=AF.Square), [hbufs[k]], [sq.bs[k]])
                else:
                    op("pool", lambda e, k=k: e.tensor_tensor(out=sq[:, k, 0:n], in0=h[:, k, 0:n], in1=h[:, k, 0:n], op=ALU.mult), [hbufs[k]], [sq.bs[k]])

            def mm(e):
                for k in range(8):
                    ins = e.matmul(pss[:, 0:n], lhsT=onesb[:], rhs=sq[:, k, 0:n], start=(k == 0), stop=(k == 7))
                return ins
            op("pe", mm, list(sq.bs) + [onesb.b], [pss.b])
            op("act", lambda e: e.activation(out=rstd[:, 0:n], in_=pss[:, 0:n], func=AF.Sqrt, scale=1.0 / D, bias=EPS), [pss.b], [rstd.b])
            op("dve", lambda e: e.reciprocal(out=rstd[:, 0:n], in_=rstd[:, 0:n]), [rstd.b], [rstd.b])
            if gscale is None:
                op("dve", lambda e: e.tensor_tensor(out=hn[:, :, 0:n], in0=h[:, :, 0:n], in1=bc(rstd[:, 0:n].unsqueeze(1), [128, 8, n]), op=ALU.mult),
                   list(hbufs) + [rstd.b], [hnbuf])
            else:
                for k in range(8):
                    op("dve", lambda e, k=k: e.scalar_tensor_tensor(out=hn[:, k, 0:n], in0=h[:, k, 0:n], scalar=gscale[:, k:k + 1], in1=rstd[:, 0:n],
                                                                  op0=ALU.mult, op1=ALU.mult), [hbufs[k], rstd.b], [hnbuf])

        def tile_cols(t0):
            lo = max(t0 - 1, 0); hi = min(t0 + TT + 1, T)
            return lo, hi, lo - (t0 - 1)

        def halo_fix(hn, hnbuf, t0):
            N = TT + 2
            if t0 == 0:
                op("dve", lambda e: e.memset(hn[:, :, 0:1], 0.0), [], [hnbuf])
            elif t0 % SEG == 0:
                op("dve", lambda e: e.tensor_scalar(out=hn[:, :, 0:1], in0=hn[:, :, 0:1], scalar1=fcol, scalar2=None, op0=ALU.mult), [hnbuf, flg.b], [hnbuf])
            if t0 + TT == T:
                op("dve", lambda e: e.memset(hn[:, :, N - 1:N], 0.0), [], [hnbuf])
                c0 = PADLO - (t0 - 1)
                op("dve", lambda e: e.tensor_scalar(out=hn[:, :, c0:N - 1], in0=hn[:, :, c0:N - 1], scalar1=nfcol, scalar2=None, op0=ALU.mult), [hnbuf, flg.b], [hnbuf])
            elif (t0 + TT) % SEG == 0:
                op("dve", lambda e: e.tensor_scalar(out=hn[:, :, N - 1:N], in0=hn[:, :, N - 1:N], scalar1=fcol, scalar2=None, op0=ALU.mult), [hnbuf, flg.b], [hnbuf])

        def ffn_setup(es, layer):
            W = {}
            W["up"] = C.sb(es, [128, 8, 2 * DFF], BF16, nb=8)
            W["dn"] = C.sb(es, [128, NJ, D], BF16, nb=NJ)
            W["cw"] = C.sb(es, [128, 3, 44]); W["cb"] = C.sb(es, [128, 44])
            P.dma(W["cw"][:], cwF[layer].rearrange("k (j p) -> p k j", p=128), writes=[W["cw"].b])
            P.dma(W["cb"][:], cbF[layer].rearrange("(j p) -> p j", p=128), writes=[W["cb"].b])
            with ExitStack() as es2:
                stg = [C.sb(es2, [128, 2816]) for _ in range(3)]
                for k in range(8):
                    for hlf in range(2):
                        load_cast(es2, W["up"][:, k, hlf * 2816:(hlf + 1) * 2816], W["up"].bs[k], w_up[layer, k * 128:(k + 1) * 128, hlf * 2816:(hlf + 1) * 2816], 2816,
                                  scale_ap=gffn[:, layer, k:k + 1], scale_buf=gffn.b, stg=stg)
                for j in range(NJ):
                    load_cast(es2, W["dn"][:, j, :], W["dn"].bs[j], w_down[layer, j * 128:(j + 1) * 128, :], 1024, stg=stg)
                P.barrier()
            return W

        def ffn_alloc(es):
            A = {}
            A["pa"] = [C.ps(es, [128, 512]) for _ in range(3)]
            A["pg"] = [C.ps(es, [128, 512]) for _ in range(3)]
            A["pd"] = [C.ps(es, [128, 512]) for _ in range(2)]
            A["ac"] = [C.sb(es, [128, TT]) for _ in range(4)]
            A["gc"] = [C.sb(es, [128, TT]) for _ in range(4)]
            A["t1"] = [C.sb(es, [128, TT]) for _ in range(3)]
            A["t2"] = [C.sb(es, [128, TT]) for _ in range(2)]
            A["m"] = C.sb(es, [128, NJ, TT], BF16, nb=NJ)
            return A

        def ffn_body(W, A, hn, hnbuf, hres, hresbufs, hook=None, hook2=None):
            N = TT + 2
            cw = W["cw"]; cb = W["cb"]

            def bufs(j):
                return (A["pa"][j % 3], A["pg"][j % 3], A["ac"][j % 4], A["gc"][j % 4], A["t1"][j % 3], A["t2"][j % 2])

            def s0(j):
                pa, pg, ac, gc, t1, t2 = bufs(j)

                def mm(e, col, pt):
                    for k in range(8):
                        ins = e.matmul(pt[:, 0:N], lhsT=W["up"][:, k, col * 128:(col + 1) * 128], rhs=hn[:, k, 0:N], start=(k == 0), stop=(k == 7))
                    return ins
                op("pe", lambda e: mm(e, j, pa), list(W["up"].bs) + [hnbuf], [pa.b])
                op("pe", lambda e: mm(e, NJ + j, pg), list(W["up"].bs) + [hnbuf], [pg.b])

            def s1(j):
                pa, pg, ac, gc, t1, t2 = bufs(j)
                for (pt, dst, col) in ((pg, gc, NJ + j), (pa, ac, j)):
                    op("act", lambda e: e.activation(out=dst[:], in_=pt[:, 1:TT + 1], func=AF.Identity, scale=cw[:, 1, col:col + 1], bias=cb[:, col:col + 1]), [pt.b, cw.b, cb.b], [dst.b])
                    op("dve", lambda e: e.scalar_tensor_tensor(out=dst[:], in0=pt[:, 0:TT], scalar=cw[:, 0, col:col + 1], in1=dst[:], op0=ALU.mult, op1=ALU.add), [pt.b, cw.b, dst.b], [dst.b])
                    op("dve", lambda e: e.scalar_tensor_tensor(out=dst[:], in0=pt[:, 2:TT + 2], scalar=cw[:, 2, col:col + 1], in1=dst[:], op0=ALU.mult, op1=ALU.add), [pt.b, cw.b, dst.b], [dst.b])

            def s2(j):
                pa, pg, ac, gc, t1, t2 = bufs(j)
                op("act", lambda e: e.activation(out=t1[:], in_=gc[:], func=AF.Gelu_apprx_tanh), [gc.b], [t1.b])

            def s3(j):
                pa, pg, ac, gc, t1, t2 = bufs(j)
                op("pool", lambda e: e.tensor_tensor(out=A["m"][:, j, :], in0=t1[:], in1=ac[:], op=ALU.mult), [t1.b, ac.b], [A["m"].bs[j]])

            stg_ = [s0, s1, s2, s3]
            for step in range(NJ + len(stg_) - 1):
                for si, f in enumerate(stg_):
                    it = step - si
                    if 0 <= it < NJ:
                        f(it)
                        if si == 0 and it == NJ - 1 and hook is not None:
                            hook()
                if step == 6 and hook2 is not None:
                    hook2()
            for o in range(8):
                pd = A["pd"][o % 2]

                def mmd(e, o=o, pd=pd):
                    for j in range(NJ):
                        ins = e.matmul(pd[:, 0:TT], lhsT=W["dn"][:, j, o * 128:(o + 1) * 128], rhs=A["m"][:, j, :], start=(j == 0), stop=(j == NJ - 1))
                    return ins
                op("pe", mmd, list(W["dn"].bs) + list(A["m"].bs), [pd.b])
                op("dve", lambda e, o=o, pd=pd: e.tensor_tensor(out=hres[:, o, 1:TT + 1], in0=pd[:, 0:TT], in1=hres[:, o, 1:TT + 1], op=ALU.add),
                   [pd.b, hresbufs[o]], [hresbufs[o]])

        with ExitStack() as es:
            win = C.sb(es, [128, 8, 1536], BF16, nb=8)
            with ExitStack() as es2:
                stg = [C.sb(es2, [128, 1536]) for _ in range(3)]
                for k in range(8):
                    load_cast(es2, win[:, k, :], win.bs[k], w_in[0, k * 128:(k + 1) * 128, :], 1536, scale_ap=gmix[:, 0, k:k + 1], scale_buf=gmix.b, stg=stg)
                P.barrier()
            xtok = [C.sb(es, [128, 3, D]) for _ in range(2)]
            h0 = [C.sb(es, [128, 8, TT], nb=8) for _ in range(2)]
            sq = C.sb(es, [128, 8, TT], BF16, nb=8); rstd = C.sb(es, [128, TT]); hnl = [C.sb(es, [128, 8, TT], BF16) for _ in range(2)]
            zt = [C.sb(es, [128, 12, TT], nb=12) for _ in range(2)]
            ptr = [C.ps(es, [128, 512]) for _ in range(2)]; pss = C.ps(es, [128, 512]); pz = [C.ps(es, [128, 512]) for _ in range(3)]
            blks = [(0, 128), (128, 128), (256, TT - 256)]

            def xload(ti):
                t0 = ti * TT; xt = xtok[ti % 2]
                for bi, (o, nb_) in enumerate(blks):
                    P.dma(xt[0:nb_, bi, :], xs[t0 + o:t0 + o + nb_, :], writes=[xt.b])

            def R0(ti):
                t0 = ti * TT; s = ti % 2
                xt = xtok[s]; h = h0[s]
                if ti == 0:
                    xload(0)
                if ti + 1 < NTILE:
                    xload(ti + 1)
                for k in range(8):
                    pt = ptr[k % 2]

                    def tr(e):
                        for bi, (o, nb_) in enumerate(blks):
                            ins = e.transpose(out=pt[:, o:o + nb_], in_=xt[0:nb_, bi, k * 128:(k + 1) * 128], identity=ident[0:nb_, 0:nb_])
                        return ins
                    op("pe", tr, [xt.b, ident.b], [pt.b])
                    if k % 2 == 0:
                        op("act", lambda e: e.copy(out=h[:, k, :], in_=pt[:, 0:TT]), [pt.b], [h.bs[k]])
                    else:
                        op("dve", lambda e: e.tensor_copy(out=h[:, k, :], in_=pt[:, 0:TT]), [pt.b], [h.bs[k]])
                P.dma(hs0.rearrange("k p t -> p k t")[:, :, t0:t0 + TT], h[:], reads=list(h.bs), writes=[hs0B])

            def R1(ti):
                h = h0[ti % 2]; hn = hnl[ti % 2]
                rmsnorm(es, h, h.bs, hn, hn.b, TT, pss, sq, rstd)

            def R2(ti):
                t0 = ti * TT; z = zt[ti % 2]; hn = hnl[ti % 2]
                for o in range(12):
                    pzz = pz[o % 3]

                    def mm(e):
                        for k in range(8):
                            ins = e.matmul(pzz[:, 0:TT], lhsT=win[:, k, o * 128:(o + 1) * 128], rhs=hn[:, k, :], start=(k == 0), stop=(k == 7))
                        return ins
                    op("pe", mm, list(win.bs) + [hn.b], [pzz.b])
                    if o % 2 == 0:
                        op("act", lambda e: e.copy(out=z[:, o, :], in_=pzz[:, 0:TT]), [pzz.b], [z.bs[o]])
                    else:
                        op("dve", lambda e: e.tensor_copy(out=z[:, o, :], in_=pzz[:, 0:TT]), [pzz.b], [z.bs[o]])
                P.dma(zs.rearrange("k p t -> p k t")[:, :, t0:t0 + TT], z[:], reads=list(z.bs), writes=[zsB])

            pipeline(NTILE, [R0, R1, R2])
            P.barrier()

        with ExitStack() as es:
            cw = C.sb(es, [128, 4, 4]); cbt = C.sb(es, [128, 4]); ncw = C.sb(es, [128, 4, 4])
            br = C.sb(es, [128, 2, 4]); bi_ = C.sb(es, [128, 2, 4]); cp = C.sb(es, [128, 2, 4])
            P.dma(cw[:], cwB[0].rearrange("k (q p) -> p k q", p=128), writes=[cw.b])
            P.dma(cbt[:], cbB[0].rearrange("(q p) -> p q", p=128), writes=[cbt.b])
            P.dma(br[:], b_r[0].rearrange("d (q p) -> p d q", p=128), writes=[br.b])
            P.dma(bi_[:], b_i[0].rearrange("d (q p) -> p d q", p=128), writes=[bi_.b])
            P.dma(cp[:], lru_lam[0].rearrange("d (q p) -> p d q", p=128), writes=[cp.b])
            op("act", lambda e: e.activation(out=cp[:], in_=cp[:], func=AF.Exp, scale=-1.0), [cp.b], [cp.b])
            op("act", lambda e: e.activation(out=cp[:], in_=cp[:], func=AF.Ln, bias=1.0), [cp.b], [cp.b])
            op("dve", lambda e: e.tensor_scalar(out=cp[:], in0=cp[:], scalar1=-8.0, scalar2=None, op0=ALU.mult), [cp.b], [cp.b])
            op("dve", lambda e: e.tensor_scalar(out=ncw[:], in0=cw[:], scalar1=nfcol, scalar2=-1.0, op0=ALU.mult, op1=ALU.mult), [cw.b, flg.b], [ncw.b])
            wg = C.sb(es, [128, 2, 2, 4, 128], BF16)
            with ExitStack() as es2:
                wst = C.sb(es2, [128, 2, 2, 4, 128])
                op("dve", lambda e: e.memset(wst[:], 0.0), [], [wst.b])
                for gi, wsrc in enumerate((w_r, w_i)):
                    for d in range(2):
                        for q in range(4):
                            for hh in range(2):
                                P.dma(wst[64 * hh:64 * hh + 64, gi, d, q, 64 * hh:64 * hh + 64], wsrc[0, d, 2 * q + hh, :, :], writes=[wst.b])
                op("dve", lambda e: e.tensor_copy(out=wg[:], in_=wst[:]), [wst.b], [wg.b])
                P.barrier()
            xb = C.sb(es, [128, T]); gb = C.sb(es, [128, T]); xc = C.sb(es, [128, T]); xcb = C.sb(es, [128, T], BF16); hsum = C.sb(es, [128, T])
            rr = [C.sb(es, [128, CH]) for _ in range(2)]; ii = [C.sb(es, [128, CH]) for _ in range(3)]
            aa = [C.sb(es, [128, CH]) for _ in range(2)]; ss_ = [C.sb(es, [128, CH]) for _ in range(2)]
            hb = [C.sb(es, [128, CH]) for _ in range(2)]
            carry = C.sb(es, [128, 1]); ybc = [C.sb(es, [128, CH], BF16) for _ in range(2)]
            pgr = [C.ps(es, [128, 3, 512]) for _ in range(2)]
            for q in range(4):
                if q == 0:
                    P.dma(xb[:], zs[4 + q, :, :], reads=[zsB], writes=[xb.b])
                P.dma(gb[:], zs[8 + q, :, :], reads=[zsB], writes=[gb.b])
                op("act", lambda e, q=q: e.activation(out=xc[:], in_=xb[:], func=AF.Identity, scale=cw[:, 2, q:q + 1], bias=cbt[:, q:q + 1]), [xb.b, cw.b, cbt.b], [xc.b])
                op("dve", lambda e, q=q: e.scalar_tensor_tensor(out=xc[:, 2:T], in0=xb[:, 0:T - 2], scalar=cw[:, 0, q:q + 1], in1=xc[:, 2:T], op0=ALU.mult, op1=ALU.add), [xb.b, xc.b, cw.b], [xc.b])
                op("dve", lambda e, q=q: e.scalar_tensor_tensor(out=xc[:, 1:T], in0=xb[:, 0:T - 1], scalar=cw[:, 1, q:q + 1], in1=xc[:, 1:T], op0=ALU.mult, op1=ALU.add), [xb.b, xc.b, cw.b], [xc.b])
                op("dve", lambda e, q=q: e.scalar_tensor_tensor(out=xc[:, 0:T - 1], in0=xb[:, 1:T], scalar=cw[:, 3, q:q + 1], in1=xc[:, 0:T - 1], op0=ALU.mult, op1=ALU.add), [xb.b, xc.b, cw.b], [xc.b])
                for sgi in range(1, NSEG):
                    B_ = sgi * SEG
                    for (to, fo, kk) in ((B_ - 1, B_, 3), (B_, B_ - 1, 1), (B_, B_ - 2, 0), (B_ + 1, B_ - 1, 0)):
                        op("dve", lambda e, to=to, fo=fo, kk=kk, q=q: e.scalar_tensor_tensor(out=xc[:, to:to + 1], in0=xb[:, fo:fo + 1], scalar=ncw[:, kk, q:q + 1], in1=xc[:, to:to + 1],
                                                                                       op0=ALU.mult, op1=ALU.add), [xb.b, xc.b, ncw.b], [xc.b])
                op("pool", lambda e: e.tensor_copy(out=xcb[:], in_=xc[:]), [xc.b], [xcb.b])
                if q + 1 < 4:
                    P.dma(xb[:], zs[4 + q + 1, :, :], reads=[zsB], writes=[xb.b])
                for d in range(2):
                    order = list(range(NCH)) if d == 0 else list(range(NCH - 1, -1, -1))

                    def L0(ci, d=d, q=q, order=order):
                        c = order[ci]; c0 = c * CH
                        pg = pgr[ci % 2]
                        for gi, dst, bias_ in ((0, rr[ci % 2], br), (1, ii[ci % 3], bi_)):
                            def mm(e):
                                for u in range(3):
                                    ins = e.matmul(pg[:, u, 0:TT], lhsT=wg[:, gi, d, q, :], rhs=xcb[:, c0 + u * TT:c0 + (u + 1) * TT], start=True, stop=True)
                                return ins
                            op("pe", mm, [wg.b, xcb.b], [pg.b])
                            op("act", lambda e: e.activation(out=dst[:].rearrange("p (u t) -> p u t", u=3), in_=pg[:, :, 0:TT], func=AF.Sigmoid, bias=bias_[:, d, q:q + 1]), [pg.b, bias_.b], [dst.b])

                    def L1(ci, d=d, q=q, order=order):
                        c = order[ci]; c0 = c * CH
                        r_ = rr[ci % 2]; i_ = ii[ci % 3]; a_ = aa[ci % 2]; s_ = ss_[ci % 2]
                        op("act", lambda e: e.activation(out=a_[:], in_=r_[:], func=AF.Exp, scale=cp[:, d, q:q + 1]), [r_.b, cp.b], [a_.b])
                        op("pool", lambda e: e.tensor_tensor(out=s_[:], in0=a_[:], in1=a_[:], op=ALU.mult), [a_.b], [s_.b])
                        op("act", lambda e: e.activation(out=s_[:], in_=s_[:], func=AF.Sqrt, scale=-1.0, bias=1.0), [s_.b], [s_.b])
                        op("pool", lambda e: e.tensor_tensor(out=i_[:], in0=i_[:], in1=xc[:, c0:c0 + CH], op=ALU.mult), [i_.b, xc.b], [i_.b])
                        op("dve", lambda e: e.tensor_tensor(out=i_[:], in0=i_[:], in1=s_[:], op=ALU.mult), [i_.b, s_.b], [i_.b])
                        if c == NCH - 1:
                            pc = PADLO - c0
                            op("dve", lambda e: e.tensor_scalar(out=i_[:, pc:CH], in0=i_[:, pc:CH], scalar1=nfcol, scalar2=None, op0=ALU.mult), [i_.b, flg.b], [i_.b])

                    def L2(ci, d=d, q=q, order=order):
                        c = order[ci]; c0 = c * CH
                        i_ = ii[ci % 3]; a_ = aa[ci % 2]
                        if ci == 0:
                            init = 0.0; rd = []
                        else:
                            init = carry[:, 0:1]; rd = [carry.b]
                        if d == 0:
                            op("dve", lambda e: e.tensor_tensor_scan(out=hsum[:, c0:c0 + CH], data0=a_[:], data1=i_[:], initial=init, op0=ALU.mult, op1=ALU.add), [a_.b, i_.b] + rd, [hsum.b])
                            last = hsum[:, c0 + CH - 1:c0 + CH]; lastb = hsum.b
                        else:
                            hbt = hb[ci % 2]
                            op("dve", lambda e: e.tensor_tensor_scan(out=hbt[:, ::-1], data0=a_[:, ::-1], data1=i_[:, ::-1], initial=init, op0=ALU.mult, op1=ALU.add), [a_.b, i_.b] + rd, [hbt.b])
                            last = hbt[:, 0:1]; lastb = hbt.b
                        crossing = (c % 2 == 1) if d == 0 else (c % 2 == 0)
                        if ci < NCH - 1:
                            if crossing:
                                op("act", lambda e: e.activation(out=carry[:], in_=last, func=AF.Copy, scale=fcol), [lastb, flg.b], [carry.b])
                            else:
                                op("act", lambda e: e.copy(out=carry[:], in_=last), [lastb], [carry.b])
                        if d == 1:
                            op("pool", lambda e: e.tensor_tensor(out=hsum[:, c0:c0 + CH], in0=hsum[:, c0:c0 + CH], in1=hbt[:], op=ALU.add), [hbt.b, hsum.b], [hsum.b])

                    pipeline(NCH, [L0, L1, L2])
                for c in range(NCH):
                    c0 = c * CH; s = c % 2
                    t1 = rr[s]; yb = ybc[s]
                    op("act", lambda e: e.activation(out=t1[:], in_=gb[:, c0:c0 + CH], func=AF.Gelu_apprx_tanh), [gb.b], [t1.b])
                    op("dve", lambda e: e.tensor_tensor(out=yb[:], in0=t1[:], in1=hsum[:, c0:c0 + CH], op=ALU.mult), [t1.b, hsum.b], [yb.b])
                    P.dma(mixs[4 + q, :, c0:c0 + CH], yb[:], reads=[yb.b], writes=[mixB])
            P.barrier()

        with ExitStack() as es:
            rdec = C.sb(es, [128, 32]); phi = C.sb(es, [128, 32], I32)
            WB = C.sb(es, [128, 32, 2, 128], BF16)
            WC = C.sb(es, [128, 32, 3, 128], BF16)
            dsk = C.sb(es, [128, 4])
            P.dma(dsk[:], s5_d[0].rearrange("(q p) -> p q", p=128), writes=[dsk.b])
            with ExitStack() as es2:
                lre = C.sb(es2, [128, 32]); lim = C.sb(es2, [128, 32]); ldt = C.sb(es2, [128, 32])
                Bre = C.sb(es2, [128, 32, 16]); Bim = C.sb(es2, [128, 32, 16])
                for gl in range(2):
                    sl = slice(64 * gl, 64 * gl + 64)
                    P.dma(lre[sl, :].rearrange("p (d j) -> p d j", d=2), lam_re[0].rearrange("d (j g) n -> g n d j", g=2)[gl], writes=[lre.b])
                    P.dma(lim[sl, :].rearrange("p (d j) -> p d j", d=2), lam_im[0].rearrange("d (j g) n -> g n d j", g=2)[gl], writes=[lim.b])
                    P.dma(ldt[sl, :].rearrange("p (d j) -> p d j", d=2), bc(log_dt[0].rearrange("d (j g) -> g d j", g=2)[gl:gl + 1], [64, 2, 16]), writes=[ldt.b])
                    for d in range(2):
                        P.dma(Bre[sl, d * 16:(d + 1) * 16, :], b_re[0, d].rearrange("(j g) n c -> g n j c", g=2)[gl], writes=[Bre.b])
                        P.dma(Bim[sl, d * 16:(d + 1) * 16, :], b_im[0, d].rearrange("(j g) n c -> g n j c", g=2)[gl], writes=[Bim.b])
                dtt = C.sb(es2, [128, 32]); xr = C.sb(es2, [128, 32]); xi = C.sb(es2, [128, 32]); er = C.sb(es2, [128, 32])
                ki = C.sb(es2, [128, 32], I32); kf = C.sb(es2, [128, 32]); fr = C.sb(es2, [128, 32]); pi_ = C.sb(es2, [128, 32], I32); pc_ = C.sb(es2, [128, 32], I32)
                cs = C.sb(es2, [128, 32]); sn = C.sb(es2, [128, 32]); q30 = C.sb(es2, [128, 32], I32)
                nr = C.sb(es2, [128, 32]); ni = C.sb(es2, [128, 32]); den = C.sb(es2, [128, 32]); cr = C.sb(es2, [128, 32]); ci_ = C.sb(es2, [128, 32]); tmp = C.sb(es2, [128, 32]); tmp2 = C.sb(es2, [128, 32])
                A1 = lambda eng, fn, r, w: op(eng, fn, [x.b for x in r], [x.b for x in w])
                A1("dve", lambda e: e.tensor_scalar(out=lre[:], in0=lre[:], scalar1=-1e-4, scalar2=None, op0=ALU.min), [lre], [lre])
                A1("act", lambda e: e.activation(out=dtt[:], in_=ldt[:], func=AF.Exp), [ldt], [dtt])
                A1("dve", lambda e: e.tensor_tensor(out=xr[:], in0=lre[:], in1=dtt[:], op=ALU.mult), [lre, dtt], [xr])
                A1("dve", lambda e: e.tensor_tensor(out=xi[:], in0=lim[:], in1=dtt[:], op=ALU.mult), [lim, dtt], [xi])
                A1("act", lambda e: e.activation(out=rdec[:], in_=xr[:], func=AF.Exp), [xr], [rdec])
                A1("dve", lambda e: e.tensor_scalar(out=fr[:], in0=xi[:], scalar1=float(1.0 / (2 * math.pi)), scalar2=None, op0=ALU.mult), [xi], [fr])
                A1("dve", lambda e: e.tensor_copy(out=ki[:], in_=fr[:]), [fr], [ki])
                A1("dve", lambda e: e.tensor_copy(out=kf[:], in_=ki[:]), [ki], [kf])
                A1("dve", lambda e: e.tensor_tensor(out=fr[:], in0=fr[:], in1=kf[:], op=ALU.subtract), [fr, kf], [fr])
                A1("dve", lambda e: e.tensor_scalar(out=fr[:], in0=fr[:], scalar1=float(2 ** 31), scalar2=None, op0=ALU.mult), [fr], [fr])
                A1("dve", lambda e: e.tensor_copy(out=pi_[:], in_=fr[:]), [fr], [pi_])
                A1("pool", lambda e: e.tensor_tensor(out=pi_[:], in0=pi_[:], in1=pi_[:], op=ALU.add), [pi_], [pi_])
                A1("pool", lambda e: e.iota(q30[:], pattern=[[0, 32]], base=2 ** 30, channel_multiplier=0), [], [q30])
                A1("pool", lambda e: e.tensor_tensor(out=pc_[:], in0=pi_[:], in1=q30[:], op=ALU.add), [pi_, q30], [pc_])
                A1("act", lambda e: e.activation(out=sn[:], in_=pi_[:], func=AF.Sin, scale=float(2 * math.pi / 2 ** 32)), [pi_], [sn])
                A1("act", lambda e: e.activation(out=cs[:], in_=pc_[:], func=AF.Sin, scale=float(2 * math.pi / 2 ** 32)), [pc_], [cs])
                A1("dve", lambda e: e.tensor_tensor(out=nr[:], in0=rdec[:], in1=cs[:], op=ALU.mult), [rdec, cs], [nr])
                A1("dve", lambda e: e.tensor_scalar(out=nr[:], in0=nr[:], scalar1=-1.0, scalar2=None, op0=ALU.add), [nr], [nr])
                A1("dve", lambda e: e.tensor_tensor(out=ni[:], in0=rdec[:], in1=sn[:], op=ALU.mult), [rdec, sn], [ni])
                A1("dve", lambda e: e.tensor_tensor(out=den[:], in0=lre[:], in1=lre[:], op=ALU.mult), [lre], [den])
                A1("dve", lambda e: e.tensor_tensor(out=tmp[:], in0=lim[:], in1=lim[:], op=ALU.mult), [lim], [tmp])
                A1("dve", lambda e: e.tensor_tensor(out=den[:], in0=den[:], in1=tmp[:], op=ALU.add), [den, tmp], [den])
                A1("dve", lambda e: e.reciprocal(out=den[:], in_=den[:]), [den], [den])
                A1("dve", lambda e: e.tensor_tensor(out=cr[:], in0=nr[:], in1=lre[:], op=ALU.mult), [nr, lre], [cr])
                A1("dve", lambda e: e.tensor_tensor(out=tmp[:], in0=ni[:], in1=lim[:], op=ALU.mult), [ni, lim], [tmp])
                A1("dve", lambda e: e.tensor_tensor(out=cr[:], in0=cr[:], in1=tmp[:], op=ALU.add), [cr, tmp], [cr])
                A1("dve", lambda e: e.tensor_tensor(out=cr[:], in0=cr[:], in1=den[:], op=ALU.mult), [cr, den], [cr])
                A1("dve", lambda e: e.tensor_tensor(out=ci_[:], in0=ni[:], in1=lre[:], op=ALU.mult), [ni, lre], [ci_])
                A1("dve", lambda e: e.tensor_tensor(out=tmp2[:], in0=nr[:], in1=lim[:], op=ALU.mult), [nr, lim], [tmp2])
                A1("dve", lambda e: e.tensor_tensor(out=ci_[:], in0=ci_[:], in1=tmp2[:], op=ALU.subtract), [ci_, tmp2], [ci_])
                A1("dve", lambda e: e.tensor_tensor(out=ci_[:], in0=ci_[:], in1=den[:], op=ALU.mult), [ci_, den], [ci_])
                A1("pool", lambda e: e.tensor_copy(out=phi[:], in_=pi_[:]), [pi_], [phi])
                XR = C.sb(es2, [128, 32, 128]); XI = C.sb(es2, [128, 32, 128]); Tm = C.sb(es2, [128, 32, 16]); Tm2 = C.sb(es2, [128, 32, 16])
                A1("pool", lambda e: e.memset(XR[:], 0.0), [], [XR])
                A1("pool", lambda e: e.memset(XI[:], 0.0), [], [XI])
                crb = lambda: bc(cr[:].unsqueeze(2), [128, 32, 16]); cib = lambda: bc(ci_[:].unsqueeze(2), [128, 32, 16])
                A1("dve", lambda e: e.tensor_tensor(out=Tm[:], in0=Bre[:], in1=crb(), op=ALU.mult), [Bre, cr], [Tm])
                A1("dve", lambda e: e.tensor_tensor(out=Tm2[:], in0=Bim[:], in1=cib(), op=ALU.mult), [Bim, ci_], [Tm2])
                A1("dve", lambda e: e.tensor_tensor(out=Tm[:], in0=Tm[:], in1=Tm2[:], op=ALU.subtract), [Tm, Tm2], [Tm])
                A1("dve", lambda e: e.tensor_tensor(out=Tm2[:], in0=Bre[:], in1=cib(), op=ALU.mult), [Bre, ci_], [Tm2])
                A1("dve", lambda e: e.tensor_tensor(out=Bre[:], in0=Bim[:], in1=crb(), op=ALU.mult), [Bim, cr, Bre], [Bre])
                A1("dve", lambda e: e.tensor_tensor(out=Tm2[:], in0=Tm2[:], in1=Bre[:], op=ALU.add), [Tm2, Bre], [Tm2])
                for st in range(32):
                    j = st % 16
                    for gl in range(2):
                        col = ((2 * j + gl) % 8) * 16
                        sl = slice(64 * gl, 64 * gl + 64)
                        A1("dve", lambda e, st=st, sl=sl, col=col: e.tensor_copy(out=XR[sl, st, col:col + 16], in_=Tm[sl, st, :]), [Tm], [XR])
                        A1("pool", lambda e, st=st, sl=sl, col=col: e.tensor_copy(out=XI[sl, st, col:col + 16], in_=Tm2[sl, st, :]), [Tm2], [XI])
                ptb = [C.ps(es2, [128, 512]) for _ in range(2)]
                for st in range(32):
                    for ri, X in enumerate((XR, XI)):
                        pt = ptb[(2 * st + ri) % 2]
                        op("pe", lambda e, st=st, X=X, pt=pt: e.transpose(out=pt[:, 0:128], in_=X[:, st, :], identity=ident[:]), [X.b, ident.b], [pt.b])
                        op("act", lambda e, st=st, ri=ri, pt=pt: e.copy(out=WB[:, st, ri, :], in_=pt[:, 0:128]), [pt.b], [WB.b])
                A1("pool", lambda e: e.memset(XR[:], 0.0), [WB], [XR])
                A1("pool", lambda e: e.memset(XI[:], 0.0), [WB], [XI])
                for st in range(32):
                    d = st // 16; j = st % 16
                    for gl in range(2):
                        g = 2 * j + gl; col = (g % 8) * 16
                        sl = slice(64 * gl, 64 * gl + 64)
                        P.dma(XR[sl, st, col:col + 16], c_re[0, d, g].rearrange("c n -> n c"), writes=[XR.b])
                        P.dma(XI[sl, st, col:col + 16], c_im[0, d, g].rearrange("c n -> n c"), writes=[XI.b])
                A1("dve", lambda e: e.tensor_copy(out=WC[:, :, 0, :], in_=XR[:]), [XR], [WC])
                A1("dve", lambda e: e.tensor_scalar(out=WC[:, :, 1, :], in0=XR[:], scalar1=-1.0, scalar2=None, op0=ALU.mult), [XR], [WC])
                A1("dve", lambda e: e.tensor_scalar(out=WC[:, :, 2, :], in0=XI[:], scalar1=-1.0, scalar2=None, op0=ALU.mult), [XI], [WC])
                P.barrier()
            CH1 = CH + 1
            c30 = C.sb(es, [128, CH1], I32)
            op("pool", lambda e: e.iota(c30[:], pattern=[[0, CH1]], base=2 ** 30, channel_multiplier=0), [], [c30.b])
            ub = C.sb(es, [128, T], BF16); yacc = C.sb(es, [128, T])
            an = C.sb(es, [128, CH1], I32); anc = C.sb(es, [128, CH1], I32)
            snT = [C.sb(es, [128, CH1]) for _ in range(2)]; csT = [C.sb(es, [128, CH1]) for _ in range(2)]
            rotc = [C.sb(es, [128, 8]) for _ in range(2)]
            NB2 = 2
            mk = lambda dt=F32: [C.sb(es, [128, CH], dt) for _ in range(NB2)]
            bre = [C.sb(es, [128, CH]) for _ in range(3)]; bim = [C.sb(es, [128, CH]) for _ in range(3)]
            ta = mk(); tc = mk(); wr_ = mk(); wi_ = mk(); gr = mk(); gi_ = mk()
            pa_ = mk(BF16); pb2 = mk(BF16); pc2 = mk(BF16); pd2 = mk(BF16)
            car = C.sb(es, [128, 4])
            pbu = [C.ps(es, [128, 512]) for _ in range(2)]; pyy = [C.ps(es, [128, 3, 512]) for _ in range(2)]
            SC = float(2 * math.pi / 2 ** 32)
            v3 = lambda ap2: ap2.rearrange("p (u t) -> p u t", u=3)
            def load_u(ctile, c):
                c0 = c * CH; uf = wr_[c % NB2]
                P.dma(uf[:], zs[ctile, :, c0:c0 + CH], reads=[zsB], writes=[uf.b])
                op("pool", lambda e: e.tensor_copy(out=ub[:, c0:c0 + CH], in_=uf[:]), [uf.b], [ub.b])
                op("act", lambda e: e.activation(out=yacc[:, c0:c0 + CH], in_=uf[:], func=AF.Copy, scale=dsk[:, ctile:ctile + 1]), [uf.b, dsk.b], [yacc.b])

            for ctile in range(4):
                if ctile == 0:
                    for c in range(NCH):
                        load_u(0, c)
                its = []
                for jj in range(4):
                    for d in range(2):
                        st = d * 16 + ctile * 4 + jj
                        order = list(range(NCH)) if d == 0 else list(range(NCH - 1, -1, -1))
                        for ci, c in enumerate(order):
                            its.append((st, d, ci, c))

                def tables(st, par):
                    sn = snT[par]; cs = csT[par]; rc = rotc[par]
                    op("pool", lambda e: e.iota(an[:], pattern=[[1, CH1]], base=0, channel_multiplier=0), [], [an.b])
                    op("pool", lambda e: e.tensor_tensor(out=an[:], in0=an[:], in1=bc(phi[:, st:st + 1], [128, CH1]), op=ALU.mult), [an.b, phi.b], [an.b])
                    op("pool", lambda e: e.tensor_tensor(out=anc[:], in0=an[:], in1=c30[:], op=ALU.add), [an.b, c30.b], [anc.b])
                    op("act", lambda e: e.activation(out=sn[:], in_=an[:], func=AF.Sin, scale=SC), [an.b], [sn.b])
                    op("act", lambda e: e.activation(out=cs[:], in_=anc[:], func=AF.Sin, scale=SC), [anc.b], [cs.b])
                    op("act", lambda e: e.copy(out=rc[:, 0:1], in_=cs[:, CH:CH1]), [cs.b], [rc.b])
                    op("act", lambda e: e.copy(out=rc[:, 1:2], in_=sn[:, CH:CH1]), [sn.b], [rc.b])
                    op("act", lambda e: e.mul(out=rc[:, 2:3], in_=sn[:, CH:CH1], mul=-1.0), [sn.b], [rc.b])
                    op("act", lambda e: e.activation(out=rc[:, 3:6], in_=rc[:, 0:3], func=AF.Copy, scale=fcol), [rc.b, flg.b], [rc.b])

                def B0(k):
                    st, d, ci, c = its[k]
                    c0 = c * CH; s = k % 3
                    if ci == 0:
                        tables(st, (k // NCH) % 2)
                    for u in range(3):
                        for ri, dstt in ((0, bre[s]), (1, bim[s])):
                            pb_ = pbu[ri]
                            op("pe", lambda e: e.matmul(pb_[:, 0:TT], lhsT=WB[:, st, ri, :], rhs=ub[:, c0 + u * TT:c0 + (u + 1) * TT], start=True, stop=True), [WB.b, ub.b], [pb_.b])
                            op("act", lambda e: e.copy(out=dstt[:, u * TT:(u + 1) * TT], in_=pb_[:, 0:TT]), [pb_.b], [dstt.b])

                def B12(k):
                    st, d, ci, c = its[k]
                    s = k % NB2; par = (k // NCH) % 2
                    sn = snT[par]; cs = csT[par]; rc = rotc[par]
                    br_ = bre[k % 3]; bi2 = bim[k % 3]; t_a = ta[s]; t_c = tc[s]; wr = wr_[s]; wi = wi_[s]; g_r = gr[s]; g_i = gi_[s]
                    cs2 = cs[:, 0:CH]; sn2 = sn[:, 0:CH]
                    brv = br_[:] if d == 0 else br_[:, ::-1]
                    biv = bi2[:] if d == 0 else bi2[:, ::-1]
                    op("dve", lambda e: e.tensor_tensor(out=wr[:], in0=brv, in1=cs2, op=ALU.mult), [br_.b, cs.b], [wr.b])
                    op("dve", lambda e: e.tensor_tensor(out=t_a[:], in0=biv, in1=sn2, op=ALU.mult), [bi2.b, sn.b], [t_a.b])
                    op("dve", lambda e: e.tensor_tensor(out=wr[:], in0=wr[:], in1=t_a[:], op=ALU.add), [wr.b, t_a.b], [wr.b])
                    op("dve", lambda e: e.tensor_tensor(out=wi[:], in0=biv, in1=cs2, op=ALU.mult), [bi2.b, cs.b], [wi.b])
                    op("dve", lambda e: e.tensor_tensor(out=t_c[:], in0=brv, in1=sn2, op=ALU.mult), [br_.b, sn.b], [t_c.b])
                    op("dve", lambda e: e.tensor_tensor(out=wi[:], in0=wi[:], in1=t_c[:], op=ALU.subtract), [wi.b, t_c.b], [wi.b])
                    dec = bc(rdec[:, st:st + 1], [128, CH])
                    for ri, (src, dst) in enumerate(((wr, g_r), (wi, g_i))):
                        if ci == 0:
                            init = 0.0; rd = []
                        else:
                            init = car[:, ri:ri + 1]; rd = [car.b]
                        op("dve", lambda e: e.tensor_tensor_scan(out=dst[:], data0=dec, data1=src[:], initial=init, op0=ALU.mult, op1=ALU.add), [src.b, rdec.b] + rd, [dst.b])
                    if ci < NCH - 1:
                        crossing = (c % 2 == 1) if d == 0 else (c % 2 == 0)
                        o3 = 3 if crossing else 0
                        lr = g_r[:, CH - 1:CH]; li = g_i[:, CH - 1:CH]
                        op("act", lambda e: e.activation(out=car[:, 2:3], in_=li, func=AF.Copy, scale=rc[:, o3 + 2:o3 + 3]), [g_i.b, rc.b], [car.b])
                        op("act", lambda e: e.activation(out=car[:, 0:1], in_=lr, func=AF.Identity, scale=rc[:, o3 + 0:o3 + 1], bias=car[:, 2:3]), [g_r.b, rc.b, car.b], [car.b])
                        op("act", lambda e: e.activation(out=car[:, 3:4], in_=li, func=AF.Copy, scale=rc[:, o3 + 0:o3 + 1]), [g_i.b, rc.b], [car.b])
                        op("act", lambda e: e.activation(out=car[:, 1:2], in_=lr, func=AF.Identity, scale=rc[:, o3 + 1:o3 + 2], bias=car[:, 3:4]), [g_r.b, rc.b, car.b], [car.b])

                def B3p(k, eng):
                    st, d, ci, c = its[k]
                    s = k % NB2; par = (k // NCH) % 2
                    sn = snT[par]; cs = csT[par]
                    g_r = gr[s]; g_i = gi_[s]
                    cs2 = cs[:, 0:CH]; sn2 = sn[:, 0:CH]
                    if eng == "pool":
                        lst = ((pa_[s], g_r, cs2, cs), (pb2[s], g_i, sn2, sn))
                        extra = [wi_[(k + 1) % NB2].b] if k + 1 < n_it else []
                    elif eng == "dve2":
                        eng = "dve"
                        lst = ((pa_[s], g_r, cs2, cs), (pb2[s], g_i, sn2, sn))
                        extra = []
                    else:
                        lst = ((pc2[s], g_r, sn2, sn), (pd2[s], g_i, cs2, cs))
                        extra = []
                    for (dst, src, tab, tabb) in lst:
                        dv = dst[:] if d == 0 else dst[:, ::-1]
                        op(eng, lambda e: e.tensor_tensor(out=dv, in0=src[:], in1=tab, op=ALU.mult), [src.b, tabb.b] + extra, [dst.b])

                def B4a(k):
                    st, d, ci, c = its[k]
                    s = k % NB2
                    py = pyy[k % 2]
                    terms = ((0, pa_[s]), (1, pb2[s]), (2, pc2[s]), (2, pd2[s]))

                    c0 = c * CH

                    def mmy(e):
                        for u in range(3):
                            e.matmul(py[:, u, 0:TT], lhsT=ident[:], rhs=yacc[:, c0 + u * TT:c0 + (u + 1) * TT], start=True, stop=False)
                            for ti_, (slot, src) in enumerate(terms):
                                ins = e.matmul(py[:, u, 0:TT], lhsT=WC[:, st, slot, :], rhs=src[:, u * TT:(u + 1) * TT], start=False, stop=(ti_ == 3))
                        return ins
                    op("pe", mmy, [WC.b, ident.b, yacc.b] + [t_[1].b for t_ in terms], [py.b])

                def B4b(k):
                    st, d, ci, c = its[k]
                    c0 = c * CH
                    py = pyy[k % 2]
                    op("act", lambda e: e.copy(out=v3(yacc[:, c0:c0 + CH]), in_=py[:, :, 0:TT]), [py.b], [yacc.b])

                n_it = len(its)
                B0(0)
                if n_it > 1:
                    B0(1)
                B12(0)
                for k in range(n_it):
                    if k + 2 < n_it:
                        B0(k + 2)
                    if k + 1 < n_it:
                        B12(k + 1)
                    if k >= 1:
                        B4b(k - 1)
                    B3p(k, "dve2")
                    B3p(k, "dve")
                    B4a(k)
                B4b(n_it - 1)
                for c in range(NCH):
                    c0 = c * CH; s = c % 2
                    yab = pa_[s]
                    op("act", lambda e: e.activation(out=yab[:], in_=yacc[:, c0:c0 + CH], func=AF.Gelu_apprx_tanh), [yacc.b], [yab.b])
                    P.dma(mixs[ctile, :, c0:c0 + CH], yab[:], reads=[yab.b], writes=[mixB])
                    if ctile + 1 < 4:
                        load_u(ctile + 1, c)
            P.barrier()

        with ExitStack() as es:
            wout = C.sb(es, [128, 8, D], BF16, nb=8); wglu = C.sb(es, [128, 4, 512], BF16, nb=4); bglu = C.sb(es, [128, 4])
            P.dma(bglu[:], b_glu[0].rearrange("(q p) -> p q", p=128), writes=[bglu.b])
            with ExitStack() as es2:
                stg = [C.sb(es2, [128, 1024]) for _ in range(3)]
                for k in range(8):
                    load_cast(es2, wout[:, k, :], wout.bs[k], w_out[0, k * 128:(k + 1) * 128, :], 1024, stg=stg)
                for k in range(4):
                    load_cast(es2, wglu[:, k, :], wglu.bs[k], w_glu[0, k * 128:(k + 1) * 128, :], 512, stg=stg)
                P.barrier()
            h0t = [C.sb(es, [128, 8, TT], nb=8) for _ in range(3)]; mx = [C.sb(es, [128, 8, TT], BF16) for _ in range(3)]
            ya2l = [C.sb(es, [128, 4, TT], BF16, nb=4) for _ in range(2)]; sg_ = [C.sb(es, [128, TT]) for _ in range(2)]
            pgl = [C.ps(es, [128, 512]) for _ in range(2)]; pwo = [C.ps(es, [128, 512]) for _ in range(2)]
            hs0v = hs0.rearrange("k p t -> p k t"); mixv = mixs.rearrange("k p t -> p k t")
            hs0T = [Buf("hs0t%d" % i) for i in range(NTILE)]

            def Q0(ti):
                t0 = ti * TT; h = h0t[ti % 3]; m_ = mx[ti % 3]
                P.dma(h[:], hs0v[:, :, t0:t0 + TT], reads=[hs0T[ti]], writes=list(h.bs))
                P.dma(m_[:], mixv[:, :, t0:t0 + TT], reads=[mixB], writes=[m_.b])

            def Q1(ti):
                m_ = mx[ti % 3]; ya2 = ya2l[ti % 2]
                for o in range(4):
                    pg_ = pgl[o % 2]; sgt = sg_[o % 2]

                    def mm(e):
                        for k in range(4):
                            ins = e.matmul(pg_[:, 0:TT], lhsT=wglu[:, k, o * 128:(o + 1) * 128], rhs=m_[:, k, :], start=(k == 0), stop=(k == 3))
                        return ins
                    op("pe", mm, list(wglu.bs) + [m_.b], [pg_.b])
                    op("act", lambda e: e.activation(out=sgt[:], in_=pg_[:, 0:TT], func=AF.Sigmoid, bias=bglu[:, o:o + 1]), [pg_.b, bglu.b], [sgt.b])
                    op("dve", lambda e: e.tensor_tensor(out=ya2[:, o, :], in0=m_[:, o, :], in1=sgt[:], op=ALU.mult), [m_.b, sgt.b], [ya2.bs[o]])

            def Q2(ti):
                t0 = ti * TT; h = h0t[ti % 3]; m_ = mx[ti % 3]; ya2 = ya2l[ti % 2]
                for o in range(8):
                    pw = pwo[o % 2]

                    def mm2(e):
                        for k in range(8):
                            rhs = ya2[:, k, :] if k < 4 else m_[:, k, :]
                            ins = e.matmul(pw[:, 0:TT], lhsT=wout[:, k, o * 128:(o + 1) * 128], rhs=rhs, start=(k == 0), stop=(k == 7))
                        return ins
                    op("pe", mm2, list(wout.bs) + list(ya2.bs) + [m_.b], [pw.b])
                    op("dve", lambda e: e.tensor_tensor(out=h[:, o, :], in0=pw[:, 0:TT], in1=h[:, o, :], op=ALU.add), [pw.b, h.bs[o]], [h.bs[o]])
                P.dma(hs0v[:, :, t0:t0 + TT], h[:], reads=list(h.bs), writes=[hs0T[ti]])

            pipeline(NTILE, [Q0, Q1, Q2])
            P.barrier()

        def ffn_phase(layer, src, srcB, dst, dstB, final):
            N = TT + 2
            with ExitStack() as es:
                W = ffn_setup(es, layer); A = ffn_alloc(es)
                hh = [C.sb(es, [128, 8, N], nb=8) for _ in range(2)]; sq = C.sb(es, [128, 8, N], BF16, nb=8); rstd = C.sb(es, [128, N]); hn = C.sb(es, [128, 8, N], BF16)
                pss = A["pd"][0]
                if final:
                    ytok = [C.sb(es, [128, D])] * 2
                srcv = src.rearrange("k p t -> p k t")
                blks = [(0, 128), (128, 128), (256, TT - 256)]

                def prologue(ti):
                    t0 = ti * TT
                    h = hh[ti % 2]
                    lo, hi, off = tile_cols(t0)
                    if off > 0:
                        op("dve", lambda e: e.memset(h[:, :, 0:1], 0.0), [], list(h.bs))
                    if hi - lo + off < N:
                        op("dve", lambda e: e.memset(h[:, :, N - 1:N], 0.0), [], list(h.bs))
                    P.dma(h[:, :, off:off + hi - lo], srcv[:, :, lo:hi], reads=[srcB], writes=list(h.bs))
                    rmsnorm(es, h, h.bs, hn, hn.b, N, pss, sq, rstd)
                    halo_fix(hn, hn.b, t0)

                def finalize(ti):
                    t0 = ti * TT
                    h = hh[ti % 2]
                    yn = h
                    for k in range(8):
                        if k % 2 == 0:
                            op("act", lambda e: e.activation(out=sq[:, k, 0:TT], in_=h[:, k, 1:TT + 1], func=AF.Square), [h.bs[k]], [sq.bs[k]])
                        else:
                            op("pool", lambda e: e.tensor_tensor(out=sq[:, k, 0:TT], in0=h[:, k, 1:TT + 1], in1=h[:, k, 1:TT + 1], op=ALU.mult), [h.bs[k]], [sq.bs[k]])

                    def mmn(e):
                        for k in range(8):
                            ins = e.matmul(pss[:, 0:TT], lhsT=onesb[:], rhs=sq[:, k, 0:TT], start=(k == 0), stop=(k == 7))
                        return ins
                    op("pe", mmn, list(sq.bs) + [onesb.b], [pss.b])
                    op("act", lambda e: e.activation(out=rstd[:, 0:TT], in_=pss[:, 0:TT], func=AF.Sqrt, scale=1.0 / D, bias=EPS), [pss.b], [rstd.b])
                    op("dve", lambda e: e.reciprocal(out=rstd[:, 0:TT], in_=rstd[:, 0:TT]), [rstd.b], [rstd.b])
                    for k in range(8):
                        op("dve", lambda e: e.scalar_tensor_tensor(out=yn[:, k, 1:TT + 1], in0=h[:, k, 1:TT + 1], scalar=gfin[:, k:k + 1], in1=rstd[:, 0:TT], op0=ALU.mult, op1=ALU.mult),
                           [h.bs[k], rstd.b, gfin.b], [h.bs[k]])
                    for bi, (o, nb_) in enumerate(blks):
                        yt = ytok[bi % 2]
                        for half in range(2):
                            pd = A["pd"][half]

                            def trf(e):
                                for kq in range(4):
                                    k = half * 4 + kq
                                    ins = e.transpose(out=pd[0:nb_, kq * 128:(kq + 1) * 128], in_=yn[:, k, 1 + o:1 + o + nb_], identity=ident[:])
                                return ins
                            op("pe", trf, [h.bs[half * 4 + kq] for kq in range(4)] + [ident.b], [pd.b])
                            if half == 0:
                                op("act", lambda e: e.copy(out=yt[0:nb_, 0:512], in_=pd[0:nb_, :]), [pd.b], [yt.b])
                            else:
                                op("dve", lambda e: e.tensor_copy(out=yt[0:nb_, 512:1024], in_=pd[0:nb_, :]), [pd.b], [yt.b])
                        P.dma(dst[t0 + o:t0 + o + nb_, :], yt[0:nb_, :], reads=[yt.b], writes=[dstB])

                prologue(0)
                for ti in range(NTILE):
                    t0 = ti * TT
                    h = hh[ti % 2]
                    ffn_body(W, A, hn, hn.b, h, h.bs, hook=(lambda ti=ti: prologue(ti + 1)) if ti + 1 < NTILE else None,
                             hook2=(lambda ti=ti: finalize(ti - 1)) if (final and ti >= 1) else None)
                    if not final:
                        P.dma(dst.rearrange("k p t -> p k t")[:, :, t0:t0 + TT], h[:, :, 1:TT + 1], reads=list(h.bs), writes=[dstB])
                if final:
                    finalize(NTILE - 1)
                P.barrier()

        ffn_phase(0, hs0, hs0B, hs2, hs2B, False)

        with ExitStack() as es:
            wq = C.sb(es, [128, 8, D], BF16, nb=8); wkd = C.sb(es, [128, 8, 4, 128], BF16, nb=8); wv = C.sb(es, [128, 8, 256], BF16, nb=8)
            wo = C.sb(es, [128, 8, D], BF16, nb=8)
            with ExitStack() as es2:
                stg = [C.sb(es2, [128, 1536]) for _ in range(3)]
                for k in range(8):
                    sgt = stg[k % 3]
                    P.dma(sgt[:], w_qkv[0, k * 128:(k + 1) * 128, :], writes=[sgt.b])
                    gk = gmix[:, 1, k:k + 1]
                    op("dve", lambda e, k=k, sgt=sgt, gk=gk: e.tensor_scalar(out=wq[:, k, :], in0=sgt[:, 0:1024], scalar1=gk, scalar2=None, op0=ALU.mult), [sgt.b, gmix.b], [wq.bs[k]])
                    for half in range(2):
                        op("pool", lambda e, k=k, sgt=sgt, gk=gk, half=half: e.tensor_scalar(out=wkd[:, k, :, half * 64:(half + 1) * 64], in0=sgt[:, 1024:1280].rearrange("p (h d) -> p h d", h=4),
                                                                                           scalar1=gk, scalar2=None, op0=ALU.mult), [sgt.b, gmix.b], [wkd.bs[k]])
                    op("act", lambda e, k=k, sgt=sgt, gk=gk: e.activation(out=wv[:, k, :], in_=sgt[:, 1280:1536], func=AF.Copy, scale=gk), [sgt.b, gmix.b], [wv.bs[k]])
                stg2 = [C.sb(es2, [128, 1024]) for _ in range(2)]
                for k in range(8):
                    load_cast(es2, wo[:, k, :], wo.bs[k], w_o[0, k * 128:(k + 1) * 128, :], 1024, stg=stg2)
                P.barrier()
            bias = C.sb(es, [128, 16, 384]); sink = C.sb(es, [128, 16])
            P.dma(sink[:], bc(sink_d[0:1, :], [128, 16]), writes=[sink.b])
            with ExitStack() as es2:
                di = C.sb(es2, [128, 384], I32); df = C.sb(es2, [128, 384]); mk = C.sb(es2, [128, 384])
                op("pool", lambda e: e.iota(di[:], pattern=[[-1, 384]], base=128, channel_multiplier=1), [], [di.b])
                op("dve", lambda e: e.tensor_copy(out=df[:], in_=di[:]), [di.b], [df.b])
                op("dve", lambda e: e.tensor_scalar(out=mk[:], in0=df[:], scalar1=-1.0, scalar2=None, op0=ALU.mult), [df.b], [mk.b])
                op("dve", lambda e: e.tensor_tensor(out=df[:], in0=df[:], in1=mk[:], op=ALU.max), [df.b, mk.b], [df.b])
                op("dve", lambda e: e.tensor_scalar(out=mk[:], in0=df[:], scalar1=128.0, scalar2=NEG, op0=ALU.is_gt, op1=ALU.mult), [df.b], [mk.b])
                for hh in range(16):
                    slope = 2.0 ** (-8.0 * (hh + 1) / 16.0)
                    op("dve", lambda e, hh=hh, slope=slope: e.scalar_tensor_tensor(out=bias[:, hh, :], in0=df[:], scalar=-slope, in1=mk[:], op0=ALU.mult, op1=ALU.add), [df.b, mk.b], [bias.b])
                P.barrier()
            NCK = 19
            KT2 = C.sb(es, [128, 4, NCK * 128], BF16, nb=NCK); V = C.sb(es, [128, NCK, 256], BF16, nb=NCK); QT = C.sb(es, [128, 8, 17 * 128], BF16, nb=17)
            h2c = [C.sb(es, [128, 8, 128], nb=8) for _ in range(2)]; sqc = [C.sb(es, [128, 8, 128], BF16, nb=8) for _ in range(2)]
            rsc = [C.sb(es, [128, 128]) for _ in range(2)]; hnc = [C.sb(es, [128, 8, 128], BF16) for _ in range(2)]
            Sb = [C.sb(es, [128, 385]) for _ in range(16)]; Pm = [C.sb(es, [128, 386], BF16) for _ in range(3)]; PT = [C.sb(es, [128, 384], BF16) for _ in range(3)]
            for hh in range(16):
                op("act", lambda e, hh=hh: e.copy(out=Sb[hh][:, 384:385], in_=sink[:, hh:hh + 1]), [sink.b], [Sb[hh].b])
            stat = [C.sb(es, [128, 4]) for _ in range(8)]
            otok = [C.sb(es, [128, D]) for _ in range(2)]; oT = [C.sb(es, [128, 8, 128], BF16) for _ in range(2)]
            h2b = [C.sb(es, [128, 8, 128], nb=8) for _ in range(2)]
            pmm = [C.ps(es, [128, 512]) for _ in range(2)]; pS = [C.ps(es, [128, 512]) for _ in range(2)]
            pPT = [C.ps(es, [128, 1024], BF16) for _ in range(2)]; pO = [C.ps(es, [128, 512]) for _ in range(2)]
            hs2v = hs2.rearrange("k p t -> p k t"); hs3v = hs3.rearrange("k p t -> p k t")
            nmm = [0]

            def grp(lhs_fn, rhs_fn, n, dst_fn, rd, wr, scale=None):
                pm = pmm[nmm[0] % 2]; nmm[0] += 1

                def mm(e):
                    for k in range(8):
                        ins = e.matmul(pm[:, 0:n], lhsT=lhs_fn(k), rhs=rhs_fn(k), start=(k == 0), stop=(k == 7))
                    return ins
                op("pe", mm, rd, [pm.b])
                if scale is not None:
                    op("act", lambda e: e.activation(out=dst_fn(), in_=pm[:, 0:n], func=AF.Copy, scale=scale), [pm.b], wr)
                elif nmm[0] % 2 == 0:
                    op("act", lambda e: e.copy(out=dst_fn(), in_=pm[:, 0:n]), [pm.b], wr)
                else:
                    op("dve", lambda e: e.tensor_copy(out=dst_fn(), in_=pm[:, 0:n]), [pm.b], wr)

            for sg in range(NSEG):
                s0 = sg * SEG

                def chunk_rng(i):
                    cs0 = s0 - 240 + 128 * i
                    return cs0, max(cs0, 0), min(cs0 + 128, T)

                def pa0(i):
                    cs0, lo, hi = chunk_rng(i)
                    if hi <= lo:
                        op("pool", lambda e: e.memset(KT2[:, :, i * 128:(i + 1) * 128], 0.0), [], [KT2.bs[i]])
                        op("pool", lambda e: e.memset(V[:, i, :], 0.0), [], [V.bs[i]])
                        return
                    hc = h2c[i % 2]
                    if hi - lo < 128:
                        op("dve", lambda e: e.memset(hc[:], 0.0), [], list(hc.bs))
                    P.dma(hc[:, :, lo - cs0:hi - cs0], hs2v[:, :, lo:hi], reads=[hs2B], writes=list(hc.bs))
                    rmsnorm(es, hc, hc.bs, hnc[i % 2], hnc[i % 2].b, 128, pmm[nmm[0] % 2], sqc[i % 2], rsc[i % 2]); nmm[0] += 1

                def pa1(i):
                    cs0, lo, hi = chunk_rng(i)
                    if hi <= lo:
                        return
                    hn_ = hnc[i % 2]
                    for hk in range(4):
                        grp(lambda k: wkd[:, k, hk, :], lambda k: hn_[:, k, :], 128, lambda: KT2[:, hk, i * 128:(i + 1) * 128], list(wkd.bs) + [hn_.b], [KT2.bs[i]])
                    grp(lambda k: hn_[:, k, :], lambda k: wv[:, k, :], 256, lambda: V[:, i, :], list(wv.bs) + [hn_.b], [V.bs[i]])
                    if 1 <= i <= 17:
                        for tq in range(8):
                            grp(lambda k: wq[:, k, tq * 128:(tq + 1) * 128], lambda k: hn_[:, k, :], 128, lambda: QT[:, tq, (i - 1) * 128:i * 128],
                                list(wq.bs) + [hn_.b], [QT.bs[i - 1]], scale=0.125)
                pipeline(NCK, [pa0, pa1])

                iters = [(i, hh) for i in range(1, 18) for hh in range(16)]

                def blk_info(i):
                    qs0 = s0 - 240 + 128 * i
                    ws = qs0 - 128
                    regions = []
                    if ws < s0:
                        regions.append((0, min(s0 - ws, 384), None if sg == 0 else nbcol))
                    if ws + 384 > s0 + SEG:
                        regions.append((s0 + SEG - ws, 384, None if sg == NSEG - 1 else nbcol))
                    if sg == NSEG - 1:
                        plo = max(PADLO - ws, 0); phi_ = min(T - ws, 384)
                        if plo < phi_:
                            regions.append((plo, phi_, npcol))
                    return qs0, regions

                def hd(n):
                    i, hh = iters[n]
                    hk = hh // 4; g = hh % 4
                    return i, hh, hk, hk * 2 + g // 2, 64 * (g % 2)

                def a0(n):
                    i, hh, hk, tq, base = hd(n)
                    ps_ = pS[n % 2]
                    if hh == 0:
                        qs0, _ = blk_info(i)
                        hb_ = h2b[i % 2]
                        qlo = max(qs0, 0)
                        if qlo > qs0:
                            op("dve", lambda e: e.memset(hb_[:], 0.0), [], list(hb_.bs))
                        P.dma(hb_[:, :, qlo - qs0:128], hs2v[:, :, qlo:qs0 + 128], reads=[hs2B], writes=list(hb_.bs))
                    op("pe", lambda e: e.matmul(ps_[:, 0:384], lhsT=QT[base:base + 64, tq, (i - 1) * 128:i * 128], rhs=KT2[base:base + 64, hk, (i - 1) * 128:(i + 2) * 128], start=True, stop=True),
                       [QT.bs[i - 1], KT2.bs[i - 1], KT2.bs[i], KT2.bs[i + 1]], [ps_.b])

                def a1(n):
                    i, hh, hk, tq, base = hd(n)
                    ps_ = pS[n % 2]; sb_ = Sb[hh]; st_ = stat[n % 8]
                    _, regions = blk_info(i)
                    op("dve", lambda e: e.tensor_tensor(out=sb_[:, 0:384], in0=ps_[:, 0:384], in1=bias[:, hh, :], op=ALU.add), [ps_.b, bias.b], [sb_.b])
                    for (rl, rh, sc) in regions:
                        if sc is None:
                            op("dve", lambda e: e.tensor_scalar(out=sb_[:, rl:rh], in0=sb_[:, rl:rh], scalar1=NEG, scalar2=None, op0=ALU.add), [sb_.b], [sb_.b])
                        else:
                            op("dve", lambda e: e.tensor_scalar(out=sb_[:, rl:rh], in0=sb_[:, rl:rh], scalar1=sc, scalar2=None, op0=ALU.add), [sb_.b, flg.b], [sb_.b])
                    op("dve", lambda e: e.reduce_max(out=st_[:, 1:2], in_=sb_[:, 0:385], axis=AX.X, negate=True), [sb_.b], [st_.b])

                def a2(n):
                    i, hh, hk, tq, base = hd(n)
                    sb_ = Sb[hh]; st_ = stat[n % 8]; pm_ = Pm[n % 3]
                    op("act", lambda e: e.activation(out=pm_[:, 0:385], in_=sb_[:, 0:385], func=AF.Exp, bias=st_[:, 1:2], accum_out=st_[:, 2:3]), [sb_.b, st_.b], [pm_.b, st_.b])

                def a3(n):
                    st_ = stat[n % 8]; pm_ = Pm[n % 3]; ppt = pPT[n % 2]
                    op("dve", lambda e: e.reciprocal(out=st_[:, 3:4], in_=st_[:, 2:3]), [st_.b], [st_.b])

                    def trp(e):
                        for c in range(3):
                            ins = e.transpose(out=ppt[:, c * 128:(c + 1) * 128], in_=pm_[:, c * 128:(c + 1) * 128], identity=identb[:])
                        return ins
                    op("pe", trp, [pm_.b, identb.b], [ppt.b])

                def a4(n):
                    ppt = pPT[n % 2]; pt_ = PT[n % 3]
                    op("act", lambda e: e.copy(out=pt_[:], in_=ppt[:, 0:384]), [ppt.b], [pt_.b])

                def a5(n):
                    i, hh, hk, tq, base = hd(n)
                    pt_ = PT[n % 3]; po = pO[n % 2]

                    def mo(e):
                        for c in range(3):
                            ins = e.matmul(po[:, 0:64], lhsT=pt_[:, c * 128:(c + 1) * 128], rhs=V[:, i - 1 + c, hk * 64:(hk + 1) * 64], start=(c == 0), stop=(c == 2))
                        return ins
                    op("pe", mo, [pt_.b, V.bs[i - 1], V.bs[i], V.bs[i + 1]], [po.b])

                def a6(n):
                    i, hh, hk, tq, base = hd(n)
                    po = pO[n % 2]; st_ = stat[n % 8]; ot = otok[i % 2]
                    op("act", lambda e: e.activation(out=ot[:, hh * 64:(hh + 1) * 64], in_=po[:, 0:64], func=AF.Copy, scale=st_[:, 3:4]), [po.b, st_.b], [ot.b])
                    if hh != 15:
                        return
                    qs0, _ = blk_info(i)
                    hb_ = h2b[i % 2]; oT_ = oT[i % 2]
                    for half in range(2):
                        pm = pmm[half]

                        def tro(e):
                            for kq in range(4):
                                k = half * 4 + kq
                                ins = e.transpose(out=pm[:, kq * 128:(kq + 1) * 128], in_=ot[:, k * 128:(k + 1) * 128], identity=ident[:])
                            return ins
                        op("pe", tro, [ot.b, ident.b], [pm.b])
                        if half == 0:
                            op("dve", lambda e: e.tensor_copy(out=oT_[:, 0:4, :].rearrange("p k t -> p (k t)"), in_=pm[:, :]), [pm.b], [oT_.b])
                        else:
                            op("act", lambda e: e.copy(out=oT_[:, 4:8, :].rearrange("p k t -> p (k t)"), in_=pm[:, :]), [pm.b], [oT_.b])
                    for o in range(8):
                        pm = pmm[nmm[0] % 2]; nmm[0] += 1

                        def mw(e):
                            for k in range(8):
                                ins = e.matmul(pm[:, 0:128], lhsT=wo[:, k, o * 128:(o + 1) * 128], rhs=oT_[:, k, :], start=(k == 0), stop=(k == 7))
                            return ins
                        op("pe", mw, list(wo.bs) + [oT_.b], [pm.b])
                        op("dve", lambda e: e.tensor_tensor(out=hb_[:, o, :], in0=pm[:, 0:128], in1=hb_[:, o, :], op=ALU.add), [pm.b, hb_.bs[o]], [hb_.bs[o]])
                    c_lo = 112 if i == 1 else 0
                    P.dma(hs3v[:, :, qs0 + c_lo:qs0 + 128], hb_[:, :, c_lo:128], reads=list(hb_.bs), writes=[hs3B])

                pipeline(len(iters), [a0, a1, a2, a3, a4, a5, a6])
            P.barrier()

        ffn_phase(1, hs3, hs3B, ys, ysB, True)
        P.barrier()
        block = top.enter_context(nc.Block())
        P.emit(block)
        print("plan: ops", P.nops, "waits", P.nwaits, "dmas", P.ndma, "sems", len(P.sems), flush=True)
    return nc


def core_inputs(inputs):
    meta = np.asarray(inputs["meta_tokens"], np.float32)
    xp = np.asarray(inputs["x_prompt"], np.float32)
    xsm = np.asarray(inputs["x_sample"], np.float32)
    wnames = ["norm_mix_g", "norm_ffn_g", "final_norm_g", "w_in_ab", "s5_lambda_re", "s5_lambda_im", "s5_log_dt", "s5_b_re", "s5_b_im",
              "s5_c_re", "s5_c_im", "s5_d", "w_glu", "b_glu", "lru_conv_w", "lru_conv_b", "lru_w_r", "lru_b_r", "lru_w_i", "lru_b_i",
              "lru_lambda", "w_out_ab", "w_qkv", "w_o", "attn_sink", "w_up", "ffn_conv_w", "ffn_conv_b", "w_down"]
    wts = {n: np.ascontiguousarray(np.asarray(inputs[n], np.float32)) for n in wnames}
    maps = []
    for c in range(8):
        X = np.zeros((T, D), np.float32)
        if c < 4:
            for k in range(4):
                X[k * SEG:k * SEG + 16] = meta
                X[k * SEG + 16:(k + 1) * SEG] = xp[4 * c + k]
            f = 0.0
        else:
            X[0:16] = meta
            X[16:16 + 8192] = xsm[c - 4]
            f = 1.0
        flags = np.tile(np.array([f, 1.0 - f, NEG * (1.0 - f), NEG * f], np.float32), (128, 1))
        m = {"xs": X, "flags": flags}
        m.update(wts)
        maps.append(m)
    return maps


_CACHE = {}


def kernel(**inputs):
    maps = core_inputs(inputs)
    if "nc" not in _CACHE:
        _CACHE["nc"] = build(debug=False)
    nc = _CACHE["nc"]
    res = run_bass_kernel_spmd(nc, maps, core_ids=list(range(8)))
    yp = np.empty((16, 2048, D), np.float32)
    ysm = np.empty((4, 8192, D), np.float32)
    for c in range(8):
        y = np.asarray(res.results[c]["ys"])
        if c < 4:
            for k in range(4):
                yp[4 * c + k] = y[k * SEG + 16:(k + 1) * SEG]
        else:
            ysm[c - 4] = y[16:16 + 8192]
    return (yp, ysm)
```

```python
import math
import os
import numpy as np
from contextlib import ExitStack
import concourse.bass as bass
import concourse.mybir as mybir
from concourse.bass_utils import run_bass_kernel_spmd

F32 = mybir.dt.float32
BF16 = mybir.dt.bfloat16
I32 = mybir.dt.int32
AF = mybir.ActivationFunctionType
ALU = mybir.AluOpType
AX = mybir.AxisListType

D = 1024
NSEG = 4
SEG = 2064
T = NSEG * SEG
TT = 344
NTILE = T // TT
CH = 1032
NCH = T // CH
DFF = 2816
NJ = DFF // 128
NEG = -1e30
EPS = 1e-6
PADLO = 8208
NDMASEM = 24
SEMCAP = 30000


class Buf:
    __slots__ = ("name", "w", "r")

    def __init__(self, name=""):
        self.name = name
        self.w = None
        self.r = []


class Rec:
    def __init__(self):
        self.calls = []

    def __getattr__(self, name):
        def f(*a, **k):
            self.calls.append((name, a, k))
            return self
        return f


class Plan:
    ENGS = ("pe", "dve", "act", "pool", "sp")

    def __init__(self, nc, es):
        self.nc = nc
        self.es = es
        self.items = {e: [] for e in self.ENGS}
        self.sems = {}
        self.epoch = {e: 0 for e in self.ENGS}
        self.count = {}
        for e in self.ENGS:
            self._newsem((e, 0))
        for i in range(NDMASEM):
            self._newsem(("dma", i))
        self.known = {e: {} for e in self.ENGS}
        self.ndma = 0
        self.nwaits = 0
        self.nops = 0

    def _newsem(self, key):
        self.sems[key] = self.es.enter_context(self.nc.semaphore("s%d" % len(self.sems)))
        self.count[key] = 0

    def _deps(self, eng, reads, writes):
        need = {}
        for b in reads:
            if b.w is not None:
                k, v = b.w
                if need.get(k, 0) < v:
                    need[k] = v
        for b in writes:
            if b.w is not None:
                k, v = b.w
                if need.get(k, 0) < v:
                    need[k] = v
            for k, v in b.r:
                if need.get(k, 0) < v:
                    need[k] = v
        waits = []
        kn = self.known[eng]
        for k, v in need.items():
            if kn.get(k, 0) >= v:
                continue
            kn[k] = v
            waits.append((k, v))
        self.nwaits += len(waits)
        return waits

    def _commit(self, sig, reads, writes):
        for b in reads:
            b.r.append(sig)
        for b in writes:
            b.w = sig
            b.r = []

    def op(self, eng, fn, reads=(), writes=()):
        waits = self._deps(eng, reads, writes)
        key = (eng, self.epoch[eng])
        if self.count[key] >= SEMCAP:
            self.epoch[eng] += 1
            key = (eng, self.epoch[eng])
            self._newsem(key)
        self.count[key] += 1
        val = self.count[key]
        rec = Rec()
        fn(rec)
        assert rec.calls
        self.items[eng].append((waits, rec.calls, (key, 1)))
        self._commit((key, val), reads, writes)
        self.nops += 1

    def dma(self, out, in_, reads=(), writes=(), eng="sp"):
        i = self.ndma % NDMASEM
        self.ndma += 1
        key = ("dma", i)
        waits = self._deps(eng, reads, writes)
        prev = self.count[key]
        if prev > 0 and self.known[eng].get(key, 0) < prev:
            self.known[eng][key] = prev
            waits.append((key, prev))
        self.count[key] += 16
        val = self.count[key]
        self.items[eng].append((waits, [("dma_start", (), {"out": out, "in_": in_})], (key, 16)))
        self._commit((key, val), reads, writes)

    def barrier(self):
        for eng in self.ENGS:
            waits = []
            for k, v in self.count.items():
                if v > 0 and self.known[eng].get(k, 0) < v:
                    self.known[eng][k] = v
                    waits.append((k, v))
            self.items[eng].append((waits, None, None))

    def emit(self, block):
        plan = self

        def run(engname):
            def f(e):
                for waits, fn, inc in plan.items[engname]:
                    for k, v in waits:
                        e.wait_ge(plan.sems[k], v)
                    if fn is None:
                        continue
                    for name, a, k in fn:
                        ins = getattr(e, name)(*a, **k)
                    ins.then_inc(plan.sems[inc[0]], inc[1])
            return f

        block.tensor(run("pe"))
        block.vector(run("dve"))
        block.scalar(run("act"))
        block.gpsimd(run("pool"))
        block.sync(run("sp"))


class Tl:
    def __init__(self, t, nb=1, name=""):
        self.t = t
        self.bs = [Buf(name + str(i)) for i in range(nb)]
        self.b = self.bs[0]

    def __getitem__(self, k):
        return self.t[k]


class Ctx:
    def __init__(self, nc, P):
        self.nc = nc
        self.P = P
        self.n = 0

    def sb(self, es, shape, dt=F32, nb=1):
        self.n += 1
        return Tl(es.enter_context(self.nc.sbuf_tensor("sb%d" % self.n, list(shape), dt)), nb, "sb%d_" % self.n)

    def ps(self, es, shape, dt=F32, nb=1):
        self.n += 1
        return Tl(es.enter_context(self.nc.psum_tensor("ps%d" % self.n, list(shape), dt)), nb, "ps%d_" % self.n)


def bc(ap, shape):
    return ap.to_broadcast(list(shape))


def pipeline(n, stage_fns):
    ns = len(stage_fns)
    for step in range(n + ns - 1):
        for si, f in enumerate(stage_fns):
            it = step - si
            if 0 <= it < n:
                f(it)


def build(debug=False):
    nc = bass.Bass("TRN2", target_bir_lowering=False)
    ein = lambda n, s, d=F32: nc.dram_tensor(n, list(s), d, kind="ExternalInput").ap()
    xs = ein("xs", [T, D])
    flags_d = ein("flags", [128, 4])
    g_mix = ein("norm_mix_g", [2, D]); g_ffn = ein("norm_ffn_g", [2, D]); g_fin = ein("final_norm_g", [D])
    w_in = ein("w_in_ab", [1, D, 1536])
    lam_re = ein("s5_lambda_re", [1, 2, 32, 64]); lam_im = ein("s5_lambda_im", [1, 2, 32, 64])
    log_dt = ein("s5_log_dt", [1, 2, 32])
    b_re = ein("s5_b_re", [1, 2, 32, 64, 16]); b_im = ein("s5_b_im", [1, 2, 32, 64, 16])
    c_re = ein("s5_c_re", [1, 2, 32, 16, 64]); c_im = ein("s5_c_im", [1, 2, 32, 16, 64])
    s5_d = ein("s5_d", [1, 512]); w_glu = ein("w_glu", [1, 512, 512]); b_glu = ein("b_glu", [1, 512])
    cwB = ein("lru_conv_w", [1, 4, 512]); cbB = ein("lru_conv_b", [1, 512])
    w_r = ein("lru_w_r", [1, 2, 8, 64, 64]); b_r = ein("lru_b_r", [1, 2, 512])
    w_i = ein("lru_w_i", [1, 2, 8, 64, 64]); b_i = ein("lru_b_i", [1, 2, 512])
    lru_lam = ein("lru_lambda", [1, 2, 512])
    w_out = ein("w_out_ab", [1, D, D]); w_qkv = ein("w_qkv", [1, D, 1536]); w_o = ein("w_o", [1, D, D])
    sink_d = ein("attn_sink", [1, 16])
    w_up = ein("w_up", [2, D, 2 * DFF]); cwF = ein("ffn_conv_w", [2, 3, 2 * DFF]); cbF = ein("ffn_conv_b", [2, 2 * DFF])
    w_down = ein("w_down", [2, DFF, D])
    ys = nc.dram_tensor("ys", [T, D], F32, kind="ExternalOutput").ap()
    skind = "ExternalOutput" if debug else "Internal"
    scr = lambda n, s, d=F32: nc.dram_tensor(n, list(s), d, kind=skind).ap()
    hs0 = scr("hs0", [8, 128, T]); zs = scr("zs", [12, 128, T]); mixs = scr("mixs", [8, 128, T], BF16)
    hs2 = scr("hs2", [8, 128, T]); hs3 = scr("hs3", [8, 128, T])
    hs0B = Buf("hs0"); zsB = Buf("zs"); mixB = Buf("mixs"); hs2B = Buf("hs2"); hs3B = Buf("hs3"); ysB = Buf("ys")

    with ExitStack() as top:
        P = Plan(nc, top)
        C = Ctx(nc, P)
        op = P.op
        nc_allow = top.enter_context(nc.allow_non_contiguous_dma(reason="small parameter layouts"))

        flg = C.sb(top, [128, 4])
        ident = C.sb(top, [128, 128]); identb = C.sb(top, [128, 128], BF16); onesb = C.sb(top, [128, 128], BF16)
        gmix = C.sb(top, [128, 2, 8]); gffn = C.sb(top, [128, 2, 8]); gfin = C.sb(top, [128, 8])
        P.dma(flg[:], flags_d[:, :], writes=[flg.b])
        P.dma(gmix[:], g_mix.rearrange("l (k p) -> p l k", p=128), writes=[gmix.b])
        P.dma(gffn[:], g_ffn.rearrange("l (k p) -> p l k", p=128), writes=[gffn.b])
        P.dma(gfin[:], g_fin.rearrange("(k p) -> p k", p=128), writes=[gfin.b])
        fcol = flg[:, 0:1]; nfcol = flg[:, 1:2]; nbcol = flg[:, 2:3]; npcol = flg[:, 3:4]
        with ExitStack() as es:
            it = C.sb(es, [128, 128], I32); ip = C.sb(es, [128, 1], I32); itf = C.sb(es, [128, 128]); ipf = C.sb(es, [128, 1])
            op("pool", lambda e: e.iota(it[:], pattern=[[1, 128]], base=0, channel_multiplier=0), writes=[it.b])
            op("pool", lambda e: e.iota(ip[:], pattern=[[0, 1]], base=0, channel_multiplier=1), writes=[ip.b])
            op("dve", lambda e: e.tensor_copy(out=itf[:], in_=it[:]), [it.b], [itf.b])
            op("dve", lambda e: e.tensor_copy(out=ipf[:], in_=ip[:]), [ip.b], [ipf.b])
            op("dve", lambda e: e.tensor_scalar(out=ident[:], in0=itf[:], scalar1=ipf[:, 0:1], scalar2=None, op0=ALU.is_equal),
               [itf.b, ipf.b], [ident.b])
            op("dve", lambda e: e.tensor_copy(out=identb[:], in_=ident[:]), [ident.b], [identb.b])
            op("dve", lambda e: e.memset(onesb[:], 1.0), [], [onesb.b])
            P.barrier()

        def load_cast(es_, dst, dstbuf, src_ap, ncols, scale_ap=None, scale_buf=None, eng_cycle=("dve", "act", "pool"), stg=None, k=[0]):
            s = stg[k[0] % len(stg)]
            eng = eng_cycle[k[0] % len(eng_cycle)]
            k[0] += 1
            P.dma(s[:, 0:ncols], src_ap, writes=[s.b])
            rd = [s.b] + ([scale_buf] if scale_buf is not None else [])
            if scale_ap is None:
                if eng == "act":
                    op("act", lambda e: e.copy(out=dst, in_=s[:, 0:ncols]), rd, [dstbuf])
                else:
                    op(eng, lambda e: e.tensor_copy(out=dst, in_=s[:, 0:ncols]), rd, [dstbuf])
            else:
                if eng == "act":
                    op("act", lambda e: e.activation(out=dst, in_=s[:, 0:ncols], func=AF.Copy, scale=scale_ap), rd, [dstbuf])
                else:
                    op(eng, lambda e: e.tensor_scalar(out=dst, in0=s[:, 0:ncols], scalar1=scale_ap, scalar2=None, op0=ALU.mult), rd, [dstbuf])

        def rmsnorm(es_, h, hbufs, hn, hnbuf, n, pss, sq, rstd, gscale=None):
            for k in range(8):
                eng = "act" if k % 2 == 0 else "pool"
                if eng == "act":
                    op("act", lambda e, k=k: e.activation(out=sq[:, k, 0:n], in_=h[:, k, 0:n], func=AF.Square), [hbufs[k]], [sq.bs[k]])
                else:
                    op("pool", lambda e, k=k: e.tensor_tensor(out=sq[:, k, 0:n], in0=h[:, k, 0:n], in1=h[:, k, 0:n], op=ALU.mult), [hbufs[k]], [sq.bs[k]])

            def mm(e):
                for k in range(8):
                    ins = e.matmul(pss[:, 0:n], lhsT=onesb[:], rhs=sq[:, k, 0:n], start=(k == 0), stop=(k == 7))
                return ins
            op("pe", mm, list(sq.bs) + [onesb.b], [pss.b])
            op("act", lambda e: e.activation(out=rstd[:, 0:n], in_=pss[:, 0:n], func=AF.Sqrt, scale=1.0 / D, bias=EPS), [pss.b], [rstd.b])
            op("dve", lambda e: e.reciprocal(out=rstd[:, 0:n], in_=rstd[:, 0:n]), [rstd.b], [rstd.b])
            if gscale is None:
                op("dve", lambda e: e.tensor_tensor(out=hn[:, :, 0:n], in0=h[:, :, 0:n], in1=bc(rstd[:, 0:n].unsqueeze(1), [128, 8, n]), op=ALU.mult),
                   list(hbufs) + [rstd.b], [hnbuf])
            else:
                for k in range(8):
                    op("dve", lambda e, k=k: e.scalar_tensor_tensor(out=hn[:, k, 0:n], in0=h[:, k, 0:n], scalar=gscale[:, k:k + 1], in1=rstd[:, 0:n],
                                                                  op0=ALU.mult, op1=ALU.mult), [hbufs[k], rstd.b], [hnbuf])

        def tile_cols(t0):
            lo = max(t0 - 1, 0); hi = min(t0 + TT + 1, T)
            return lo, hi, lo - (t0 - 1)

        def halo_fix(hn, hnbuf, t0):
            N = TT + 2
            if t0 == 0:
                op("dve", lambda e: e.memset(hn[:, :, 0:1], 0.0), [], [hnbuf])
            elif t0 % SEG == 0:
                op("dve", lambda e: e.tensor_scalar(out=hn[:, :, 0:1], in0=hn[:, :, 0:1], scalar1=fcol, scalar2=None, op0=ALU.mult), [hnbuf, flg.b], [hnbuf])
            if t0 + TT == T:
                op("dve", lambda e: e.memset(hn[:, :, N - 1:N], 0.0), [], [hnbuf])
                c0 = PADLO - (t0 - 1)
                op("dve", lambda e: e.tensor_scalar(out=hn[:, :, c0:N - 1], in0=hn[:, :, c0:N - 1], scalar1=nfcol, scalar2=None, op0=ALU.mult), [hnbuf, flg.b], [hnbuf])
            elif (t0 + TT) % SEG == 0:
                op("dve", lambda e: e.tensor_scalar(out=hn[:, :, N - 1:N], in0=hn[:, :, N - 1:N], scalar1=fcol, scalar2=None, op0=ALU.mult), [hnbuf, flg.b], [hnbuf])

        def ffn_setup(es, layer):
            W = {}
            W["up"] = C.sb(es, [128, 8, 2 * DFF], BF16, nb=8)
            W["dn"] = C.sb(es, [128, NJ, D], BF16, nb=NJ)
            W["cw"] = C.sb(es, [128, 3, 44]); W["cb"] = C.sb(es, [128, 44])
            P.dma(W["cw"][:], cwF[layer].rearrange("k (j p) -> p k j", p=128), writes=[W["cw"].b])
            P.dma(W["cb"][:], cbF[layer].rearrange("(j p) -> p j", p=128), writes=[W["cb"].b])
            with ExitStack() as es2:
                stg = [C.sb(es2, [128, 2816]) for _ in range(3)]
                for k in range(8):
                    for hlf in range(2):
                        load_cast(es2, W["up"][:, k, hlf * 2816:(hlf + 1) * 2816], W["up"].bs[k], w_up[layer, k * 128:(k + 1) * 128, hlf * 2816:(hlf + 1) * 2816], 2816,
                                  scale_ap=gffn[:, layer, k:k + 1], scale_buf=gffn.b, stg=stg)
                for j in range(NJ):
                    load_cast(es2, W["dn"][:, j, :], W["dn"].bs[j], w_down[layer, j * 128:(j + 1) * 128, :], 1024, stg=stg)
                P.barrier()
            return W

        def ffn_alloc(es):
            A = {}
            A["pa"] = [C.ps(es, [128, 512]) for _ in range(3)]
            A["pg"] = [C.ps(es, [128, 512]) for _ in range(3)]
            A["pd"] = [C.ps(es, [128, 512]) for _ in range(2)]
            A["ac"] = [C.sb(es, [128, TT]) for _ in range(4)]
            A["gc"] = [C.sb(es, [128, TT]) for _ in range(4)]
            A["t1"] = [C.sb(es, [128, TT]) for _ in range(3)]
            A["t2"] = [C.sb(es, [128, TT]) for _ in range(2)]
            A["m"] = C.sb(es, [128, NJ, TT], BF16, nb=NJ)
            return A

        def ffn_body(W, A, hn, hnbuf, hres, hresbufs, hook=None, hook2=None):
            N = TT + 2
            cw = W["cw"]; cb = W["cb"]

            def bufs(j):
                return (A["pa"][j % 3], A["pg"][j % 3], A["ac"][j % 4], A["gc"][j % 4], A["t1"][j % 3], A["t2"][j % 2])

            def s0(j):
                pa, pg, ac, gc, t1, t2 = bufs(j)

                def mm(e, col, pt):
                    for k in range(8):
                        ins = e.matmul(pt[:, 0:N], lhsT=W["up"][:, k, col * 128:(col + 1) * 128], rhs=hn[:, k, 0:N], start=(k == 0), stop=(k == 7))
                    return ins
                op("pe", lambda e: mm(e, j, pa), list(W["up"].bs) + [hnbuf], [pa.b])
                op("pe", lambda e: mm(e, NJ + j, pg), list(W["up"].bs) + [hnbuf], [pg.b])

            def s1(j):
                pa, pg, ac, gc, t1, t2 = bufs(j)
                for (pt, dst, col) in ((pg, gc, NJ + j), (pa, ac, j)):
                    op("act", lambda e: e.activation(out=dst[:], in_=pt[:, 1:TT + 1], func=AF.Identity, scale=cw[:, 1, col:col + 1], bias=cb[:, col:col + 1]), [pt.b, cw.b, cb.b], [dst.b])
                    op("dve", lambda e: e.scalar_tensor_tensor(out=dst[:], in0=pt[:, 0:TT], scalar=cw[:, 0, col:col + 1], in1=dst[:], op0=ALU.mult, op1=ALU.add), [pt.b, cw.b, dst.b], [dst.b])
                    op("dve", lambda e: e.scalar_tensor_tensor(out=dst[:], in0=pt[:, 2:TT + 2], scalar=cw[:, 2, col:col + 1], in1=dst[:], op0=ALU.mult, op1=ALU.add), [pt.b, cw.b, dst.b], [dst.b])

            def s2(j):
                pa, pg, ac, gc, t1, t2 = bufs(j)
                op("act", lambda e: e.activation(out=t1[:], in_=gc[:], func=AF.Gelu_apprx_tanh), [gc.b], [t1.b])

            def s3(j):
                pa, pg, ac, gc, t1, t2 = bufs(j)
                op("pool", lambda e: e.tensor_tensor(out=A["m"][:, j, :], in0=t1[:], in1=ac[:], op=ALU.mult), [t1.b, ac.b], [A["m"].bs[j]])

            stg_ = [s0, s1, s2, s3]
            for step in range(NJ + len(stg_) - 1):
                for si, f in enumerate(stg_):
                    it = step - si
                    if 0 <= it < NJ:
                        f(it)
                        if si == 0 and it == NJ - 1 and hook is not None:
                            hook()
                if step == 6 and hook2 is not None:
                    hook2()
            JH = NJ - 3
            banks4 = [A["pd"][0], A["pd"][1], A["pa"][1], A["pg"][1]]

            def mm_head(e):
                for j in range(JH):
                    for g in range(4):
                        ins = e.matmul(banks4[g][:, 0:TT], lhsT=W["dn"][:, j, g * 128:(g + 1) * 128], rhs=A["m"][:, j, :], start=(j == 0), stop=False)
                return ins
            op("pe", mm_head, list(W["dn"].bs) + list(A["m"].bs[0:JH]), [bk.b for bk in banks4])

            def mm_tail(e):
                for g in range(4):
                    for j in range(JH, NJ):
                        ins = e.matmul(banks4[g][:, 0:TT], lhsT=W["dn"][:, j, g * 128:(g + 1) * 128], rhs=A["m"][:, j, :], start=False, stop=(j == NJ - 1))
                return ins
            op("pe", mm_tail, list(W["dn"].bs) + list(A["m"].bs[JH:NJ]), [bk.b for bk in banks4])
            for g in range(4):
                op("dve", lambda e, g=g: e.tensor_tensor(out=hres[:, g, 1:TT + 1], in0=banks4[g][:, 0:TT], in1=hres[:, g, 1:TT + 1], op=ALU.add),
                   [banks4[g].b, hresbufs[g]], [hresbufs[g]])
            for o in range(4, 8):
                pd = A["pd"][o % 2]

                def mmd(e, o=o, pd=pd):
                    for j in range(NJ):
                        ins = e.matmul(pd[:, 0:TT], lhsT=W["dn"][:, j, o * 128:(o + 1) * 128], rhs=A["m"][:, j, :], start=(j == 0), stop=(j == NJ - 1))
                    return ins
                op("pe", mmd, list(W["dn"].bs) + list(A["m"].bs), [pd.b])
                op("dve", lambda e, o=o, pd=pd: e.tensor_tensor(out=hres[:, o, 1:TT + 1], in0=pd[:, 0:TT], in1=hres[:, o, 1:TT + 1], op=ALU.add),
                   [pd.b, hresbufs[o]], [hresbufs[o]])

        with ExitStack() as es:
            win = C.sb(es, [128, 8, 1536], BF16, nb=8)
            with ExitStack() as es2:
                stg = [C.sb(es2, [128, 1536]) for _ in range(3)]
                for k in range(8):
                    load_cast(es2, win[:, k, :], win.bs[k], w_in[0, k * 128:(k + 1) * 128, :], 1536, scale_ap=gmix[:, 0, k:k + 1], scale_buf=gmix.b, stg=stg)
                P.barrier()
            xtok = [C.sb(es, [128, 3, D]) for _ in range(2)]
            h0 = [C.sb(es, [128, 8, TT], nb=8) for _ in range(2)]
            sq = C.sb(es, [128, 8, TT], BF16, nb=8); rstd = C.sb(es, [128, TT]); hnl = [C.sb(es, [128, 8, TT], BF16) for _ in range(2)]
            zt = [C.sb(es, [128, 12, TT], nb=12) for _ in range(2)]
            ptr = [C.ps(es, [128, 512]) for _ in range(2)]; pss = C.ps(es, [128, 512]); pz = [C.ps(es, [128, 512]) for _ in range(3)]
            blks = [(0, 128), (128, 128), (256, TT - 256)]

            def xload(ti):
                t0 = ti * TT; xt = xtok[ti % 2]
                for bi, (o, nb_) in enumerate(blks):
                    P.dma(xt[0:nb_, bi, :], xs[t0 + o:t0 + o + nb_, :], writes=[xt.b])

            def R0(ti):
                t0 = ti * TT; s = ti % 2
                xt = xtok[s]; h = h0[s]
                if ti == 0:
                    xload(0)
                if ti + 1 < NTILE:
                    xload(ti + 1)
                for k in range(8):
                    pt = ptr[k % 2]

                    def tr(e):
                        for bi, (o, nb_) in enumerate(blks):
                            ins = e.transpose(out=pt[:, o:o + nb_], in_=xt[0:nb_, bi, k * 128:(k + 1) * 128], identity=ident[0:nb_, 0:nb_])
                        return ins
                    op("pe", tr, [xt.b, ident.b], [pt.b])
                    if k % 2 == 0:
                        op("act", lambda e: e.copy(out=h[:, k, :], in_=pt[:, 0:TT]), [pt.b], [h.bs[k]])
                    else:
                        op("dve", lambda e: e.tensor_copy(out=h[:, k, :], in_=pt[:, 0:TT]), [pt.b], [h.bs[k]])
                P.dma(hs0.rearrange("k p t -> p k t")[:, :, t0:t0 + TT], h[:], reads=list(h.bs), writes=[hs0B])

            def R1(ti):
                h = h0[ti % 2]; hn = hnl[ti % 2]
                rmsnorm(es, h, h.bs, hn, hn.b, TT, pss, sq, rstd)

            def R2(ti):
                t0 = ti * TT; z = zt[ti % 2]; hn = hnl[ti % 2]
                for o in range(12):
                    pzz = pz[o % 3]

                    def mm(e):
                        for k in range(8):
                            ins = e.matmul(pzz[:, 0:TT], lhsT=win[:, k, o * 128:(o + 1) * 128], rhs=hn[:, k, :], start=(k == 0), stop=(k == 7))
                        return ins
                    op("pe", mm, list(win.bs) + [hn.b], [pzz.b])
                    if o % 2 == 0:
                        op("act", lambda e: e.copy(out=z[:, o, :], in_=pzz[:, 0:TT]), [pzz.b], [z.bs[o]])
                    else:
                        op("dve", lambda e: e.tensor_copy(out=z[:, o, :], in_=pzz[:, 0:TT]), [pzz.b], [z.bs[o]])
                P.dma(zs.rearrange("k p t -> p k t")[:, :, t0:t0 + TT], z[:], reads=list(z.bs), writes=[zsB])

            pipeline(NTILE, [R0, R1, R2])
            P.barrier()

        with ExitStack() as es:
            cw = C.sb(es, [128, 4, 4]); cbt = C.sb(es, [128, 4]); ncw = C.sb(es, [128, 4, 4])
            br = C.sb(es, [128, 2, 4]); bi_ = C.sb(es, [128, 2, 4]); cp = C.sb(es, [128, 2, 4])
            P.dma(cw[:], cwB[0].rearrange("k (q p) -> p k q", p=128), writes=[cw.b])
            P.dma(cbt[:], cbB[0].rearrange("(q p) -> p q", p=128), writes=[cbt.b])
            P.dma(br[:], b_r[0].rearrange("d (q p) -> p d q", p=128), writes=[br.b])
            P.dma(bi_[:], b_i[0].rearrange("d (q p) -> p d q", p=128), writes=[bi_.b])
            P.dma(cp[:], lru_lam[0].rearrange("d (q p) -> p d q", p=128), writes=[cp.b])
            op("act", lambda e: e.activation(out=cp[:], in_=cp[:], func=AF.Exp, scale=-1.0), [cp.b], [cp.b])
            op("act", lambda e: e.activation(out=cp[:], in_=cp[:], func=AF.Ln, bias=1.0), [cp.b], [cp.b])
            op("dve", lambda e: e.tensor_scalar(out=cp[:], in0=cp[:], scalar1=-8.0, scalar2=None, op0=ALU.mult), [cp.b], [cp.b])
            op("dve", lambda e: e.tensor_scalar(out=ncw[:], in0=cw[:], scalar1=nfcol, scalar2=-1.0, op0=ALU.mult, op1=ALU.mult), [cw.b, flg.b], [ncw.b])
            wg = C.sb(es, [128, 2, 2, 4, 128], BF16)
            with ExitStack() as es2:
                wst = C.sb(es2, [128, 2, 2, 4, 128])
                op("dve", lambda e: e.memset(wst[:], 0.0), [], [wst.b])
                for gi, wsrc in enumerate((w_r, w_i)):
                    for d in range(2):
                        for q in range(4):
                            for hh in range(2):
                                P.dma(wst[64 * hh:64 * hh + 64, gi, d, q, 64 * hh:64 * hh + 64], wsrc[0, d, 2 * q + hh, :, :], writes=[wst.b])
                op("dve", lambda e: e.tensor_copy(out=wg[:], in_=wst[:]), [wst.b], [wg.b])
                P.barrier()
            xb = C.sb(es, [128, T]); gb = C.sb(es, [128, T]); xc = C.sb(es, [128, T]); xcb = C.sb(es, [128, T], BF16); hsum = C.sb(es, [128, T])
            rr = [C.sb(es, [128, CH]) for _ in range(2)]; ii = [C.sb(es, [128, CH]) for _ in range(3)]
            aa = [C.sb(es, [128, CH]) for _ in range(2)]; ss_ = [C.sb(es, [128, CH]) for _ in range(2)]
            hb = [C.sb(es, [128, CH]) for _ in range(2)]
            carry = C.sb(es, [128, 1]); ybc = [C.sb(es, [128, CH], BF16) for _ in range(2)]
            pgr = [C.ps(es, [128, 3, 512]) for _ in range(2)]
            for q in range(4):
                if q == 0:
                    P.dma(xb[:], zs[4 + q, :, :], reads=[zsB], writes=[xb.b])
                P.dma(gb[:], zs[8 + q, :, :], reads=[zsB], writes=[gb.b])
                op("act", lambda e, q=q: e.activation(out=xc[:], in_=xb[:], func=AF.Identity, scale=cw[:, 2, q:q + 1], bias=cbt[:, q:q + 1]), [xb.b, cw.b, cbt.b], [xc.b])
                op("dve", lambda e, q=q: e.scalar_tensor_tensor(out=xc[:, 2:T], in0=xb[:, 0:T - 2], scalar=cw[:, 0, q:q + 1], in1=xc[:, 2:T], op0=ALU.mult, op1=ALU.add), [xb.b, xc.b, cw.b], [xc.b])
                op("dve", lambda e, q=q: e.scalar_tensor_tensor(out=xc[:, 1:T], in0=xb[:, 0:T - 1], scalar=cw[:, 1, q:q + 1], in1=xc[:, 1:T], op0=ALU.mult, op1=ALU.add), [xb.b, xc.b, cw.b], [xc.b])
                op("dve", lambda e, q=q: e.scalar_tensor_tensor(out=xc[:, 0:T - 1], in0=xb[:, 1:T], scalar=cw[:, 3, q:q + 1], in1=xc[:, 0:T - 1], op0=ALU.mult, op1=ALU.add), [xb.b, xc.b, cw.b], [xc.b])
                for sgi in range(1, NSEG):
                    B_ = sgi * SEG
                    for (to, fo, kk) in ((B_ - 1, B_, 3), (B_, B_ - 1, 1), (B_, B_ - 2, 0), (B_ + 1, B_ - 1, 0)):
                        op("dve", lambda e, to=to, fo=fo, kk=kk, q=q: e.scalar_tensor_tensor(out=xc[:, to:to + 1], in0=xb[:, fo:fo + 1], scalar=ncw[:, kk, q:q + 1], in1=xc[:, to:to + 1],
                                                                                       op0=ALU.mult, op1=ALU.add), [xb.b, xc.b, ncw.b], [xc.b])
                op("pool", lambda e: e.tensor_copy(out=xcb[:], in_=xc[:]), [xc.b], [xcb.b])
                if q + 1 < 4:
                    P.dma(xb[:], zs[4 + q + 1, :, :], reads=[zsB], writes=[xb.b])
                for d in range(2):
                    order = list(range(NCH)) if d == 0 else list(range(NCH - 1, -1, -1))

                    def L0(ci, d=d, q=q, order=order):
                        c = order[ci]; c0 = c * CH
                        pg = pgr[ci % 2]
                        for gi, dst, bias_ in ((0, rr[ci % 2], br), (1, ii[ci % 3], bi_)):
                            def mm(e):
                                for u in range(3):
                                    ins = e.matmul(pg[:, u, 0:TT], lhsT=wg[:, gi, d, q, :], rhs=xcb[:, c0 + u * TT:c0 + (u + 1) * TT], start=True, stop=True)
                                return ins
                            op("pe", mm, [wg.b, xcb.b], [pg.b])
                            op("act", lambda e: e.activation(out=dst[:].rearrange("p (u t) -> p u t", u=3), in_=pg[:, :, 0:TT], func=AF.Sigmoid, bias=bias_[:, d, q:q + 1]), [pg.b, bias_.b], [dst.b])

                    def L1(ci, d=d, q=q, order=order):
                        c = order[ci]; c0 = c * CH
                        r_ = rr[ci % 2]; i_ = ii[ci % 3]; a_ = aa[ci % 2]; s_ = ss_[ci % 2]
                        op("act", lambda e: e.activation(out=a_[:], in_=r_[:], func=AF.Exp, scale=cp[:, d, q:q + 1]), [r_.b, cp.b], [a_.b])
                        op("pool", lambda e: e.tensor_tensor(out=s_[:], in0=a_[:], in1=a_[:], op=ALU.mult), [a_.b], [s_.b])
                        op("act", lambda e: e.activation(out=s_[:], in_=s_[:], func=AF.Sqrt, scale=-1.0, bias=1.0), [s_.b], [s_.b])
                        op("pool", lambda e: e.tensor_tensor(out=i_[:], in0=i_[:], in1=xc[:, c0:c0 + CH], op=ALU.mult), [i_.b, xc.b], [i_.b])
                        op("dve", lambda e: e.tensor_tensor(out=i_[:], in0=i_[:], in1=s_[:], op=ALU.mult), [i_.b, s_.b], [i_.b])
                        if c == NCH - 1:
                            pc = PADLO - c0
                            op("dve", lambda e: e.tensor_scalar(out=i_[:, pc:CH], in0=i_[:, pc:CH], scalar1=nfcol, scalar2=None, op0=ALU.mult), [i_.b, flg.b], [i_.b])

                    def L2(ci, d=d, q=q, order=order):
                        c = order[ci]; c0 = c * CH
                        i_ = ii[ci % 3]; a_ = aa[ci % 2]
                        if ci == 0:
                            init = 0.0; rd = []
                        else:
                            init = carry[:, 0:1]; rd = [carry.b]
                        if d == 0:
                            op("dve", lambda e: e.tensor_tensor_scan(out=hsum[:, c0:c0 + CH], data0=a_[:], data1=i_[:], initial=init, op0=ALU.mult, op1=ALU.add), [a_.b, i_.b] + rd, [hsum.b])
                            last = hsum[:, c0 + CH - 1:c0 + CH]; lastb = hsum.b
                        else:
                            hbt = hb[ci % 2]
                            op("dve", lambda e: e.tensor_tensor_scan(out=hbt[:, ::-1], data0=a_[:, ::-1], data1=i_[:, ::-1], initial=init, op0=ALU.mult, op1=ALU.add), [a_.b, i_.b] + rd, [hbt.b])
                            last = hbt[:, 0:1]; lastb = hbt.b
                        crossing = (c % 2 == 1) if d == 0 else (c % 2 == 0)
                        if ci < NCH - 1:
                            if crossing:
                                op("act", lambda e: e.activation(out=carry[:], in_=last, func=AF.Copy, scale=fcol), [lastb, flg.b], [carry.b])
                            else:
                                op("act", lambda e: e.copy(out=carry[:], in_=last), [lastb], [carry.b])
                        if d == 1:
                            op("pool", lambda e: e.tensor_tensor(out=hsum[:, c0:c0 + CH], in0=hsum[:, c0:c0 + CH], in1=hbt[:], op=ALU.add), [hbt.b, hsum.b], [hsum.b])

                    pipeline(NCH, [L0, L1, L2])
                for c in range(NCH):
                    c0 = c * CH; s = c % 2
                    t1 = rr[s]; yb = ybc[s]
                    op("act", lambda e: e.activation(out=t1[:], in_=gb[:, c0:c0 + CH], func=AF.Gelu_apprx_tanh), [gb.b], [t1.b])
                    op("dve", lambda e: e.tensor_tensor(out=yb[:], in0=t1[:], in1=hsum[:, c0:c0 + CH], op=ALU.mult), [t1.b, hsum.b], [yb.b])
                    P.dma(mixs[4 + q, :, c0:c0 + CH], yb[:], reads=[yb.b], writes=[mixB])
            P.barrier()

        with ExitStack() as es:
            rdec = C.sb(es, [128, 32]); phi = C.sb(es, [128, 32], I32)
            WB = C.sb(es, [128, 32, 2, 128], BF16)
            WC = C.sb(es, [128, 32, 3, 128], BF16)
            dsk = C.sb(es, [128, 4])
            P.dma(dsk[:], s5_d[0].rearrange("(q p) -> p q", p=128), writes=[dsk.b])
            with ExitStack() as es2:
                lre = C.sb(es2, [128, 32]); lim = C.sb(es2, [128, 32]); ldt = C.sb(es2, [128, 32])
                Bre = C.sb(es2, [128, 32, 16]); Bim = C.sb(es2, [128, 32, 16])
                for gl in range(2):
                    sl = slice(64 * gl, 64 * gl + 64)
                    P.dma(lre[sl, :].rearrange("p (d j) -> p d j", d=2), lam_re[0].rearrange("d (j g) n -> g n d j", g=2)[gl], writes=[lre.b])
                    P.dma(lim[sl, :].rearrange("p (d j) -> p d j", d=2), lam_im[0].rearrange("d (j g) n -> g n d j", g=2)[gl], writes=[lim.b])
                    P.dma(ldt[sl, :].rearrange("p (d j) -> p d j", d=2), bc(log_dt[0].rearrange("d (j g) -> g d j", g=2)[gl:gl + 1], [64, 2, 16]), writes=[ldt.b])
                    for d in range(2):
                        P.dma(Bre[sl, d * 16:(d + 1) * 16, :], b_re[0, d].rearrange("(j g) n c -> g n j c", g=2)[gl], writes=[Bre.b])
                        P.dma(Bim[sl, d * 16:(d + 1) * 16, :], b_im[0, d].rearrange("(j g) n c -> g n j c", g=2)[gl], writes=[Bim.b])
                dtt = C.sb(es2, [128, 32]); xr = C.sb(es2, [128, 32]); xi = C.sb(es2, [128, 32]); er = C.sb(es2, [128, 32])
                ki = C.sb(es2, [128, 32], I32); kf = C.sb(es2, [128, 32]); fr = C.sb(es2, [128, 32]); pi_ = C.sb(es2, [128, 32], I32); pc_ = C.sb(es2, [128, 32], I32)
                cs = C.sb(es2, [128, 32]); sn = C.sb(es2, [128, 32]); q30 = C.sb(es2, [128, 32], I32)
                nr = C.sb(es2, [128, 32]); ni = C.sb(es2, [128, 32]); den = C.sb(es2, [128, 32]); cr = C.sb(es2, [128, 32]); ci_ = C.sb(es2, [128, 32]); tmp = C.sb(es2, [128, 32]); tmp2 = C.sb(es2, [128, 32])
                A1 = lambda eng, fn, r, w: op(eng, fn, [x.b for x in r], [x.b for x in w])
                A1("dve", lambda e: e.tensor_scalar(out=lre[:], in0=lre[:], scalar1=-1e-4, scalar2=None, op0=ALU.min), [lre], [lre])
                A1("act", lambda e: e.activation(out=dtt[:], in_=ldt[:], func=AF.Exp), [ldt], [dtt])
                A1("dve", lambda e: e.tensor_tensor(out=xr[:], in0=lre[:], in1=dtt[:], op=ALU.mult), [lre, dtt], [xr])
                A1("dve", lambda e: e.tensor_tensor(out=xi[:], in0=lim[:], in1=dtt[:], op=ALU.mult), [lim, dtt], [xi])
                A1("act", lambda e: e.activation(out=rdec[:], in_=xr[:], func=AF.Exp), [xr], [rdec])
                A1("dve", lambda e: e.tensor_scalar(out=fr[:], in0=xi[:], scalar1=float(1.0 / (2 * math.pi)), scalar2=None, op0=ALU.mult), [xi], [fr])
                A1("dve", lambda e: e.tensor_copy(out=ki[:], in_=fr[:]), [fr], [ki])
                A1("dve", lambda e: e.tensor_copy(out=kf[:], in_=ki[:]), [ki], [kf])
                A1("dve", lambda e: e.tensor_tensor(out=fr[:], in0=fr[:], in1=kf[:], op=ALU.subtract), [fr, kf], [fr])
                A1("dve", lambda e: e.tensor_scalar(out=fr[:], in0=fr[:], scalar1=float(2 ** 31), scalar2=None, op0=ALU.mult), [fr], [fr])
                A1("dve", lambda e: e.tensor_copy(out=pi_[:], in_=fr[:]), [fr], [pi_])
                A1("pool", lambda e: e.tensor_tensor(out=pi_[:], in0=pi_[:], in1=pi_[:], op=ALU.add), [pi_], [pi_])
                A1("pool", lambda e: e.iota(q30[:], pattern=[[0, 32]], base=2 ** 30, channel_multiplier=0), [], [q30])
                A1("pool", lambda e: e.tensor_tensor(out=pc_[:], in0=pi_[:], in1=q30[:], op=ALU.add), [pi_, q30], [pc_])
                A1("act", lambda e: e.activation(out=sn[:], in_=pi_[:], func=AF.Sin, scale=float(2 * math.pi / 2 ** 32)), [pi_], [sn])
                A1("act", lambda e: e.activation(out=cs[:], in_=pc_[:], func=AF.Sin, scale=float(2 * math.pi / 2 ** 32)), [pc_], [cs])
                A1("dve", lambda e: e.tensor_tensor(out=nr[:], in0=rdec[:], in1=cs[:], op=ALU.mult), [rdec, cs], [nr])
                A1("dve", lambda e: e.tensor_scalar(out=nr[:], in0=nr[:], scalar1=-1.0, scalar2=None, op0=ALU.add), [nr], [nr])
                A1("dve", lambda e: e.tensor_tensor(out=ni[:], in0=rdec[:], in1=sn[:], op=ALU.mult), [rdec, sn], [ni])
                A1("dve", lambda e: e.tensor_tensor(out=den[:], in0=lre[:], in1=lre[:], op=ALU.mult), [lre], [den])
                A1("dve", lambda e: e.tensor_tensor(out=tmp[:], in0=lim[:], in1=lim[:], op=ALU.mult), [lim], [tmp])
                A1("dve", lambda e: e.tensor_tensor(out=den[:], in0=den[:], in1=tmp[:], op=ALU.add), [den, tmp], [den])
                A1("dve", lambda e: e.reciprocal(out=den[:], in_=den[:]), [den], [den])
                A1("dve", lambda e: e.tensor_tensor(out=cr[:], in0=nr[:], in1=lre[:], op=ALU.mult), [nr, lre], [cr])
                A1("dve", lambda e: e.tensor_tensor(out=tmp[:], in0=ni[:], in1=lim[:], op=ALU.mult), [ni, lim], [tmp])
                A1("dve", lambda e: e.tensor_tensor(out=cr[:], in0=cr[:], in1=tmp[:], op=ALU.add), [cr, tmp], [cr])
                A1("dve", lambda e: e.tensor_tensor(out=cr[:], in0=cr[:], in1=den[:], op=ALU.mult), [cr, den], [cr])
                A1("dve", lambda e: e.tensor_tensor(out=ci_[:], in0=ni[:], in1=lre[:], op=ALU.mult), [ni, lre], [ci_])
                A1("dve", lambda e: e.tensor_tensor(out=tmp2[:], in0=nr[:], in1=lim[:], op=ALU.mult), [nr, lim], [tmp2])
                A1("dve", lambda e: e.tensor_tensor(out=ci_[:], in0=ci_[:], in1=tmp2[:], op=ALU.subtract), [ci_, tmp2], [ci_])
                A1("dve", lambda e: e.tensor_tensor(out=ci_[:], in0=ci_[:], in1=den[:], op=ALU.mult), [ci_, den], [ci_])
                A1("pool", lambda e: e.tensor_copy(out=phi[:], in_=pi_[:]), [pi_], [phi])
                XR = C.sb(es2, [128, 32, 128]); XI = C.sb(es2, [128, 32, 128]); Tm = C.sb(es2, [128, 32, 16]); Tm2 = C.sb(es2, [128, 32, 16])
                A1("pool", lambda e: e.memset(XR[:], 0.0), [], [XR])
                A1("pool", lambda e: e.memset(XI[:], 0.0), [], [XI])
                crb = lambda: bc(cr[:].unsqueeze(2), [128, 32, 16]); cib = lambda: bc(ci_[:].unsqueeze(2), [128, 32, 16])
                A1("dve", lambda e: e.tensor_tensor(out=Tm[:], in0=Bre[:], in1=crb(), op=ALU.mult), [Bre, cr], [Tm])
                A1("dve", lambda e: e.tensor_tensor(out=Tm2[:], in0=Bim[:], in1=cib(), op=ALU.mult), [Bim, ci_], [Tm2])
                A1("dve", lambda e: e.tensor_tensor(out=Tm[:], in0=Tm[:], in1=Tm2[:], op=ALU.subtract), [Tm, Tm2], [Tm])
                A1("dve", lambda e: e.tensor_tensor(out=Tm2[:], in0=Bre[:], in1=cib(), op=ALU.mult), [Bre, ci_], [Tm2])
                A1("dve", lambda e: e.tensor_tensor(out=Bre[:], in0=Bim[:], in1=crb(), op=ALU.mult), [Bim, cr, Bre], [Bre])
                A1("dve", lambda e: e.tensor_tensor(out=Tm2[:], in0=Tm2[:], in1=Bre[:], op=ALU.add), [Tm2, Bre], [Tm2])
                for st in range(32):
                    j = st % 16
                    for gl in range(2):
                        col = ((2 * j + gl) % 8) * 16
                        sl = slice(64 * gl, 64 * gl + 64)
                        A1("dve", lambda e, st=st, sl=sl, col=col: e.tensor_copy(out=XR[sl, st, col:col + 16], in_=Tm[sl, st, :]), [Tm], [XR])
                        A1("pool", lambda e, st=st, sl=sl, col=col: e.tensor_copy(out=XI[sl, st, col:col + 16], in_=Tm2[sl, st, :]), [Tm2], [XI])
                ptb = [C.ps(es2, [128, 512]) for _ in range(2)]
                for st in range(32):
                    for ri, X in enumerate((XR, XI)):
                        pt = ptb[(2 * st + ri) % 2]
                        op("pe", lambda e, st=st, X=X, pt=pt: e.transpose(out=pt[:, 0:128], in_=X[:, st, :], identity=ident[:]), [X.b, ident.b], [pt.b])
                        op("act", lambda e, st=st, ri=ri, pt=pt: e.copy(out=WB[:, st, ri, :], in_=pt[:, 0:128]), [pt.b], [WB.b])
                A1("pool", lambda e: e.memset(XR[:], 0.0), [WB], [XR])
                A1("pool", lambda e: e.memset(XI[:], 0.0), [WB], [XI])
                for st in range(32):
                    d = st // 16; j = st % 16
                    for gl in range(2):
                        g = 2 * j + gl; col = (g % 8) * 16
                        sl = slice(64 * gl, 64 * gl + 64)
                        P.dma(XR[sl, st, col:col + 16], c_re[0, d, g].rearrange("c n -> n c"), writes=[XR.b])
                        P.dma(XI[sl, st, col:col + 16], c_im[0, d, g].rearrange("c n -> n c"), writes=[XI.b])
                A1("dve", lambda e: e.tensor_copy(out=WC[:, :, 0, :], in_=XR[:]), [XR], [WC])
                A1("dve", lambda e: e.tensor_scalar(out=WC[:, :, 1, :], in0=XR[:], scalar1=-1.0, scalar2=None, op0=ALU.mult), [XR], [WC])
                A1("dve", lambda e: e.tensor_scalar(out=WC[:, :, 2, :], in0=XI[:], scalar1=-1.0, scalar2=None, op0=ALU.mult), [XI], [WC])
                P.barrier()
            CH1 = CH + 1
            c30 = C.sb(es, [128, CH1], I32)
            op("pool", lambda e: e.iota(c30[:], pattern=[[0, CH1]], base=2 ** 30, channel_multiplier=0), [], [c30.b])
            ub = C.sb(es, [128, T], BF16); yacc = C.sb(es, [128, T])
            an = C.sb(es, [128, CH1], I32); anc = C.sb(es, [128, CH1], I32)
            snT = [C.sb(es, [128, CH1]) for _ in range(2)]; csT = [C.sb(es, [128, CH1]) for _ in range(2)]
            rotc = [C.sb(es, [128, 8]) for _ in range(2)]
            NB2 = 2
            mk = lambda dt=F32: [C.sb(es, [128, CH], dt) for _ in range(NB2)]
            bre = [C.sb(es, [128, CH]) for _ in range(3)]; bim = [C.sb(es, [128, CH]) for _ in range(3)]
            ta = mk(); tc = mk(); wr_ = mk(); wi_ = mk(); gr = mk(); gi_ = mk()
            pa_ = mk(BF16); pb2 = mk(BF16); pc2 = mk(BF16); pd2 = mk(BF16)
            car = C.sb(es, [128, 4])
            pbu = [C.ps(es, [128, 512]) for _ in range(2)]; pyy = [C.ps(es, [128, 3, 512]) for _ in range(2)]
            SC = float(2 * math.pi / 2 ** 32)
            v3 = lambda ap2: ap2.rearrange("p (u t) -> p u t", u=3)
            for ctile in range(4):
                for c in range(NCH):
                    c0 = c * CH; uf = wr_[c % NB2]
                    P.dma(uf[:], zs[ctile, :, c0:c0 + CH], reads=[zsB], writes=[uf.b])
                    op("pool", lambda e: e.tensor_copy(out=ub[:, c0:c0 + CH], in_=uf[:]), [uf.b], [ub.b])
                    op("act", lambda e: e.activation(out=yacc[:, c0:c0 + CH], in_=uf[:], func=AF.Copy, scale=dsk[:, ctile:ctile + 1]), [uf.b, dsk.b], [yacc.b])
                its = []
                for jj in range(4):
                    for d in range(2):
                        st = d * 16 + ctile * 4 + jj
                        order = list(range(NCH)) if d == 0 else list(range(NCH - 1, -1, -1))
                        for ci, c in enumerate(order):
                            its.append((st, d, ci, c))

                def tables(st, par):
                    sn = snT[par]; cs = csT[par]; rc = rotc[par]
                    op("pool", lambda e: e.iota(an[:], pattern=[[1, CH1]], base=0, channel_multiplier=0), [], [an.b])
                    op("pool", lambda e: e.tensor_tensor(out=an[:], in0=an[:], in1=bc(phi[:, st:st + 1], [128, CH1]), op=ALU.mult), [an.b, phi.b], [an.b])
                    op("pool", lambda e: e.tensor_tensor(out=anc[:], in0=an[:], in1=c30[:], op=ALU.add), [an.b, c30.b], [anc.b])
                    op("act", lambda e: e.activation(out=sn[:], in_=an[:], func=AF.Sin, scale=SC), [an.b], [sn.b])
                    op("act", lambda e: e.activation(out=cs[:], in_=anc[:], func=AF.Sin, scale=SC), [anc.b], [cs.b])
                    op("act", lambda e: e.copy(out=rc[:, 0:1], in_=cs[:, CH:CH1]), [cs.b], [rc.b])
                    op("act", lambda e: e.copy(out=rc[:, 1:2], in_=sn[:, CH:CH1]), [sn.b], [rc.b])
                    op("act", lambda e: e.mul(out=rc[:, 2:3], in_=sn[:, CH:CH1], mul=-1.0), [sn.b], [rc.b])
                    op("act", lambda e: e.activation(out=rc[:, 3:6], in_=rc[:, 0:3], func=AF.Copy, scale=fcol), [rc.b, flg.b], [rc.b])

                def B0(k):
                    st, d, ci, c = its[k]
                    c0 = c * CH; s = k % 3
                    if ci == 0:
                        tables(st, (k // NCH) % 2)
                    for u in range(3):
                        for ri, dstt in ((0, bre[s]), (1, bim[s])):
                            pb_ = pbu[ri]
                            op("pe", lambda e: e.matmul(pb_[:, 0:TT], lhsT=WB[:, st, ri, :], rhs=ub[:, c0 + u * TT:c0 + (u + 1) * TT], start=True, stop=True), [WB.b, ub.b], [pb_.b])
                            op("act", lambda e: e.copy(out=dstt[:, u * TT:(u + 1) * TT], in_=pb_[:, 0:TT]), [pb_.b], [dstt.b])

                def B12(k):
                    st, d, ci, c = its[k]
                    s = k % NB2; par = (k // NCH) % 2
                    sn = snT[par]; cs = csT[par]; rc = rotc[par]
                    br_ = bre[k % 3]; bi2 = bim[k % 3]; t_a = ta[s]; t_c = tc[s]; wr = wr_[s]; wi = wi_[s]; g_r = gr[s]; g_i = gi_[s]
                    cs2 = cs[:, 0:CH]; sn2 = sn[:, 0:CH]
                    brv = br_[:] if d == 0 else br_[:, ::-1]
                    biv = bi2[:] if d == 0 else bi2[:, ::-1]
                    op("dve", lambda e: e.tensor_tensor(out=wr[:], in0=brv, in1=cs2, op=ALU.mult), [br_.b, cs.b], [wr.b])
                    op("dve", lambda e: e.tensor_tensor(out=t_a[:], in0=biv, in1=sn2, op=ALU.mult), [bi2.b, sn.b], [t_a.b])
                    op("dve", lambda e: e.tensor_tensor(out=wr[:], in0=wr[:], in1=t_a[:], op=ALU.add), [wr.b, t_a.b], [wr.b])
                    op("dve", lambda e: e.tensor_tensor(out=wi[:], in0=biv, in1=cs2, op=ALU.mult), [bi2.b, cs.b], [wi.b])
                    op("dve", lambda e: e.tensor_tensor(out=t_c[:], in0=brv, in1=sn2, op=ALU.mult), [br_.b, sn.b], [t_c.b])
                    op("dve", lambda e: e.tensor_tensor(out=wi[:], in0=wi[:], in1=t_c[:], op=ALU.subtract), [wi.b, t_c.b], [wi.b])
                    dec = bc(rdec[:, st:st + 1], [128, CH])
                    for ri, (src, dst) in enumerate(((wr, g_r), (wi, g_i))):
                        if ci == 0:
                            init = 0.0; rd = []
                        else:
                            init = car[:, ri:ri + 1]; rd = [car.b]
                        op("dve", lambda e: e.tensor_tensor_scan(out=dst[:], data0=dec, data1=src[:], initial=init, op0=ALU.mult, op1=ALU.add), [src.b, rdec.b] + rd, [dst.b])
                    if ci < NCH - 1:
                        crossing = (c % 2 == 1) if d == 0 else (c % 2 == 0)
                        o3 = 3 if crossing else 0
                        lr = g_r[:, CH - 1:CH]; li = g_i[:, CH - 1:CH]
                        op("act", lambda e: e.activation(out=car[:, 2:3], in_=li, func=AF.Copy, scale=rc[:, o3 + 2:o3 + 3]), [g_i.b, rc.b], [car.b])
                        op("act", lambda e: e.activation(out=car[:, 0:1], in_=lr, func=AF.Identity, scale=rc[:, o3 + 0:o3 + 1], bias=car[:, 2:3]), [g_r.b, rc.b, car.b], [car.b])
                        op("act", lambda e: e.activation(out=car[:, 3:4], in_=li, func=AF.Copy, scale=rc[:, o3 + 0:o3 + 1]), [g_i.b, rc.b], [car.b])
                        op("act", lambda e: e.activation(out=car[:, 1:2], in_=lr, func=AF.Identity, scale=rc[:, o3 + 1:o3 + 2], bias=car[:, 3:4]), [g_r.b, rc.b, car.b], [car.b])

                def B3p(k, eng):
                    st, d, ci, c = its[k]
                    s = k % NB2; par = (k // NCH) % 2
                    sn = snT[par]; cs = csT[par]
                    g_r = gr[s]; g_i = gi_[s]
                    cs2 = cs[:, 0:CH]; sn2 = sn[:, 0:CH]
                    if eng == "pool":
                        lst = ((pa_[s], g_r, cs2, cs), (pb2[s], g_i, sn2, sn))
                        extra = [wi_[(k + 1) % NB2].b] if k + 1 < n_it else []
                    elif eng == "dve2":
                        eng = "dve"
                        lst = ((pa_[s], g_r, cs2, cs), (pb2[s], g_i, sn2, sn))
                        extra = []
                    else:
                        lst = ((pc2[s], g_r, sn2, sn), (pd2[s], g_i, cs2, cs))
                        extra = []
                    for (dst, src, tab, tabb) in lst:
                        dv = dst[:] if d == 0 else dst[:, ::-1]
                        op(eng, lambda e: e.tensor_tensor(out=dv, in0=src[:], in1=tab, op=ALU.mult), [src.b, tabb.b] + extra, [dst.b])

                def B4a(k):
                    st, d, ci, c = its[k]
                    s = k % NB2
                    py = pyy[k % 2]
                    terms = ((0, pa_[s]), (1, pb2[s]), (2, pc2[s]), (2, pd2[s]))

                    c0 = c * CH

                    def mmy(e):
                        for u in range(3):
                            e.matmul(py[:, u, 0:TT], lhsT=ident[:], rhs=yacc[:, c0 + u * TT:c0 + (u + 1) * TT], start=True, stop=False)
                            for ti_, (slot, src) in enumerate(terms):
                                ins = e.matmul(py[:, u, 0:TT], lhsT=WC[:, st, slot, :], rhs=src[:, u * TT:(u + 1) * TT], start=False, stop=(ti_ == 3))
                        return ins
                    op("pe", mmy, [WC.b, ident.b, yacc.b] + [t_[1].b for t_ in terms], [py.b])

                def B4b(k):
                    st, d, ci, c = its[k]
                    c0 = c * CH
                    py = pyy[k % 2]
                    op("act", lambda e: e.copy(out=v3(yacc[:, c0:c0 + CH]), in_=py[:, :, 0:TT]), [py.b], [yacc.b])

                n_it = len(its)
                B0(0)
                if n_it > 1:
                    B0(1)
                B12(0)
                for k in range(n_it):
                    if k + 2 < n_it:
                        B0(k + 2)
                    if k + 1 < n_it:
                        B12(k + 1)
                    if k >= 1:
                        B4b(k - 1)
                    B3p(k, "dve2")
                    B3p(k, "dve")
                    B4a(k)
                B4b(n_it - 1)
                for c in range(NCH):
                    c0 = c * CH; s = c % 2
                    yab = pa_[s]
                    op("act", lambda e: e.activation(out=yab[:], in_=yacc[:, c0:c0 + CH], func=AF.Gelu_apprx_tanh), [yacc.b], [yab.b])
                    P.dma(mixs[ctile, :, c0:c0 + CH], yab[:], reads=[yab.b], writes=[mixB])
            P.barrier()

        with ExitStack() as es:
            wout = C.sb(es, [128, 8, D], BF16, nb=8); wglu = C.sb(es, [128, 4, 512], BF16, nb=4); bglu = C.sb(es, [128, 4])
            P.dma(bglu[:], b_glu[0].rearrange("(q p) -> p q", p=128), writes=[bglu.b])
            with ExitStack() as es2:
                stg = [C.sb(es2, [128, 1024]) for _ in range(3)]
                for k in range(8):
                    load_cast(es2, wout[:, k, :], wout.bs[k], w_out[0, k * 128:(k + 1) * 128, :], 1024, stg=stg)
                for k in range(4):
                    load_cast(es2, wglu[:, k, :], wglu.bs[k], w_glu[0, k * 128:(k + 1) * 128, :], 512, stg=stg)
                P.barrier()
            h0t = [C.sb(es, [128, 8, TT], nb=8) for _ in range(3)]; mx = [C.sb(es, [128, 8, TT], BF16) for _ in range(3)]
            ya2l = [C.sb(es, [128, 4, TT], BF16, nb=4) for _ in range(2)]; sg_ = [C.sb(es, [128, TT]) for _ in range(2)]
            pgl = [C.ps(es, [128, 512]) for _ in range(2)]; pwo = [C.ps(es, [128, 512]) for _ in range(2)]
            hs0v = hs0.rearrange("k p t -> p k t"); mixv = mixs.rearrange("k p t -> p k t")
            hs0T = [Buf("hs0t%d" % i) for i in range(NTILE)]

            def Q0(ti):
                t0 = ti * TT; h = h0t[ti % 3]; m_ = mx[ti % 3]
                P.dma(h[:], hs0v[:, :, t0:t0 + TT], reads=[hs0T[ti]], writes=list(h.bs))
                P.dma(m_[:], mixv[:, :, t0:t0 + TT], reads=[mixB], writes=[m_.b])

            def Q1(ti):
                m_ = mx[ti % 3]; ya2 = ya2l[ti % 2]
                for o in range(4):
                    pg_ = pgl[o % 2]; sgt = sg_[o % 2]

                    def mm(e):
                        for k in range(4):
                            ins = e.matmul(pg_[:, 0:TT], lhsT=wglu[:, k, o * 128:(o + 1) * 128], rhs=m_[:, k, :], start=(k == 0), stop=(k == 3))
                        return ins
                    op("pe", mm, list(wglu.bs) + [m_.b], [pg_.b])
                    op("act", lambda e: e.activation(out=sgt[:], in_=pg_[:, 0:TT], func=AF.Sigmoid, bias=bglu[:, o:o + 1]), [pg_.b, bglu.b], [sgt.b])
                    op("dve", lambda e: e.tensor_tensor(out=ya2[:, o, :], in0=m_[:, o, :], in1=sgt[:], op=ALU.mult), [m_.b, sgt.b], [ya2.bs[o]])

            def Q2(ti):
                t0 = ti * TT; h = h0t[ti % 3]; m_ = mx[ti % 3]; ya2 = ya2l[ti % 2]
                for o in range(8):
                    pw = pwo[o % 2]

                    def mm2(e):
                        for k in range(8):
                            rhs = ya2[:, k, :] if k < 4 else m_[:, k, :]
                            ins = e.matmul(pw[:, 0:TT], lhsT=wout[:, k, o * 128:(o + 1) * 128], rhs=rhs, start=(k == 0), stop=(k == 7))
                        return ins
                    op("pe", mm2, list(wout.bs) + list(ya2.bs) + [m_.b], [pw.b])
                    op("dve", lambda e: e.tensor_tensor(out=h[:, o, :], in0=pw[:, 0:TT], in1=h[:, o, :], op=ALU.add), [pw.b, h.bs[o]], [h.bs[o]])
                P.dma(hs0v[:, :, t0:t0 + TT], h[:], reads=list(h.bs), writes=[hs0T[ti]])

            pipeline(NTILE, [Q0, Q1, Q2])
            P.barrier()

        def ffn_phase(layer, src, srcB, dst, dstB, final):
            N = TT + 2
            with ExitStack() as es:
                W = ffn_setup(es, layer); A = ffn_alloc(es)
                hh = [C.sb(es, [128, 8, N], nb=8) for _ in range(2)]; sq = C.sb(es, [128, 8, N], BF16, nb=8); rstd = C.sb(es, [128, N]); hn = C.sb(es, [128, 8, N], BF16)
                pss = A["pd"][0]
                if final:
                    ytok = [C.sb(es, [128, D])] * 2
                srcv = src.rearrange("k p t -> p k t")
                blks = [(0, 128), (128, 128), (256, TT - 256)]

                def prologue(ti):
                    t0 = ti * TT
                    h = hh[ti % 2]
                    lo, hi, off = tile_cols(t0)
                    if off > 0:
                        op("dve", lambda e: e.memset(h[:, :, 0:1], 0.0), [], list(h.bs))
                    if hi - lo + off < N:
                        op("dve", lambda e: e.memset(h[:, :, N - 1:N], 0.0), [], list(h.bs))
                    P.dma(h[:, :, off:off + hi - lo], srcv[:, :, lo:hi], reads=[srcB], writes=list(h.bs))
                    rmsnorm(es, h, h.bs, hn, hn.b, N, pss, sq, rstd)
                    halo_fix(hn, hn.b, t0)

                def finalize(ti):
                    t0 = ti * TT
                    h = hh[ti % 2]
                    yn = h
                    for k in range(8):
                        if k % 2 == 0:
                            op("act", lambda e: e.activation(out=sq[:, k, 0:TT], in_=h[:, k, 1:TT + 1], func=AF.Square), [h.bs[k]], [sq.bs[k]])
                        else:
                            op("pool", lambda e: e.tensor_tensor(out=sq[:, k, 0:TT], in0=h[:, k, 1:TT + 1], in1=h[:, k, 1:TT + 1], op=ALU.mult), [h.bs[k]], [sq.bs[k]])

                    def mmn(e):
                        for k in range(8):
                            ins = e.matmul(pss[:, 0:TT], lhsT=onesb[:], rhs=sq[:, k, 0:TT], start=(k == 0), stop=(k == 7))
                        return ins
                    op("pe", mmn, list(sq.bs) + [onesb.b], [pss.b])
                    op("act", lambda e: e.activation(out=rstd[:, 0:TT], in_=pss[:, 0:TT], func=AF.Sqrt, scale=1.0 / D, bias=EPS), [pss.b], [rstd.b])
                    op("dve", lambda e: e.reciprocal(out=rstd[:, 0:TT], in_=rstd[:, 0:TT]), [rstd.b], [rstd.b])
                    for k in range(8):
                        op("dve", lambda e: e.scalar_tensor_tensor(out=yn[:, k, 1:TT + 1], in0=h[:, k, 1:TT + 1], scalar=gfin[:, k:k + 1], in1=rstd[:, 0:TT], op0=ALU.mult, op1=ALU.mult),
                           [h.bs[k], rstd.b, gfin.b], [h.bs[k]])
                    for bi, (o, nb_) in enumerate(blks):
                        yt = ytok[bi % 2]
                        for half in range(2):
                            pd = A["pd"][half]

                            def trf(e):
                                for kq in range(4):
                                    k = half * 4 + kq
                                    ins = e.transpose(out=pd[0:nb_, kq * 128:(kq + 1) * 128], in_=yn[:, k, 1 + o:1 + o + nb_], identity=ident[:])
                                return ins
                            op("pe", trf, [h.bs[half * 4 + kq] for kq in range(4)] + [ident.b], [pd.b])
                            if half == 0:
                                op("act", lambda e: e.copy(out=yt[0:nb_, 0:512], in_=pd[0:nb_, :]), [pd.b], [yt.b])
                            else:
                                op("dve", lambda e: e.tensor_copy(out=yt[0:nb_, 512:1024], in_=pd[0:nb_, :]), [pd.b], [yt.b])
                        P.dma(dst[t0 + o:t0 + o + nb_, :], yt[0:nb_, :], reads=[yt.b], writes=[dstB])

                prologue(0)
                for ti in range(NTILE):
                    t0 = ti * TT
                    h = hh[ti % 2]
                    ffn_body(W, A, hn, hn.b, h, h.bs, hook=(lambda ti=ti: prologue(ti + 1)) if ti + 1 < NTILE else None,
                             hook2=(lambda ti=ti: finalize(ti - 1)) if (final and ti >= 1) else None)
                    if not final:
                        P.dma(dst.rearrange("k p t -> p k t")[:, :, t0:t0 + TT], h[:, :, 1:TT + 1], reads=list(h.bs), writes=[dstB])
                if final:
                    finalize(NTILE - 1)
                P.barrier()

        ffn_phase(0, hs0, hs0B, hs2, hs2B, False)

        with ExitStack() as es:
            wq = C.sb(es, [128, 8, D], BF16, nb=8); wkd = C.sb(es, [128, 8, 4, 128], BF16, nb=8); wv = C.sb(es, [128, 8, 256], BF16, nb=8)
            wo = C.sb(es, [128, 8, D], BF16, nb=8)
            with ExitStack() as es2:
                stg = [C.sb(es2, [128, 1536]) for _ in range(3)]
                for k in range(8):
                    sgt = stg[k % 3]
                    P.dma(sgt[:], w_qkv[0, k * 128:(k + 1) * 128, :], writes=[sgt.b])
                    gk = gmix[:, 1, k:k + 1]
                    op("dve", lambda e, k=k, sgt=sgt, gk=gk: e.tensor_scalar(out=wq[:, k, :], in0=sgt[:, 0:1024], scalar1=gk, scalar2=None, op0=ALU.mult), [sgt.b, gmix.b], [wq.bs[k]])
                    for half in range(2):
                        op("pool", lambda e, k=k, sgt=sgt, gk=gk, half=half: e.tensor_scalar(out=wkd[:, k, :, half * 64:(half + 1) * 64], in0=sgt[:, 1024:1280].rearrange("p (h d) -> p h d", h=4),
                                                                                           scalar1=gk, scalar2=None, op0=ALU.mult), [sgt.b, gmix.b], [wkd.bs[k]])
                    op("act", lambda e, k=k, sgt=sgt, gk=gk: e.activation(out=wv[:, k, :], in_=sgt[:, 1280:1536], func=AF.Copy, scale=gk), [sgt.b, gmix.b], [wv.bs[k]])
                stg2 = [C.sb(es2, [128, 1024]) for _ in range(2)]
                for k in range(8):
                    load_cast(es2, wo[:, k, :], wo.bs[k], w_o[0, k * 128:(k + 1) * 128, :], 1024, stg=stg2)
                P.barrier()
            bias = C.sb(es, [128, 16, 384]); sink = C.sb(es, [128, 16])
            P.dma(sink[:], bc(sink_d[0:1, :], [128, 16]), writes=[sink.b])
            with ExitStack() as es2:
                di = C.sb(es2, [128, 384], I32); df = C.sb(es2, [128, 384]); mk = C.sb(es2, [128, 384])
                op("pool", lambda e: e.iota(di[:], pattern=[[-1, 384]], base=128, channel_multiplier=1), [], [di.b])
                op("dve", lambda e: e.tensor_copy(out=df[:], in_=di[:]), [di.b], [df.b])
                op("dve", lambda e: e.tensor_scalar(out=mk[:], in0=df[:], scalar1=-1.0, scalar2=None, op0=ALU.mult), [df.b], [mk.b])
                op("dve", lambda e: e.tensor_tensor(out=df[:], in0=df[:], in1=mk[:], op=ALU.max), [df.b, mk.b], [df.b])
                op("dve", lambda e: e.tensor_scalar(out=mk[:], in0=df[:], scalar1=128.0, scalar2=NEG, op0=ALU.is_gt, op1=ALU.mult), [df.b], [mk.b])
                for hh in range(16):
                    slope = 2.0 ** (-8.0 * (hh + 1) / 16.0)
                    op("dve", lambda e, hh=hh, slope=slope: e.scalar_tensor_tensor(out=bias[:, hh, :], in0=df[:], scalar=-slope, in1=mk[:], op0=ALU.mult, op1=ALU.add), [df.b, mk.b], [bias.b])
                P.barrier()
            NCK = 19
            KT2 = C.sb(es, [128, 4, NCK * 128], BF16, nb=NCK); V = C.sb(es, [128, NCK, 256], BF16, nb=NCK); QT = C.sb(es, [128, 8, 17 * 128], BF16, nb=17)
            h2c = [C.sb(es, [128, 8, 128], nb=8) for _ in range(2)]; sqc = [C.sb(es, [128, 8, 128], BF16, nb=8) for _ in range(2)]
            rsc = [C.sb(es, [128, 128]) for _ in range(2)]; hnc = [C.sb(es, [128, 8, 128], BF16) for _ in range(2)]
            Sb = [C.sb(es, [128, 385]) for _ in range(16)]; Pm = [C.sb(es, [128, 386], BF16) for _ in range(3)]; PT = [C.sb(es, [128, 384], BF16) for _ in range(3)]
            for hh in range(16):
                op("act", lambda e, hh=hh: e.copy(out=Sb[hh][:, 384:385], in_=sink[:, hh:hh + 1]), [sink.b], [Sb[hh].b])
            stat = [C.sb(es, [128, 4]) for _ in range(8)]
            otok = [C.sb(es, [128, D]) for _ in range(2)]; oT = [C.sb(es, [128, 8, 128], BF16) for _ in range(2)]
            h2b = [C.sb(es, [128, 8, 128], nb=8) for _ in range(2)]
            pmm = [C.ps(es, [128, 512]) for _ in range(2)]; pS = [C.ps(es, [128, 512]) for _ in range(2)]
            pPT = [C.ps(es, [128, 1024], BF16) for _ in range(2)]; pO = [C.ps(es, [128, 512]) for _ in range(2)]
            hs2v = hs2.rearrange("k p t -> p k t"); hs3v = hs3.rearrange("k p t -> p k t")
            nmm = [0]

            def grp(lhs_fn, rhs_fn, n, dst_fn, rd, wr, scale=None):
                pm = pmm[nmm[0] % 2]; nmm[0] += 1

                def mm(e):
                    for k in range(8):
                        ins = e.matmul(pm[:, 0:n], lhsT=lhs_fn(k), rhs=rhs_fn(k), start=(k == 0), stop=(k == 7))
                    return ins
                op("pe", mm, rd, [pm.b])
                if scale is not None:
                    op("act", lambda e: e.activation(out=dst_fn(), in_=pm[:, 0:n], func=AF.Copy, scale=scale), [pm.b], wr)
                elif nmm[0] % 2 == 0:
                    op("act", lambda e: e.copy(out=dst_fn(), in_=pm[:, 0:n]), [pm.b], wr)
                else:
                    op("dve", lambda e: e.tensor_copy(out=dst_fn(), in_=pm[:, 0:n]), [pm.b], wr)

            for sg in range(NSEG):
                s0 = sg * SEG

                def chunk_rng(i):
                    cs0 = s0 - 240 + 128 * i
                    return cs0, max(cs0, 0), min(cs0 + 128, T)

                def pa0(i):
                    cs0, lo, hi = chunk_rng(i)
                    if hi <= lo:
                        op("pool", lambda e: e.memset(KT2[:, :, i * 128:(i + 1) * 128], 0.0), [], [KT2.bs[i]])
                        op("pool", lambda e: e.memset(V[:, i, :], 0.0), [], [V.bs[i]])
                        return
                    hc = h2c[i % 2]
                    if hi - lo < 128:
                        op("dve", lambda e: e.memset(hc[:], 0.0), [], list(hc.bs))
                    P.dma(hc[:, :, lo - cs0:hi - cs0], hs2v[:, :, lo:hi], reads=[hs2B], writes=list(hc.bs))
                    rmsnorm(es, hc, hc.bs, hnc[i % 2], hnc[i % 2].b, 128, pmm[nmm[0] % 2], sqc[i % 2], rsc[i % 2]); nmm[0] += 1

                def pa1(i):
                    cs0, lo, hi = chunk_rng(i)
                    if hi <= lo:
                        return
                    hn_ = hnc[i % 2]
                    for hk in range(4):
                        grp(lambda k: wkd[:, k, hk, :], lambda k: hn_[:, k, :], 128, lambda: KT2[:, hk, i * 128:(i + 1) * 128], list(wkd.bs) + [hn_.b], [KT2.bs[i]])
                    grp(lambda k: hn_[:, k, :], lambda k: wv[:, k, :], 256, lambda: V[:, i, :], list(wv.bs) + [hn_.b], [V.bs[i]])
                    if 1 <= i <= 17:
                        for tq in range(8):
                            grp(lambda k: wq[:, k, tq * 128:(tq + 1) * 128], lambda k: hn_[:, k, :], 128, lambda: QT[:, tq, (i - 1) * 128:i * 128],
                                list(wq.bs) + [hn_.b], [QT.bs[i - 1]], scale=0.125)
                pipeline(NCK, [pa0, pa1])

                iters = [(i, hh) for i in range(1, 18) for hh in range(16)]

                def blk_info(i):
                    qs0 = s0 - 240 + 128 * i
                    ws = qs0 - 128
                    regions = []
                    if ws < s0:
                        regions.append((0, min(s0 - ws, 384), None if sg == 0 else nbcol))
                    if ws + 384 > s0 + SEG:
                        regions.append((s0 + SEG - ws, 384, None if sg == NSEG - 1 else nbcol))
                    if sg == NSEG - 1:
                        plo = max(PADLO - ws, 0); phi_ = min(T - ws, 384)
                        if plo < phi_:
                            regions.append((plo, phi_, npcol))
                    return qs0, regions

                def hd(n):
                    i, hh = iters[n]
                    hk = hh // 4; g = hh % 4
                    return i, hh, hk, hk * 2 + g // 2, 64 * (g % 2)

                def a0(n):
                    i, hh, hk, tq, base = hd(n)
                    ps_ = pS[n % 2]
                    if hh == 0:
                        qs0, _ = blk_info(i)
                        hb_ = h2b[i % 2]
                        qlo = max(qs0, 0)
                        if qlo > qs0:
                            op("dve", lambda e: e.memset(hb_[:], 0.0), [], list(hb_.bs))
                        P.dma(hb_[:, :, qlo - qs0:128], hs2v[:, :, qlo:qs0 + 128], reads=[hs2B], writes=list(hb_.bs))
                    op("pe", lambda e: e.matmul(ps_[:, 0:384], lhsT=QT[base:base + 64, tq, (i - 1) * 128:i * 128], rhs=KT2[base:base + 64, hk, (i - 1) * 128:(i + 2) * 128], start=True, stop=True),
                       [QT.bs[i - 1], KT2.bs[i - 1], KT2.bs[i], KT2.bs[i + 1]], [ps_.b])

                def a1(n):
                    i, hh, hk, tq, base = hd(n)
                    ps_ = pS[n % 2]; sb_ = Sb[hh]; st_ = stat[n % 8]
                    _, regions = blk_info(i)
                    op("dve", lambda e: e.tensor_tensor(out=sb_[:, 0:384], in0=ps_[:, 0:384], in1=bias[:, hh, :], op=ALU.add), [ps_.b, bias.b], [sb_.b])
                    for (rl, rh, sc) in regions:
                        if sc is None:
                            op("dve", lambda e: e.tensor_scalar(out=sb_[:, rl:rh], in0=sb_[:, rl:rh], scalar1=NEG, scalar2=None, op0=ALU.add), [sb_.b], [sb_.b])
                        else:
                            op("dve", lambda e: e.tensor_scalar(out=sb_[:, rl:rh], in0=sb_[:, rl:rh], scalar1=sc, scalar2=None, op0=ALU.add), [sb_.b, flg.b], [sb_.b])
                    op("dve", lambda e: e.reduce_max(out=st_[:, 1:2], in_=sb_[:, 0:385], axis=AX.X, negate=True), [sb_.b], [st_.b])

                def a2(n):
                    i, hh, hk, tq, base = hd(n)
                    sb_ = Sb[hh]; st_ = stat[n % 8]; pm_ = Pm[n % 3]
                    op("act", lambda e: e.activation(out=pm_[:, 0:385], in_=sb_[:, 0:385], func=AF.Exp, bias=st_[:, 1:2], accum_out=st_[:, 2:3]), [sb_.b, st_.b], [pm_.b, st_.b])

                def a3(n):
                    st_ = stat[n % 8]; pm_ = Pm[n % 3]; ppt = pPT[n % 2]
                    op("dve", lambda e: e.reciprocal(out=st_[:, 3:4], in_=st_[:, 2:3]), [st_.b], [st_.b])

                    def trp(e):
                        for c in range(3):
                            ins = e.transpose(out=ppt[:, c * 128:(c + 1) * 128], in_=pm_[:, c * 128:(c + 1) * 128], identity=identb[:])
                        return ins
                    op("pe", trp, [pm_.b, identb.b], [ppt.b])

                def a4(n):
                    ppt = pPT[n % 2]; pt_ = PT[n % 3]
                    op("act", lambda e: e.copy(out=pt_[:], in_=ppt[:, 0:384]), [ppt.b], [pt_.b])

                def a5(n):
                    i, hh, hk, tq, base = hd(n)
                    pt_ = PT[n % 3]; po = pO[n % 2]

                    def mo(e):
                        for c in range(3):
                            ins = e.matmul(po[:, 0:64], lhsT=pt_[:, c * 128:(c + 1) * 128], rhs=V[:, i - 1 + c, hk * 64:(hk + 1) * 64], start=(c == 0), stop=(c == 2))
                        return ins
                    op("pe", mo, [pt_.b, V.bs[i - 1], V.bs[i], V.bs[i + 1]], [po.b])

                def a6(n):
                    i, hh, hk, tq, base = hd(n)
                    po = pO[n % 2]; st_ = stat[n % 8]; ot = otok[i % 2]
                    op("act", lambda e: e.activation(out=ot[:, hh * 64:(hh + 1) * 64], in_=po[:, 0:64], func=AF.Copy, scale=st_[:, 3:4]), [po.b, st_.b], [ot.b])
                    if hh != 15:
                        return
                    qs0, _ = blk_info(i)
                    hb_ = h2b[i % 2]; oT_ = oT[i % 2]
                    for half in range(2):
                        pm = pmm[half]

                        def tro(e):
                            for kq in range(4):
                                k = half * 4 + kq
                                ins = e.transpose(out=pm[:, kq * 128:(kq + 1) * 128], in_=ot[:, k * 128:(k + 1) * 128], identity=ident[:])
                            return ins
                        op("pe", tro, [ot.b, ident.b], [pm.b])
                        if half == 0:
                            op("dve", lambda e: e.tensor_copy(out=oT_[:, 0:4, :].rearrange("p k t -> p (k t)"), in_=pm[:, :]), [pm.b], [oT_.b])
                        else:
                            op("act", lambda e: e.copy(out=oT_[:, 4:8, :].rearrange("p k t -> p (k t)"), in_=pm[:, :]), [pm.b], [oT_.b])
                    for o in range(8):
                        pm = pmm[nmm[0] % 2]; nmm[0] += 1

                        def mw(e):
                            for k in range(8):
                                ins = e.matmul(pm[:, 0:128], lhsT=wo[:, k, o * 128:(o + 1) * 128], rhs=oT_[:, k, :], start=(k == 0), stop=(k == 7))
                            return ins
                        op("pe", mw, list(wo.bs) + [oT_.b], [pm.b])
                        op("dve", lambda e: e.tensor_tensor(out=hb_[:, o, :], in0=pm[:, 0:128], in1=hb_[:, o, :], op=ALU.add), [pm.b, hb_.bs[o]], [hb_.bs[o]])
                    c_lo = 112 if i == 1 else 0
                    P.dma(hs3v[:, :, qs0 + c_lo:qs0 + 128], hb_[:, :, c_lo:128], reads=list(hb_.bs), writes=[hs3B])

                pipeline(len(iters), [a0, a1, a2, a3, a4, a5, a6])
            P.barrier()

        ffn_phase(1, hs3, hs3B, ys, ysB, True)
        P.barrier()
        block = top.enter_context(nc.Block())
        P.emit(block)
        print("plan: ops", P.nops, "waits", P.nwaits, "dmas", P.ndma, "sems", len(P.sems), flush=True)
    return nc


def core_inputs(inputs):
    meta = np.asarray(inputs["meta_tokens"], np.float32)
    xp = np.asarray(inputs["x_prompt"], np.float32)
    xsm = np.asarray(inputs["x_sample"], np.float32)
    wnames = ["norm_mix_g", "norm_ffn_g", "final_norm_g", "w_in_ab", "s5_lambda_re", "s5_lambda_im", "s5_log_dt", "s5_b_re", "s5_b_im",
              "s5_c_re", "s5_c_im", "s5_d", "w_glu", "b_glu", "lru_conv_w", "lru_conv_b", "lru_w_r", "lru_b_r", "lru_w_i", "lru_b_i",
              "lru_lambda", "w_out_ab", "w_qkv", "w_o", "attn_sink", "w_up", "ffn_conv_w", "ffn_conv_b", "w_down"]
    wts = {n: np.ascontiguousarray(np.asarray(inputs[n], np.float32)) for n in wnames}
    maps = []
    for c in range(8):
        X = np.zeros((T, D), np.float32)
        if c < 4:
            for k in range(4):
                X[k * SEG:k * SEG + 16] = meta
                X[k * SEG + 16:(k + 1) * SEG] = xp[4 * c + k]
            f = 0.0
        else:
            X[0:16] = meta
            X[16:16 + 8192] = xsm[c - 4]
            f = 1.0
        flags = np.tile(np.array([f, 1.0 - f, NEG * (1.0 - f), NEG * f], np.float32), (128, 1))
        m = {"xs": X, "flags": flags}
        m.update(wts)
        maps.append(m)
    return maps


_CACHE = {}


def kernel(**inputs):
    maps = core_inputs(inputs)
    if "nc" not in _CACHE:
        _CACHE["nc"] = build(debug=False)
    nc = _CACHE["nc"]
    res = run_bass_kernel_spmd(nc, maps, core_ids=list(range(8)))
    yp = np.empty((16, 2048, D), np.float32)
    ysm = np.empty((4, 8192, D), np.float32)
    for c in range(8):
        y = np.asarray(res.results[c]["ys"])
        if c < 4:
            for k in range(4):
                yp[4 * c + k] = y[k * SEG + 16:(k + 1) * SEG]
        else:
            ysm[c - 4] = y[16:16 + 8192]
    return (yp, ysm)
```

```python
import math
import os
import numpy as np
from contextlib import ExitStack
import concourse.bass as bass
import concourse.mybir as mybir
from concourse.bass_utils import run_bass_kernel_spmd

F32 = mybir.dt.float32
BF16 = mybir.dt.bfloat16
I32 = mybir.dt.int32
AF = mybir.ActivationFunctionType
ALU = mybir.AluOpType
AX = mybir.AxisListType

D = 1024
NSEG = 4
SEG = 2064
T = NSEG * SEG
TT = 344
NTILE = T // TT
CH = 1032
NCH = T // CH
DFF = 2816
NJ = DFF // 128
NEG = -1e30
EPS = 1e-6
PADLO = 8208
NDMASEM = 24
SEMCAP = 30000


class Buf:
    __slots__ = ("name", "w", "r")

    def __init__(self, name=""):
        self.name = name
        self.w = None
        self.r = []


class Rec:
    def __init__(self):
        self.calls = []

    def __getattr__(self, name):
        def f(*a, **k):
            self.calls.append((name, a, k))
            return self
        return f


class Plan:
    ENGS = ("pe", "dve", "act", "pool", "sp")

    def __init__(self, nc, es):
        self.nc = nc
        self.es = es
        self.items = {e: [] for e in self.ENGS}
        self.sems = {}
        self.epoch = {e: 0 for e in self.ENGS}
        self.count = {}
        for e in self.ENGS:
            self._newsem((e, 0))
        for i in range(NDMASEM):
            self._newsem(("dma", i))
        self.known = {e: {} for e in self.ENGS}
        self.ndma = 0
        self.nwaits = 0
        self.nops = 0

    def _newsem(self, key):
        self.sems[key] = self.es.enter_context(self.nc.semaphore("s%d" % len(self.sems)))
        self.count[key] = 0

    def _deps(self, eng, reads, writes):
        need = {}
        for b in reads:
            if b.w is not None:
                k, v = b.w
                if need.get(k, 0) < v:
                    need[k] = v
        for b in writes:
            if b.w is not None:
                k, v = b.w
                if need.get(k, 0) < v:
                    need[k] = v
            for k, v in b.r:
                if need.get(k, 0) < v:
                    need[k] = v
        waits = []
        kn = self.known[eng]
        for k, v in need.items():
            if kn.get(k, 0) >= v:
                continue
            kn[k] = v
            waits.append((k, v))
        self.nwaits += len(waits)
        return waits

    def _commit(self, sig, reads, writes):
        for b in reads:
            b.r.append(sig)
        for b in writes:
            b.w = sig
            b.r = []

    def op(self, eng, fn, reads=(), writes=()):
        waits = self._deps(eng, reads, writes)
        key = (eng, self.epoch[eng])
        if self.count[key] >= SEMCAP:
            self.epoch[eng] += 1
            key = (eng, self.epoch[eng])
            self._newsem(key)
        self.count[key] += 1
        val = self.count[key]
        rec = Rec()
        fn(rec)
        assert rec.calls
        self.items[eng].append((waits, rec.calls, (key, 1)))
        self._commit((key, val), reads, writes)
        self.nops += 1

    def dma(self, out, in_, reads=(), writes=(), eng="sp"):
        i = self.ndma % NDMASEM
        self.ndma += 1
        key = ("dma", i)
        waits = self._deps(eng, reads, writes)
        prev = self.count[key]
        if prev > 0 and self.known[eng].get(key, 0) < prev:
            self.known[eng][key] = prev
            waits.append((key, prev))
        self.count[key] += 16
        val = self.count[key]
        self.items[eng].append((waits, [("dma_start", (), {"out": out, "in_": in_})], (key, 16)))
        self._commit((key, val), reads, writes)

    def barrier(self):
        for eng in self.ENGS:
            waits = []
            for k, v in self.count.items():
                if v > 0 and self.known[eng].get(k, 0) < v:
                    self.known[eng][k] = v
                    waits.append((k, v))
            self.items[eng].append((waits, None, None))

    def emit(self, block):
        plan = self

        def run(engname):
            def f(e):
                for waits, fn, inc in plan.items[engname]:
                    for k, v in waits:
                        e.wait_ge(plan.sems[k], v)
                    if fn is None:
                        continue
                    for name, a, k in fn:
                        ins = getattr(e, name)(*a, **k)
                    ins.then_inc(plan.sems[inc[0]], inc[1])
            return f

        block.tensor(run("pe"))
        block.vector(run("dve"))
        block.scalar(run("act"))
        block.gpsimd(run("pool"))
        block.sync(run("sp"))


class Tl:
    def __init__(self, t, nb=1, name=""):
        self.t = t
        self.bs = [Buf(name + str(i)) for i in range(nb)]
        self.b = self.bs[0]

    def __getitem__(self, k):
        return self.t[k]


class Ctx:
    def __init__(self, nc, P):
        self.nc = nc
        self.P = P
        self.n = 0

    def sb(self, es, shape, dt=F32, nb=1):
        self.n += 1
        return Tl(es.enter_context(self.nc.sbuf_tensor("sb%d" % self.n, list(shape), dt)), nb, "sb%d_" % self.n)

    def ps(self, es, shape, dt=F32, nb=1):
        self.n += 1
        return Tl(es.enter_context(self.nc.psum_tensor("ps%d" % self.n, list(shape), dt)), nb, "ps%d_" % self.n)


def bc(ap, shape):
    return ap.to_broadcast(list(shape))


def pipeline(n, stage_fns):
    ns = len(stage_fns)
    for step in range(n + ns - 1):
        for si, f in enumerate(stage_fns):
            it = step - si
            if 0 <= it < n:
                f(it)


def build(debug=False):
    nc = bass.Bass("TRN2", target_bir_lowering=False)
    ein = lambda n, s, d=F32: nc.dram_tensor(n, list(s), d, kind="ExternalInput").ap()
    xs = ein("xs", [T, D])
    flags_d = ein("flags", [128, 4])
    g_mix = ein("norm_mix_g", [2, D]); g_ffn = ein("norm_ffn_g", [2, D]); g_fin = ein("final_norm_g", [D])
    w_in = ein("w_in_ab", [1, D, 1536])
    lam_re = ein("s5_lambda_re", [1, 2, 32, 64]); lam_im = ein("s5_lambda_im", [1, 2, 32, 64])
    log_dt = ein("s5_log_dt", [1, 2, 32])
    b_re = ein("s5_b_re", [1, 2, 32, 64, 16]); b_im = ein("s5_b_im", [1, 2, 32, 64, 16])
    c_re = ein("s5_c_re", [1, 2, 32, 16, 64]); c_im = ein("s5_c_im", [1, 2, 32, 16, 64])
    s5_d = ein("s5_d", [1, 512]); w_glu = ein("w_glu", [1, 512, 512]); b_glu = ein("b_glu", [1, 512])
    cwB = ein("lru_conv_w", [1, 4, 512]); cbB = ein("lru_conv_b", [1, 512])
    w_r = ein("lru_w_r", [1, 2, 8, 64, 64]); b_r = ein("lru_b_r", [1, 2, 512])
    w_i = ein("lru_w_i", [1, 2, 8, 64, 64]); b_i = ein("lru_b_i", [1, 2, 512])
    lru_lam = ein("lru_lambda", [1, 2, 512])
    w_out = ein("w_out_ab", [1, D, D]); w_qkv = ein("w_qkv", [1, D, 1536]); w_o = ein("w_o", [1, D, D])
    sink_d = ein("attn_sink", [1, 16])
    w_up = ein("w_up", [2, D, 2 * DFF]); cwF = ein("ffn_conv_w", [2, 3, 2 * DFF]); cbF = ein("ffn_conv_b", [2, 2 * DFF])
    w_down = ein("w_down", [2, DFF, D])
    ys = nc.dram_tensor("ys", [T, D], F32, kind="ExternalOutput").ap()
    skind = "ExternalOutput" if debug else "Internal"
    scr = lambda n, s, d=F32: nc.dram_tensor(n, list(s), d, kind=skind).ap()
    hs0 = scr("hs0", [8, 128, T]); zs = scr("zs", [12, 128, T]); mixs = scr("mixs", [8, 128, T], BF16)
    hs2 = scr("hs2", [8, 128, T]); hs3 = scr("hs3", [8, 128, T])
    hs0B = Buf("hs0"); zsB = Buf("zs"); mixB = Buf("mixs"); hs2B = Buf("hs2"); hs3B = Buf("hs3"); ysB = Buf("ys")

    with ExitStack() as top:
        P = Plan(nc, top)
        C = Ctx(nc, P)
        op = P.op
        nc_allow = top.enter_context(nc.allow_non_contiguous_dma(reason="small parameter layouts"))

        flg = C.sb(top, [128, 4])
        ident = C.sb(top, [128, 128]); identb = C.sb(top, [128, 128], BF16); onesb = C.sb(top, [128, 128], BF16)
        gmix = C.sb(top, [128, 2, 8]); gffn = C.sb(top, [128, 2, 8]); gfin = C.sb(top, [128, 8])
        P.dma(flg[:], flags_d[:, :], writes=[flg.b])
        P.dma(gmix[:], g_mix.rearrange("l (k p) -> p l k", p=128), writes=[gmix.b])
        P.dma(gffn[:], g_ffn.rearrange("l (k p) -> p l k", p=128), writes=[gffn.b])
        P.dma(gfin[:], g_fin.rearrange("(k p) -> p k", p=128), writes=[gfin.b])
        fcol = flg[:, 0:1]; nfcol = flg[:, 1:2]; nbcol = flg[:, 2:3]; npcol = flg[:, 3:4]
        with ExitStack() as es:
            it = C.sb(es, [128, 128], I32); ip = C.sb(es, [128, 1], I32); itf = C.sb(es, [128, 128]); ipf = C.sb(es, [128, 1])
            op("pool", lambda e: e.iota(it[:], pattern=[[1, 128]], base=0, channel_multiplier=0), writes=[it.b])
            op("pool", lambda e: e.iota(ip[:], pattern=[[0, 1]], base=0, channel_multiplier=1), writes=[ip.b])
            op("dve", lambda e: e.tensor_copy(out=itf[:], in_=it[:]), [it.b], [itf.b])
            op("dve", lambda e: e.tensor_copy(out=ipf[:], in_=ip[:]), [ip.b], [ipf.b])
            op("dve", lambda e: e.tensor_scalar(out=ident[:], in0=itf[:], scalar1=ipf[:, 0:1], scalar2=None, op0=ALU.is_equal),
               [itf.b, ipf.b], [ident.b])
            op("dve", lambda e: e.tensor_copy(out=identb[:], in_=ident[:]), [ident.b], [identb.b])
            op("dve", lambda e: e.memset(onesb[:], 1.0), [], [onesb.b])
            P.barrier()

        def load_cast(es_, dst, dstbuf, src_ap, ncols, scale_ap=None, scale_buf=None, eng_cycle=("dve", "act", "pool"), stg=None, k=[0]):
            s = stg[k[0] % len(stg)]
            eng = eng_cycle[k[0] % len(eng_cycle)]
            k[0] += 1
            P.dma(s[:, 0:ncols], src_ap, writes=[s.b])
            rd = [s.b] + ([scale_buf] if scale_buf is not None else [])
            if scale_ap is None:
                if eng == "act":
                    op("act", lambda e: e.copy(out=dst, in_=s[:, 0:ncols]), rd, [dstbuf])
                else:
                    op(eng, lambda e: e.tensor_copy(out=dst, in_=s[:, 0:ncols]), rd, [dstbuf])
            else:
                if eng == "act":
                    op("act", lambda e: e.activation(out=dst, in_=s[:, 0:ncols], func=AF.Copy, scale=scale_ap), rd, [dstbuf])
                else:
                    op(eng, lambda e: e.tensor_scalar(out=dst, in0=s[:, 0:ncols], scalar1=scale_ap, scalar2=None, op0=ALU.mult), rd, [dstbuf])

        def rmsnorm(es_, h, hbufs, hn, hnbuf, n, pss, sq, rstd, gscale=None):
            for k in range(8):
                eng = "act" if k % 2 == 0 else "pool"
                if eng == "act":
                    op("act", lambda e, k=k: e.activation(out=sq[:, k, 0:n], in_=h[:, k, 0:n], func=AF.Square), [hbufs[k]], [sq.bs[k]])
                else:
                    op("pool", lambda e, k=k: e.tensor_tensor(out=sq[:, k, 0:n], in0=h[:, k, 0:n], in1=h[:, k, 0:n], op=ALU.mult), [hbufs[k]], [sq.bs[k]])

            def mm(e):
                for k in range(8):
                    ins = e.matmul(pss[:, 0:n], lhsT=onesb[:], rhs=sq[:, k, 0:n], start=(k == 0), stop=(k == 7))
                return ins
            op("pe", mm, list(sq.bs) + [onesb.b], [pss.b])
            op("act", lambda e: e.activation(out=rstd[:, 0:n], in_=pss[:, 0:n], func=AF.Sqrt, scale=1.0 / D, bias=EPS), [pss.b], [rstd.b])
            op("dve", lambda e: e.reciprocal(out=rstd[:, 0:n], in_=rstd[:, 0:n]), [rstd.b], [rstd.b])
            if gscale is None:
                op("dve", lambda e: e.tensor_tensor(out=hn[:, :, 0:n], in0=h[:, :, 0:n], in1=bc(rstd[:, 0:n].unsqueeze(1), [128, 8, n]), op=ALU.mult),
                   list(hbufs) + [rstd.b], [hnbuf])
            else:
                for k in range(8):
                    op("dve", lambda e, k=k: e.scalar_tensor_tensor(out=hn[:, k, 0:n], in0=h[:, k, 0:n], scalar=gscale[:, k:k + 1], in1=rstd[:, 0:n],
                                                                  op0=ALU.mult, op1=ALU.mult), [hbufs[k], rstd.b], [hnbuf])

        def tile_cols(t0):
            lo = max(t0 - 1, 0); hi = min(t0 + TT + 1, T)
            return lo, hi, lo - (t0 - 1)

        def halo_fix(hn, hnbuf, t0):
            N = TT + 2
            if t0 == 0:
                op("dve", lambda e: e.memset(hn[:, :, 0:1], 0.0), [], [hnbuf])
            elif t0 % SEG == 0:
                op("dve", lambda e: e.tensor_scalar(out=hn[:, :, 0:1], in0=hn[:, :, 0:1], scalar1=fcol, scalar2=None, op0=ALU.mult), [hnbuf, flg.b], [hnbuf])
            if t0 + TT == T:
                op("dve", lambda e: e.memset(hn[:, :, N - 1:N], 0.0), [], [hnbuf])
                c0 = PADLO - (t0 - 1)
                op("dve", lambda e: e.tensor_scalar(out=hn[:, :, c0:N - 1], in0=hn[:, :, c0:N - 1], scalar1=nfcol, scalar2=None, op0=ALU.mult), [hnbuf, flg.b], [hnbuf])
            elif (t0 + TT) % SEG == 0:
                op("dve", lambda e: e.tensor_scalar(out=hn[:, :, N - 1:N], in0=hn[:, :, N - 1:N], scalar1=fcol, scalar2=None, op0=ALU.mult), [hnbuf, flg.b], [hnbuf])

        def ffn_setup(es, layer):
            W = {}
            W["up"] = C.sb(es, [128, 8, 2 * DFF], BF16, nb=8)
            W["dn"] = C.sb(es, [128, NJ, D], BF16, nb=NJ)
            W["cw"] = C.sb(es, [128, 3, 44]); W["cb"] = C.sb(es, [128, 44])
            P.dma(W["cw"][:], cwF[layer].rearrange("k (j p) -> p k j", p=128), writes=[W["cw"].b])
            P.dma(W["cb"][:], cbF[layer].rearrange("(j p) -> p j", p=128), writes=[W["cb"].b])
            with ExitStack() as es2:
                stg = [C.sb(es2, [128, 2816]) for _ in range(3)]
                for k in range(8):
                    for hlf in range(2):
                        load_cast(es2, W["up"][:, k, hlf * 2816:(hlf + 1) * 2816], W["up"].bs[k], w_up[layer, k * 128:(k + 1) * 128, hlf * 2816:(hlf + 1) * 2816], 2816,
                                  scale_ap=gffn[:, layer, k:k + 1], scale_buf=gffn.b, stg=stg)
                for j in range(NJ):
                    load_cast(es2, W["dn"][:, j, :], W["dn"].bs[j], w_down[layer, j * 128:(j + 1) * 128, :], 1024, stg=stg)
                P.barrier()
            return W

        def ffn_alloc(es):
            A = {}
            A["pa"] = [C.ps(es, [128, 512]) for _ in range(3)]
            A["pg"] = [C.ps(es, [128, 512]) for _ in range(3)]
            A["pd"] = [C.ps(es, [128, 512]) for _ in range(2)]
            A["ac"] = [C.sb(es, [128, TT]) for _ in range(4)]
            A["gc"] = [C.sb(es, [128, TT]) for _ in range(4)]
            A["t1"] = [C.sb(es, [128, TT]) for _ in range(3)]
            A["t2"] = [C.sb(es, [128, TT]) for _ in range(2)]
            A["m"] = C.sb(es, [128, NJ, TT], BF16, nb=NJ)
            return A

        def ffn_body(W, A, hn, hnbuf, hres, hresbufs, hook=None, hook2=None, hook3=None):
            N = TT + 2
            cw = W["cw"]; cb = W["cb"]

            def bufs(j):
                return (A["pa"][j % 3], A["pg"][j % 3], A["ac"][j % 4], A["gc"][j % 4], A["t1"][j % 3], A["t2"][j % 2])

            def s0(j):
                pa, pg, ac, gc, t1, t2 = bufs(j)

                def mm(e, col, pt):
                    for k in range(8):
                        ins = e.matmul(pt[:, 0:N], lhsT=W["up"][:, k, col * 128:(col + 1) * 128], rhs=hn[:, k, 0:N], start=(k == 0), stop=(k == 7))
                    return ins
                op("pe", lambda e: mm(e, j, pa), list(W["up"].bs) + [hnbuf], [pa.b])
                op("pe", lambda e: mm(e, NJ + j, pg), list(W["up"].bs) + [hnbuf], [pg.b])

            def s1(j):
                pa, pg, ac, gc, t1, t2 = bufs(j)
                for (pt, dst, col) in ((pg, gc, NJ + j), (pa, ac, j)):
                    op("act", lambda e: e.activation(out=dst[:], in_=pt[:, 1:TT + 1], func=AF.Identity, scale=cw[:, 1, col:col + 1], bias=cb[:, col:col + 1]), [pt.b, cw.b, cb.b], [dst.b])
                    op("dve", lambda e: e.scalar_tensor_tensor(out=dst[:], in0=pt[:, 0:TT], scalar=cw[:, 0, col:col + 1], in1=dst[:], op0=ALU.mult, op1=ALU.add), [pt.b, cw.b, dst.b], [dst.b])
                    op("dve", lambda e: e.scalar_tensor_tensor(out=dst[:], in0=pt[:, 2:TT + 2], scalar=cw[:, 2, col:col + 1], in1=dst[:], op0=ALU.mult, op1=ALU.add), [pt.b, cw.b, dst.b], [dst.b])

            def s2(j):
                pa, pg, ac, gc, t1, t2 = bufs(j)
                op("act", lambda e: e.activation(out=t1[:], in_=gc[:], func=AF.Gelu_apprx_tanh), [gc.b], [t1.b])

            def s3(j):
                pa, pg, ac, gc, t1, t2 = bufs(j)
                op("pool", lambda e: e.tensor_tensor(out=A["m"][:, j, :], in0=t1[:], in1=ac[:], op=ALU.mult), [t1.b, ac.b], [A["m"].bs[j]])

            stg_ = [s0, s1, s2, s3]
            for step in range(NJ + len(stg_) - 1):
                for si, f in enumerate(stg_):
                    it = step - si
                    if 0 <= it < NJ:
                        f(it)
                        if si == 0 and it == NJ - 1 and hook is not None:
                            hook()
                if step == 4 and hook2 is not None:
                    hook2()
                if step == 10 and hook3 is not None:
                    hook3()
            JH = NJ - 3
            banks4 = [A["pd"][0], A["pd"][1], A["pa"][1], A["pg"][1]]

            def mm_head(e):
                for j in range(JH):
                    for g in range(4):
                        ins = e.matmul(banks4[g][:, 0:TT], lhsT=W["dn"][:, j, g * 128:(g + 1) * 128], rhs=A["m"][:, j, :], start=(j == 0), stop=False)
                return ins
            op("pe", mm_head, list(W["dn"].bs) + list(A["m"].bs[0:JH]), [bk.b for bk in banks4])

            def mm_tail(e):
                for g in range(4):
                    for j in range(JH, NJ):
                        ins = e.matmul(banks4[g][:, 0:TT], lhsT=W["dn"][:, j, g * 128:(g + 1) * 128], rhs=A["m"][:, j, :], start=False, stop=(j == NJ - 1))
                return ins
            op("pe", mm_tail, list(W["dn"].bs) + list(A["m"].bs[JH:NJ]), [bk.b for bk in banks4])
            for g in range(4):
                op("dve", lambda e, g=g: e.tensor_tensor(out=hres[:, g, 1:TT + 1], in0=banks4[g][:, 0:TT], in1=hres[:, g, 1:TT + 1], op=ALU.add),
                   [banks4[g].b, hresbufs[g]], [hresbufs[g]])
            for o in range(4, 8):
                pd = A["pd"][o % 2]

                def mmd(e, o=o, pd=pd):
                    for j in range(NJ):
                        ins = e.matmul(pd[:, 0:TT], lhsT=W["dn"][:, j, o * 128:(o + 1) * 128], rhs=A["m"][:, j, :], start=(j == 0), stop=(j == NJ - 1))
                    return ins
                op("pe", mmd, list(W["dn"].bs) + list(A["m"].bs), [pd.b])
                op("dve", lambda e, o=o, pd=pd: e.tensor_tensor(out=hres[:, o, 1:TT + 1], in0=pd[:, 0:TT], in1=hres[:, o, 1:TT + 1], op=ALU.add),
                   [pd.b, hresbufs[o]], [hresbufs[o]])

        with ExitStack() as es:
            win = C.sb(es, [128, 8, 1536], BF16, nb=8)
            with ExitStack() as es2:
                stg = [C.sb(es2, [128, 1536]) for _ in range(3)]
                for k in range(8):
                    load_cast(es2, win[:, k, :], win.bs[k], w_in[0, k * 128:(k + 1) * 128, :], 1536, scale_ap=gmix[:, 0, k:k + 1], scale_buf=gmix.b, stg=stg)
                P.barrier()
            xtok = [C.sb(es, [128, 3, D]) for _ in range(2)]
            h0 = [C.sb(es, [128, 8, TT], nb=8) for _ in range(2)]
            sq = C.sb(es, [128, 8, TT], BF16, nb=8); rstd = C.sb(es, [128, TT]); hnl = [C.sb(es, [128, 8, TT], BF16) for _ in range(2)]
            zt = [C.sb(es, [128, 12, TT], nb=12) for _ in range(2)]
            ptr = [C.ps(es, [128, 512]) for _ in range(2)]; pss = C.ps(es, [128, 512]); pz = [C.ps(es, [128, 512]) for _ in range(3)]
            blks = [(0, 128), (128, 128), (256, TT - 256)]

            def xload(ti):
                t0 = ti * TT; xt = xtok[ti % 2]
                for bi, (o, nb_) in enumerate(blks):
                    P.dma(xt[0:nb_, bi, :], xs[t0 + o:t0 + o + nb_, :], writes=[xt.b])

            def R0(ti):
                t0 = ti * TT; s = ti % 2
                xt = xtok[s]; h = h0[s]
                if ti == 0:
                    xload(0)
                if ti + 1 < NTILE:
                    xload(ti + 1)
                for k in range(8):
                    pt = ptr[k % 2]

                    def tr(e):
                        for bi, (o, nb_) in enumerate(blks):
                            ins = e.transpose(out=pt[:, o:o + nb_], in_=xt[0:nb_, bi, k * 128:(k + 1) * 128], identity=ident[0:nb_, 0:nb_])
                        return ins
                    op("pe", tr, [xt.b, ident.b], [pt.b])
                    if k % 2 == 0:
                        op("act", lambda e: e.copy(out=h[:, k, :], in_=pt[:, 0:TT]), [pt.b], [h.bs[k]])
                    else:
                        op("dve", lambda e: e.tensor_copy(out=h[:, k, :], in_=pt[:, 0:TT]), [pt.b], [h.bs[k]])
                P.dma(hs0.rearrange("k p t -> p k t")[:, :, t0:t0 + TT], h[:], reads=list(h.bs), writes=[hs0B])

            def R1(ti):
                h = h0[ti % 2]; hn = hnl[ti % 2]
                rmsnorm(es, h, h.bs, hn, hn.b, TT, pss, sq, rstd)

            def R2(ti):
                t0 = ti * TT; z = zt[ti % 2]; hn = hnl[ti % 2]
                for o in range(12):
                    pzz = pz[o % 3]

                    def mm(e):
                        for k in range(8):
                            ins = e.matmul(pzz[:, 0:TT], lhsT=win[:, k, o * 128:(o + 1) * 128], rhs=hn[:, k, :], start=(k == 0), stop=(k == 7))
                        return ins
                    op("pe", mm, list(win.bs) + [hn.b], [pzz.b])
                    if o % 2 == 0:
                        op("act", lambda e: e.copy(out=z[:, o, :], in_=pzz[:, 0:TT]), [pzz.b], [z.bs[o]])
                    else:
                        op("dve", lambda e: e.tensor_copy(out=z[:, o, :], in_=pzz[:, 0:TT]), [pzz.b], [z.bs[o]])
                P.dma(zs.rearrange("k p t -> p k t")[:, :, t0:t0 + TT], z[:], reads=list(z.bs), writes=[zsB])

            pipeline(NTILE, [R0, R1, R2])
            P.barrier()

        with ExitStack() as es:
            cw = C.sb(es, [128, 4, 4]); cbt = C.sb(es, [128, 4]); ncw = C.sb(es, [128, 4, 4])
            br = C.sb(es, [128, 2, 4]); bi_ = C.sb(es, [128, 2, 4]); cp = C.sb(es, [128, 2, 4])
            P.dma(cw[:], cwB[0].rearrange("k (q p) -> p k q", p=128), writes=[cw.b])
            P.dma(cbt[:], cbB[0].rearrange("(q p) -> p q", p=128), writes=[cbt.b])
            P.dma(br[:], b_r[0].rearrange("d (q p) -> p d q", p=128), writes=[br.b])
            P.dma(bi_[:], b_i[0].rearrange("d (q p) -> p d q", p=128), writes=[bi_.b])
            P.dma(cp[:], lru_lam[0].rearrange("d (q p) -> p d q", p=128), writes=[cp.b])
            op("act", lambda e: e.activation(out=cp[:], in_=cp[:], func=AF.Exp, scale=-1.0), [cp.b], [cp.b])
            op("act", lambda e: e.activation(out=cp[:], in_=cp[:], func=AF.Ln, bias=1.0), [cp.b], [cp.b])
            op("dve", lambda e: e.tensor_scalar(out=cp[:], in0=cp[:], scalar1=-8.0, scalar2=None, op0=ALU.mult), [cp.b], [cp.b])
            op("dve", lambda e: e.tensor_scalar(out=ncw[:], in0=cw[:], scalar1=nfcol, scalar2=-1.0, op0=ALU.mult, op1=ALU.mult), [cw.b, flg.b], [ncw.b])
            wg = C.sb(es, [128, 2, 2, 4, 128], BF16)
            with ExitStack() as es2:
                wst = C.sb(es2, [128, 2, 2, 4, 128])
                op("dve", lambda e: e.memset(wst[:], 0.0), [], [wst.b])
                for gi, wsrc in enumerate((w_r, w_i)):
                    for d in range(2):
                        for q in range(4):
                            for hh in range(2):
                                P.dma(wst[64 * hh:64 * hh + 64, gi, d, q, 64 * hh:64 * hh + 64], wsrc[0, d, 2 * q + hh, :, :], writes=[wst.b])
                op("dve", lambda e: e.tensor_copy(out=wg[:], in_=wst[:]), [wst.b], [wg.b])
                P.barrier()
            xb = C.sb(es, [128, T]); gb = C.sb(es, [128, T]); xc = C.sb(es, [128, T]); xcb = C.sb(es, [128, T], BF16); hsum = C.sb(es, [128, T])
            rr = [C.sb(es, [128, CH]) for _ in range(2)]; ii = [C.sb(es, [128, CH]) for _ in range(3)]
            aa = [C.sb(es, [128, CH]) for _ in range(2)]; ss_ = [C.sb(es, [128, CH]) for _ in range(2)]
            hb = [C.sb(es, [128, CH]) for _ in range(2)]
            carry = C.sb(es, [128, 1]); ybc = [C.sb(es, [128, CH], BF16) for _ in range(2)]
            pgr = [C.ps(es, [128, 3, 512]) for _ in range(2)]
            for q in range(4):
                if q == 0:
                    P.dma(xb[:], zs[4 + q, :, :], reads=[zsB], writes=[xb.b])
                P.dma(gb[:], zs[8 + q, :, :], reads=[zsB], writes=[gb.b])
                op("act", lambda e, q=q: e.activation(out=xc[:], in_=xb[:], func=AF.Identity, scale=cw[:, 2, q:q + 1], bias=cbt[:, q:q + 1]), [xb.b, cw.b, cbt.b], [xc.b])
                op("dve", lambda e, q=q: e.scalar_tensor_tensor(out=xc[:, 2:T], in0=xb[:, 0:T - 2], scalar=cw[:, 0, q:q + 1], in1=xc[:, 2:T], op0=ALU.mult, op1=ALU.add), [xb.b, xc.b, cw.b], [xc.b])
                op("dve", lambda e, q=q: e.scalar_tensor_tensor(out=xc[:, 1:T], in0=xb[:, 0:T - 1], scalar=cw[:, 1, q:q + 1], in1=xc[:, 1:T], op0=ALU.mult, op1=ALU.add), [xb.b, xc.b, cw.b], [xc.b])
                op("dve", lambda e, q=q: e.scalar_tensor_tensor(out=xc[:, 0:T - 1], in0=xb[:, 1:T], scalar=cw[:, 3, q:q + 1], in1=xc[:, 0:T - 1], op0=ALU.mult, op1=ALU.add), [xb.b, xc.b, cw.b], [xc.b])
                for sgi in range(1, NSEG):
                    B_ = sgi * SEG
                    for (to, fo, kk) in ((B_ - 1, B_, 3), (B_, B_ - 1, 1), (B_, B_ - 2, 0), (B_ + 1, B_ - 1, 0)):
                        op("dve", lambda e, to=to, fo=fo, kk=kk, q=q: e.scalar_tensor_tensor(out=xc[:, to:to + 1], in0=xb[:, fo:fo + 1], scalar=ncw[:, kk, q:q + 1], in1=xc[:, to:to + 1],
                                                                                       op0=ALU.mult, op1=ALU.add), [xb.b, xc.b, ncw.b], [xc.b])
                op("pool", lambda e: e.tensor_copy(out=xcb[:], in_=xc[:]), [xc.b], [xcb.b])
                if q + 1 < 4:
                    P.dma(xb[:], zs[4 + q + 1, :, :], reads=[zsB], writes=[xb.b])
                for d in range(2):
                    order = list(range(NCH)) if d == 0 else list(range(NCH - 1, -1, -1))

                    def L0(ci, d=d, q=q, order=order):
                        c = order[ci]; c0 = c * CH
                        pg = pgr[ci % 2]
                        for gi, dst, bias_ in ((0, rr[ci % 2], br), (1, ii[ci % 3], bi_)):
                            def mm(e):
                                for u in range(3):
                                    ins = e.matmul(pg[:, u, 0:TT], lhsT=wg[:, gi, d, q, :], rhs=xcb[:, c0 + u * TT:c0 + (u + 1) * TT], start=True, stop=True)
                                return ins
                            op("pe", mm, [wg.b, xcb.b], [pg.b])
                            op("act", lambda e: e.activation(out=dst[:].rearrange("p (u t) -> p u t", u=3), in_=pg[:, :, 0:TT], func=AF.Sigmoid, bias=bias_[:, d, q:q + 1]), [pg.b, bias_.b], [dst.b])

                    def L1(ci, d=d, q=q, order=order):
                        c = order[ci]; c0 = c * CH
                        r_ = rr[ci % 2]; i_ = ii[ci % 3]; a_ = aa[ci % 2]; s_ = ss_[ci % 2]
                        op("act", lambda e: e.activation(out=a_[:], in_=r_[:], func=AF.Exp, scale=cp[:, d, q:q + 1]), [r_.b, cp.b], [a_.b])
                        op("pool", lambda e: e.tensor_tensor(out=s_[:], in0=a_[:], in1=a_[:], op=ALU.mult), [a_.b], [s_.b])
                        op("act", lambda e: e.activation(out=s_[:], in_=s_[:], func=AF.Sqrt, scale=-1.0, bias=1.0), [s_.b], [s_.b])
                        op("pool", lambda e: e.tensor_tensor(out=i_[:], in0=i_[:], in1=xc[:, c0:c0 + CH], op=ALU.mult), [i_.b, xc.b], [i_.b])
                        op("dve", lambda e: e.tensor_tensor(out=i_[:], in0=i_[:], in1=s_[:], op=ALU.mult), [i_.b, s_.b], [i_.b])
                        if c == NCH - 1:
                            pc = PADLO - c0
                            op("dve", lambda e: e.tensor_scalar(out=i_[:, pc:CH], in0=i_[:, pc:CH], scalar1=nfcol, scalar2=None, op0=ALU.mult), [i_.b, flg.b], [i_.b])

                    def L2(ci, d=d, q=q, order=order):
                        c = order[ci]; c0 = c * CH
                        i_ = ii[ci % 3]; a_ = aa[ci % 2]
                        if ci == 0:
                            init = 0.0; rd = []
                        else:
                            init = carry[:, 0:1]; rd = [carry.b]
                        if d == 0:
                            op("dve", lambda e: e.tensor_tensor_scan(out=hsum[:, c0:c0 + CH], data0=a_[:], data1=i_[:], initial=init, op0=ALU.mult, op1=ALU.add), [a_.b, i_.b] + rd, [hsum.b])
                            last = hsum[:, c0 + CH - 1:c0 + CH]; lastb = hsum.b
                        else:
                            hbt = hb[ci % 2]
                            op("dve", lambda e: e.tensor_tensor_scan(out=hbt[:, ::-1], data0=a_[:, ::-1], data1=i_[:, ::-1], initial=init, op0=ALU.mult, op1=ALU.add), [a_.b, i_.b] + rd, [hbt.b])
                            last = hbt[:, 0:1]; lastb = hbt.b
                        crossing = (c % 2 == 1) if d == 0 else (c % 2 == 0)
                        if ci < NCH - 1:
                            if crossing:
                                op("act", lambda e: e.activation(out=carry[:], in_=last, func=AF.Copy, scale=fcol), [lastb, flg.b], [carry.b])
                            else:
                                op("act", lambda e: e.copy(out=carry[:], in_=last), [lastb], [carry.b])
                        if d == 1:
                            op("pool", lambda e: e.tensor_tensor(out=hsum[:, c0:c0 + CH], in0=hsum[:, c0:c0 + CH], in1=hbt[:], op=ALU.add), [hbt.b, hsum.b], [hsum.b])

                    pipeline(NCH, [L0, L1, L2])
                for c in range(NCH):
                    c0 = c * CH; s = c % 2
                    t1 = rr[s]; yb = ybc[s]
                    op("act", lambda e: e.activation(out=t1[:], in_=gb[:, c0:c0 + CH], func=AF.Gelu_apprx_tanh), [gb.b], [t1.b])
                    op("dve", lambda e: e.tensor_tensor(out=yb[:], in0=t1[:], in1=hsum[:, c0:c0 + CH], op=ALU.mult), [t1.b, hsum.b], [yb.b])
                    P.dma(mixs[4 + q, :, c0:c0 + CH], yb[:], reads=[yb.b], writes=[mixB])
            P.barrier()

        with ExitStack() as es:
            rdec = C.sb(es, [128, 32]); phi = C.sb(es, [128, 32], I32)
            WB = C.sb(es, [128, 32, 2, 128], BF16)
            WC = C.sb(es, [128, 32, 3, 128], BF16)
            dsk = C.sb(es, [128, 4])
            P.dma(dsk[:], s5_d[0].rearrange("(q p) -> p q", p=128), writes=[dsk.b])
            with ExitStack() as es2:
                lre = C.sb(es2, [128, 32]); lim = C.sb(es2, [128, 32]); ldt = C.sb(es2, [128, 32])
                Bre = C.sb(es2, [128, 32, 16]); Bim = C.sb(es2, [128, 32, 16])
                for gl in range(2):
                    sl = slice(64 * gl, 64 * gl + 64)
                    P.dma(lre[sl, :].rearrange("p (d j) -> p d j", d=2), lam_re[0].rearrange("d (j g) n -> g n d j", g=2)[gl], writes=[lre.b])
                    P.dma(lim[sl, :].rearrange("p (d j) -> p d j", d=2), lam_im[0].rearrange("d (j g) n -> g n d j", g=2)[gl], writes=[lim.b])
                    P.dma(ldt[sl, :].rearrange("p (d j) -> p d j", d=2), bc(log_dt[0].rearrange("d (j g) -> g d j", g=2)[gl:gl + 1], [64, 2, 16]), writes=[ldt.b])
                    for d in range(2):
                        P.dma(Bre[sl, d * 16:(d + 1) * 16, :], b_re[0, d].rearrange("(j g) n c -> g n j c", g=2)[gl], writes=[Bre.b])
                        P.dma(Bim[sl, d * 16:(d + 1) * 16, :], b_im[0, d].rearrange("(j g) n c -> g n j c", g=2)[gl], writes=[Bim.b])
                dtt = C.sb(es2, [128, 32]); xr = C.sb(es2, [128, 32]); xi = C.sb(es2, [128, 32]); er = C.sb(es2, [128, 32])
                ki = C.sb(es2, [128, 32], I32); kf = C.sb(es2, [128, 32]); fr = C.sb(es2, [128, 32]); pi_ = C.sb(es2, [128, 32], I32); pc_ = C.sb(es2, [128, 32], I32)
                cs = C.sb(es2, [128, 32]); sn = C.sb(es2, [128, 32]); q30 = C.sb(es2, [128, 32], I32)
                nr = C.sb(es2, [128, 32]); ni = C.sb(es2, [128, 32]); den = C.sb(es2, [128, 32]); cr = C.sb(es2, [128, 32]); ci_ = C.sb(es2, [128, 32]); tmp = C.sb(es2, [128, 32]); tmp2 = C.sb(es2, [128, 32])
                A1 = lambda eng, fn, r, w: op(eng, fn, [x.b for x in r], [x.b for x in w])
                A1("dve", lambda e: e.tensor_scalar(out=lre[:], in0=lre[:], scalar1=-1e-4, scalar2=None, op0=ALU.min), [lre], [lre])
                A1("act", lambda e: e.activation(out=dtt[:], in_=ldt[:], func=AF.Exp), [ldt], [dtt])
                A1("dve", lambda e: e.tensor_tensor(out=xr[:], in0=lre[:], in1=dtt[:], op=ALU.mult), [lre, dtt], [xr])
                A1("dve", lambda e: e.tensor_tensor(out=xi[:], in0=lim[:], in1=dtt[:], op=ALU.mult), [lim, dtt], [xi])
                A1("act", lambda e: e.activation(out=rdec[:], in_=xr[:], func=AF.Exp), [xr], [rdec])
                A1("dve", lambda e: e.tensor_scalar(out=fr[:], in0=xi[:], scalar1=float(1.0 / (2 * math.pi)), scalar2=None, op0=ALU.mult), [xi], [fr])
                A1("dve", lambda e: e.tensor_copy(out=ki[:], in_=fr[:]), [fr], [ki])
                A1("dve", lambda e: e.tensor_copy(out=kf[:], in_=ki[:]), [ki], [kf])
                A1("dve", lambda e: e.tensor_tensor(out=fr[:], in0=fr[:], in1=kf[:], op=ALU.subtract), [fr, kf], [fr])
                A1("dve", lambda e: e.tensor_scalar(out=fr[:], in0=fr[:], scalar1=float(2 ** 31), scalar2=None, op0=ALU.mult), [fr], [fr])
                A1("dve", lambda e: e.tensor_copy(out=pi_[:], in_=fr[:]), [fr], [pi_])
                A1("pool", lambda e: e.tensor_tensor(out=pi_[:], in0=pi_[:], in1=pi_[:], op=ALU.add), [pi_], [pi_])
                A1("pool", lambda e: e.iota(q30[:], pattern=[[0, 32]], base=2 ** 30, channel_multiplier=0), [], [q30])
                A1("pool", lambda e: e.tensor_tensor(out=pc_[:], in0=pi_[:], in1=q30[:], op=ALU.add), [pi_, q30], [pc_])
                A1("act", lambda e: e.activation(out=sn[:], in_=pi_[:], func=AF.Sin, scale=float(2 * math.pi / 2 ** 32)), [pi_], [sn])
                A1("act", lambda e: e.activation(out=cs[:], in_=pc_[:], func=AF.Sin, scale=float(2 * math.pi / 2 ** 32)), [pc_], [cs])
                A1("dve", lambda e: e.tensor_tensor(out=nr[:], in0=rdec[:], in1=cs[:], op=ALU.mult), [rdec, cs], [nr])
                A1("dve", lambda e: e.tensor_scalar(out=nr[:], in0=nr[:], scalar1=-1.0, scalar2=None, op0=ALU.add), [nr], [nr])
                A1("dve", lambda e: e.tensor_tensor(out=ni[:], in0=rdec[:], in1=sn[:], op=ALU.mult), [rdec, sn], [ni])
                A1("dve", lambda e: e.tensor_tensor(out=den[:], in0=lre[:], in1=lre[:], op=ALU.mult), [lre], [den])
                A1("dve", lambda e: e.tensor_tensor(out=tmp[:], in0=lim[:], in1=lim[:], op=ALU.mult), [lim], [tmp])
                A1("dve", lambda e: e.tensor_tensor(out=den[:], in0=den[:], in1=tmp[:], op=ALU.add), [den, tmp], [den])
                A1("dve", lambda e: e.reciprocal(out=den[:], in_=den[:]), [den], [den])
                A1("dve", lambda e: e.tensor_tensor(out=cr[:], in0=nr[:], in1=lre[:], op=ALU.mult), [nr, lre], [cr])
                A1("dve", lambda e: e.tensor_tensor(out=tmp[:], in0=ni[:], in1=lim[:], op=ALU.mult), [ni, lim], [tmp])
                A1("dve", lambda e: e.tensor_tensor(out=cr[:], in0=cr[:], in1=tmp[:], op=ALU.add), [cr, tmp], [cr])
                A1("dve", lambda e: e.tensor_tensor(out=cr[:], in0=cr[:], in1=den[:], op=ALU.mult), [cr, den], [cr])
                A1("dve", lambda e: e.tensor_tensor(out=ci_[:], in0=ni[:], in1=lre[:], op=ALU.mult), [ni, lre], [ci_])
                A1("dve", lambda e: e.tensor_tensor(out=tmp2[:], in0=nr[:], in1=lim[:], op=ALU.mult), [nr, lim], [tmp2])
                A1("dve", lambda e: e.tensor_tensor(out=ci_[:], in0=ci_[:], in1=tmp2[:], op=ALU.subtract), [ci_, tmp2], [ci_])
                A1("dve", lambda e: e.tensor_tensor(out=ci_[:], in0=ci_[:], in1=den[:], op=ALU.mult), [ci_, den], [ci_])
                A1("pool", lambda e: e.tensor_copy(out=phi[:], in_=pi_[:]), [pi_], [phi])
                XR = C.sb(es2, [128, 32, 128]); XI = C.sb(es2, [128, 32, 128]); Tm = C.sb(es2, [128, 32, 16]); Tm2 = C.sb(es2, [128, 32, 16])
                A1("pool", lambda e: e.memset(XR[:], 0.0), [], [XR])
                A1("pool", lambda e: e.memset(XI[:], 0.0), [], [XI])
                crb = lambda: bc(cr[:].unsqueeze(2), [128, 32, 16]); cib = lambda: bc(ci_[:].unsqueeze(2), [128, 32, 16])
                A1("dve", lambda e: e.tensor_tensor(out=Tm[:], in0=Bre[:], in1=crb(), op=ALU.mult), [Bre, cr], [Tm])
                A1("dve", lambda e: e.tensor_tensor(out=Tm2[:], in0=Bim[:], in1=cib(), op=ALU.mult), [Bim, ci_], [Tm2])
                A1("dve", lambda e: e.tensor_tensor(out=Tm[:], in0=Tm[:], in1=Tm2[:], op=ALU.subtract), [Tm, Tm2], [Tm])
                A1("dve", lambda e: e.tensor_tensor(out=Tm2[:], in0=Bre[:], in1=cib(), op=ALU.mult), [Bre, ci_], [Tm2])
                A1("dve", lambda e: e.tensor_tensor(out=Bre[:], in0=Bim[:], in1=crb(), op=ALU.mult), [Bim, cr, Bre], [Bre])
                A1("dve", lambda e: e.tensor_tensor(out=Tm2[:], in0=Tm2[:], in1=Bre[:], op=ALU.add), [Tm2, Bre], [Tm2])
                for st in range(32):
                    j = st % 16
                    for gl in range(2):
                        col = ((2 * j + gl) % 8) * 16
                        sl = slice(64 * gl, 64 * gl + 64)
                        A1("dve", lambda e, st=st, sl=sl, col=col: e.tensor_copy(out=XR[sl, st, col:col + 16], in_=Tm[sl, st, :]), [Tm], [XR])
                        A1("pool", lambda e, st=st, sl=sl, col=col: e.tensor_copy(out=XI[sl, st, col:col + 16], in_=Tm2[sl, st, :]), [Tm2], [XI])
                ptb = [C.ps(es2, [128, 512]) for _ in range(2)]
                for st in range(32):
                    for ri, X in enumerate((XR, XI)):
                        pt = ptb[(2 * st + ri) % 2]
                        op("pe", lambda e, st=st, X=X, pt=pt: e.transpose(out=pt[:, 0:128], in_=X[:, st, :], identity=ident[:]), [X.b, ident.b], [pt.b])
                        op("act", lambda e, st=st, ri=ri, pt=pt: e.copy(out=WB[:, st, ri, :], in_=pt[:, 0:128]), [pt.b], [WB.b])
                A1("pool", lambda e: e.memset(XR[:], 0.0), [WB], [XR])
                A1("pool", lambda e: e.memset(XI[:], 0.0), [WB], [XI])
                for st in range(32):
                    d = st // 16; j = st % 16
                    for gl in range(2):
                        g = 2 * j + gl; col = (g % 8) * 16
                        sl = slice(64 * gl, 64 * gl + 64)
                        P.dma(XR[sl, st, col:col + 16], c_re[0, d, g].rearrange("c n -> n c"), writes=[XR.b])
                        P.dma(XI[sl, st, col:col + 16], c_im[0, d, g].rearrange("c n -> n c"), writes=[XI.b])
                A1("dve", lambda e: e.tensor_copy(out=WC[:, :, 0, :], in_=XR[:]), [XR], [WC])
                A1("dve", lambda e: e.tensor_scalar(out=WC[:, :, 1, :], in0=XR[:], scalar1=-1.0, scalar2=None, op0=ALU.mult), [XR], [WC])
                A1("dve", lambda e: e.tensor_scalar(out=WC[:, :, 2, :], in0=XI[:], scalar1=-1.0, scalar2=None, op0=ALU.mult), [XI], [WC])
                P.barrier()
            CH1 = CH + 1
            c30 = C.sb(es, [128, CH1], I32)
            op("pool", lambda e: e.iota(c30[:], pattern=[[0, CH1]], base=2 ** 30, channel_multiplier=0), [], [c30.b])
            ub = C.sb(es, [128, T], BF16); yacc = C.sb(es, [128, T])
            an = C.sb(es, [128, CH1], I32); anc = C.sb(es, [128, CH1], I32)
            snT = [C.sb(es, [128, CH1]) for _ in range(2)]; csT = [C.sb(es, [128, CH1]) for _ in range(2)]
            rotc = [C.sb(es, [128, 8]) for _ in range(2)]
            NB2 = 2
            mk = lambda dt=F32: [C.sb(es, [128, CH], dt) for _ in range(NB2)]
            bre = [C.sb(es, [128, CH]) for _ in range(3)]; bim = [C.sb(es, [128, CH]) for _ in range(3)]
            ta = mk(); tc = mk(); wr_ = mk(); wi_ = mk(); gr = mk(); gi_ = mk()
            pa_ = mk(BF16); pb2 = mk(BF16); pc2 = mk(BF16); pd2 = mk(BF16)
            car = C.sb(es, [128, 4])
            pbu = [C.ps(es, [128, 512]) for _ in range(2)]; pyy = [C.ps(es, [128, 3, 512]) for _ in range(2)]
            SC = float(2 * math.pi / 2 ** 32)
            v3 = lambda ap2: ap2.rearrange("p (u t) -> p u t", u=3)
            for ctile in range(4):
                for c in range(NCH):
                    c0 = c * CH; uf = wr_[c % NB2]
                    P.dma(uf[:], zs[ctile, :, c0:c0 + CH], reads=[zsB], writes=[uf.b])
                    op("pool", lambda e: e.tensor_copy(out=ub[:, c0:c0 + CH], in_=uf[:]), [uf.b], [ub.b])
                    op("act", lambda e: e.activation(out=yacc[:, c0:c0 + CH], in_=uf[:], func=AF.Copy, scale=dsk[:, ctile:ctile + 1]), [uf.b, dsk.b], [yacc.b])
                its = []
                for jj in range(4):
                    for d in range(2):
                        st = d * 16 + ctile * 4 + jj
                        order = list(range(NCH)) if d == 0 else list(range(NCH - 1, -1, -1))
                        for ci, c in enumerate(order):
                            its.append((st, d, ci, c))

                def tables(st, par):
                    sn = snT[par]; cs = csT[par]; rc = rotc[par]
                    op("pool", lambda e: e.iota(an[:], pattern=[[1, CH1]], base=0, channel_multiplier=0), [], [an.b])
                    op("pool", lambda e: e.tensor_tensor(out=an[:], in0=an[:], in1=bc(phi[:, st:st + 1], [128, CH1]), op=ALU.mult), [an.b, phi.b], [an.b])
                    op("pool", lambda e: e.tensor_tensor(out=anc[:], in0=an[:], in1=c30[:], op=ALU.add), [an.b, c30.b], [anc.b])
                    op("act", lambda e: e.activation(out=sn[:], in_=an[:], func=AF.Sin, scale=SC), [an.b], [sn.b])
                    op("act", lambda e: e.activation(out=cs[:], in_=anc[:], func=AF.Sin, scale=SC), [anc.b], [cs.b])
                    op("act", lambda e: e.copy(out=rc[:, 0:1], in_=cs[:, CH:CH1]), [cs.b], [rc.b])
                    op("act", lambda e: e.copy(out=rc[:, 1:2], in_=sn[:, CH:CH1]), [sn.b], [rc.b])
                    op("act", lambda e: e.mul(out=rc[:, 2:3], in_=sn[:, CH:CH1], mul=-1.0), [sn.b], [rc.b])
                    op("act", lambda e: e.activation(out=rc[:, 3:6], in_=rc[:, 0:3], func=AF.Copy, scale=fcol), [rc.b, flg.b], [rc.b])

                def B0(k):
                    st, d, ci, c = its[k]
                    c0 = c * CH; s = k % 3
                    if ci == 0:
                        tables(st, (k // NCH) % 2)
                    for u in range(3):
                        for ri, dstt in ((0, bre[s]), (1, bim[s])):
                            pb_ = pbu[ri]
                            op("pe", lambda e: e.matmul(pb_[:, 0:TT], lhsT=WB[:, st, ri, :], rhs=ub[:, c0 + u * TT:c0 + (u + 1) * TT], start=True, stop=True), [WB.b, ub.b], [pb_.b])
                            op("act", lambda e: e.copy(out=dstt[:, u * TT:(u + 1) * TT], in_=pb_[:, 0:TT]), [pb_.b], [dstt.b])

                def B12(k):
                    st, d, ci, c = its[k]
                    s = k % NB2; par = (k // NCH) % 2
                    sn = snT[par]; cs = csT[par]; rc = rotc[par]
                    br_ = bre[k % 3]; bi2 = bim[k % 3]; t_a = ta[s]; t_c = tc[s]; wr = wr_[s]; wi = wi_[s]; g_r = gr[s]; g_i = gi_[s]
                    cs2 = cs[:, 0:CH]; sn2 = sn[:, 0:CH]
                    brv = br_[:] if d == 0 else br_[:, ::-1]
                    biv = bi2[:] if d == 0 else bi2[:, ::-1]
                    op("dve", lambda e: e.tensor_tensor(out=wr[:], in0=brv, in1=cs2, op=ALU.mult), [br_.b, cs.b], [wr.b])
                    op("dve", lambda e: e.tensor_tensor(out=t_a[:], in0=biv, in1=sn2, op=ALU.mult), [bi2.b, sn.b], [t_a.b])
                    op("dve", lambda e: e.tensor_tensor(out=wr[:], in0=wr[:], in1=t_a[:], op=ALU.add), [wr.b, t_a.b], [wr.b])
                    op("dve", lambda e: e.tensor_tensor(out=wi[:], in0=biv, in1=cs2, op=ALU.mult), [bi2.b, cs.b], [wi.b])
                    op("dve", lambda e: e.tensor_tensor(out=t_c[:], in0=brv, in1=sn2, op=ALU.mult), [br_.b, sn.b], [t_c.b])
                    op("dve", lambda e: e.tensor_tensor(out=wi[:], in0=wi[:], in1=t_c[:], op=ALU.subtract), [wi.b, t_c.b], [wi.b])
                    dec = bc(rdec[:, st:st + 1], [128, CH])
                    for ri, (src, dst) in enumerate(((wr, g_r), (wi, g_i))):
                        if ci == 0:
                            init = 0.0; rd = []
                        else:
                            init = car[:, ri:ri + 1]; rd = [car.b]
                        op("dve", lambda e: e.tensor_tensor_scan(out=dst[:], data0=dec, data1=src[:], initial=init, op0=ALU.mult, op1=ALU.add), [src.b, rdec.b] + rd, [dst.b])
                    if ci < NCH - 1:
                        crossing = (c % 2 == 1) if d == 0 else (c % 2 == 0)
                        o3 = 3 if crossing else 0
                        lr = g_r[:, CH - 1:CH]; li = g_i[:, CH - 1:CH]
                        op("act", lambda e: e.activation(out=car[:, 2:3], in_=li, func=AF.Copy, scale=rc[:, o3 + 2:o3 + 3]), [g_i.b, rc.b], [car.b])
                        op("act", lambda e: e.activation(out=car[:, 0:1], in_=lr, func=AF.Identity, scale=rc[:, o3 + 0:o3 + 1], bias=car[:, 2:3]), [g_r.b, rc.b, car.b], [car.b])
                        op("act", lambda e: e.activation(out=car[:, 3:4], in_=li, func=AF.Copy, scale=rc[:, o3 + 0:o3 + 1]), [g_i.b, rc.b], [car.b])
                        op("act", lambda e: e.activation(out=car[:, 1:2], in_=lr, func=AF.Identity, scale=rc[:, o3 + 1:o3 + 2], bias=car[:, 3:4]), [g_r.b, rc.b, car.b], [car.b])

                def B3p(k, eng):
                    st, d, ci, c = its[k]
                    s = k % NB2; par = (k // NCH) % 2
                    sn = snT[par]; cs = csT[par]
                    g_r = gr[s]; g_i = gi_[s]
                    cs2 = cs[:, 0:CH]; sn2 = sn[:, 0:CH]
                    if eng == "pool":
                        lst = ((pa_[s], g_r, cs2, cs), (pb2[s], g_i, sn2, sn))
                        extra = [wi_[(k + 1) % NB2].b] if k + 1 < n_it else []
                    elif eng == "dve2":
                        eng = "dve"
                        lst = ((pa_[s], g_r, cs2, cs), (pb2[s], g_i, sn2, sn))
                        extra = []
                    else:
                        lst = ((pc2[s], g_r, sn2, sn), (pd2[s], g_i, cs2, cs))
                        extra = []
                    for (dst, src, tab, tabb) in lst:
                        dv = dst[:] if d == 0 else dst[:, ::-1]
                        op(eng, lambda e: e.tensor_tensor(out=dv, in0=src[:], in1=tab, op=ALU.mult), [src.b, tabb.b] + extra, [dst.b])

                def B4a(k):
                    st, d, ci, c = its[k]
                    s = k % NB2
                    py = pyy[k % 2]
                    terms = ((0, pa_[s]), (1, pb2[s]), (2, pc2[s]), (2, pd2[s]))

                    c0 = c * CH

                    def mmy(e):
                        for u in range(3):
                            e.matmul(py[:, u, 0:TT], lhsT=ident[:], rhs=yacc[:, c0 + u * TT:c0 + (u + 1) * TT], start=True, stop=False)
                            for ti_, (slot, src) in enumerate(terms):
                                ins = e.matmul(py[:, u, 0:TT], lhsT=WC[:, st, slot, :], rhs=src[:, u * TT:(u + 1) * TT], start=False, stop=(ti_ == 3))
                        return ins
                    op("pe", mmy, [WC.b, ident.b, yacc.b] + [t_[1].b for t_ in terms], [py.b])

                def B4b(k):
                    st, d, ci, c = its[k]
                    c0 = c * CH
                    py = pyy[k % 2]
                    op("act", lambda e: e.copy(out=v3(yacc[:, c0:c0 + CH]), in_=py[:, :, 0:TT]), [py.b], [yacc.b])

                n_it = len(its)
                B0(0)
                if n_it > 1:
                    B0(1)
                B12(0)
                for k in range(n_it):
                    if k + 2 < n_it:
                        B0(k + 2)
                    if k + 1 < n_it:
                        B12(k + 1)
                    if k >= 1:
                        B4b(k - 1)
                    B3p(k, "dve2")
                    B3p(k, "dve")
                    B4a(k)
                B4b(n_it - 1)
                for c in range(NCH):
                    c0 = c * CH; s = c % 2
                    yab = pa_[s]
                    op("act", lambda e: e.activation(out=yab[:], in_=yacc[:, c0:c0 + CH], func=AF.Gelu_apprx_tanh), [yacc.b], [yab.b])
                    P.dma(mixs[ctile, :, c0:c0 + CH], yab[:], reads=[yab.b], writes=[mixB])
            P.barrier()

        with ExitStack() as es:
            wout = C.sb(es, [128, 8, D], BF16, nb=8); wglu = C.sb(es, [128, 4, 512], BF16, nb=4); bglu = C.sb(es, [128, 4])
            P.dma(bglu[:], b_glu[0].rearrange("(q p) -> p q", p=128), writes=[bglu.b])
            with ExitStack() as es2:
                stg = [C.sb(es2, [128, 1024]) for _ in range(3)]
                for k in range(8):
                    load_cast(es2, wout[:, k, :], wout.bs[k], w_out[0, k * 128:(k + 1) * 128, :], 1024, stg=stg)
                for k in range(4):
                    load_cast(es2, wglu[:, k, :], wglu.bs[k], w_glu[0, k * 128:(k + 1) * 128, :], 512, stg=stg)
                P.barrier()
            h0t = [C.sb(es, [128, 8, TT], nb=8) for _ in range(3)]; mx = [C.sb(es, [128, 8, TT], BF16) for _ in range(3)]
            ya2l = [C.sb(es, [128, 4, TT], BF16, nb=4) for _ in range(2)]; sg_ = [C.sb(es, [128, TT]) for _ in range(2)]
            pgl = [C.ps(es, [128, 512]) for _ in range(2)]; pwo = [C.ps(es, [128, 512]) for _ in range(2)]
            hs0v = hs0.rearrange("k p t -> p k t"); mixv = mixs.rearrange("k p t -> p k t")
            hs0T = [Buf("hs0t%d" % i) for i in range(NTILE)]

            def Q0(ti):
                t0 = ti * TT; h = h0t[ti % 3]; m_ = mx[ti % 3]
                P.dma(h[:], hs0v[:, :, t0:t0 + TT], reads=[hs0T[ti]], writes=list(h.bs))
                P.dma(m_[:], mixv[:, :, t0:t0 + TT], reads=[mixB], writes=[m_.b])

            def Q1(ti):
                m_ = mx[ti % 3]; ya2 = ya2l[ti % 2]
                for o in range(4):
                    pg_ = pgl[o % 2]; sgt = sg_[o % 2]

                    def mm(e):
                        for k in range(4):
                            ins = e.matmul(pg_[:, 0:TT], lhsT=wglu[:, k, o * 128:(o + 1) * 128], rhs=m_[:, k, :], start=(k == 0), stop=(k == 3))
                        return ins
                    op("pe", mm, list(wglu.bs) + [m_.b], [pg_.b])
                    op("act", lambda e: e.activation(out=sgt[:], in_=pg_[:, 0:TT], func=AF.Sigmoid, bias=bglu[:, o:o + 1]), [pg_.b, bglu.b], [sgt.b])
                    op("dve", lambda e: e.tensor_tensor(out=ya2[:, o, :], in0=m_[:, o, :], in1=sgt[:], op=ALU.mult), [m_.b, sgt.b], [ya2.bs[o]])

            def Q2(ti):
                t0 = ti * TT; h = h0t[ti % 3]; m_ = mx[ti % 3]; ya2 = ya2l[ti % 2]
                for o in range(8):
                    pw = pwo[o % 2]

                    def mm2(e):
                        for k in range(8):
                            rhs = ya2[:, k, :] if k < 4 else m_[:, k, :]
                            ins = e.matmul(pw[:, 0:TT], lhsT=wout[:, k, o * 128:(o + 1) * 128], rhs=rhs, start=(k == 0), stop=(k == 7))
                        return ins
                    op("pe", mm2, list(wout.bs) + list(ya2.bs) + [m_.b], [pw.b])
                    op("dve", lambda e: e.tensor_tensor(out=h[:, o, :], in0=pw[:, 0:TT], in1=h[:, o, :], op=ALU.add), [pw.b, h.bs[o]], [h.bs[o]])
                P.dma(hs0v[:, :, t0:t0 + TT], h[:], reads=list(h.bs), writes=[hs0T[ti]])

            pipeline(NTILE, [Q0, Q1, Q2])
            P.barrier()

        def ffn_phase(layer, src, srcB, dst, dstB, final):
            N = TT + 2
            with ExitStack() as es:
                W = ffn_setup(es, layer); A = ffn_alloc(es)
                hh = [C.sb(es, [128, 8, N], nb=8) for _ in range(2)]; sq = C.sb(es, [128, 8, N], BF16, nb=8); rstd = C.sb(es, [128, N]); hn = C.sb(es, [128, 8, N], BF16)
                pss = A["pd"][0]
                if final:
                    ytok = [C.sb(es, [128, D])] * 2
                srcv = src.rearrange("k p t -> p k t")
                blks = [(0, 128), (128, 128), (256, TT - 256)]

                def prologue(ti):
                    t0 = ti * TT
                    h = hh[ti % 2]
                    lo, hi, off = tile_cols(t0)
                    if off > 0:
                        op("dve", lambda e: e.memset(h[:, :, 0:1], 0.0), [], list(h.bs))
                    if hi - lo + off < N:
                        op("dve", lambda e: e.memset(h[:, :, N - 1:N], 0.0), [], list(h.bs))
                    P.dma(h[:, :, off:off + hi - lo], srcv[:, :, lo:hi], reads=[srcB], writes=list(h.bs))
                    rmsnorm(es, h, h.bs, hn, hn.b, N, pss, sq, rstd)
                    halo_fix(hn, hn.b, t0)

                def finalize(ti):
                    t0 = ti * TT
                    h = hh[ti % 2]
                    yn = h
                    for k in range(8):
                        if k % 2 == 0:
                            op("act", lambda e: e.activation(out=sq[:, k, 0:TT], in_=h[:, k, 1:TT + 1], func=AF.Square), [h.bs[k]], [sq.bs[k]])
                        else:
                            op("pool", lambda e: e.tensor_tensor(out=sq[:, k, 0:TT], in0=h[:, k, 1:TT + 1], in1=h[:, k, 1:TT + 1], op=ALU.mult), [h.bs[k]], [sq.bs[k]])

                    def mmn(e):
                        for k in range(8):
                            ins = e.matmul(pss[:, 0:TT], lhsT=onesb[:], rhs=sq[:, k, 0:TT], start=(k == 0), stop=(k == 7))
                        return ins
                    op("pe", mmn, list(sq.bs) + [onesb.b], [pss.b])
                    op("act", lambda e: e.activation(out=rstd[:, 0:TT], in_=pss[:, 0:TT], func=AF.Sqrt, scale=1.0 / D, bias=EPS), [pss.b], [rstd.b])
                    op("dve", lambda e: e.reciprocal(out=rstd[:, 0:TT], in_=rstd[:, 0:TT]), [rstd.b], [rstd.b])
                    for k in range(8):
                        op("dve", lambda e: e.scalar_tensor_tensor(out=yn[:, k, 1:TT + 1], in0=h[:, k, 1:TT + 1], scalar=gfin[:, k:k + 1], in1=rstd[:, 0:TT], op0=ALU.mult, op1=ALU.mult),
                           [h.bs[k], rstd.b, gfin.b], [h.bs[k]])

                def finalize_b(ti):
                    t0 = ti * TT
                    h = hh[ti % 2]
                    yn = h
                    for bi, (o, nb_) in enumerate(blks):
                        yt = ytok[bi % 2]
                        for half in range(2):
                            pd = A["pd"][half]

                            def trf(e):
                                for kq in range(4):
                                    k = half * 4 + kq
                                    ins = e.transpose(out=pd[0:nb_, kq * 128:(kq + 1) * 128], in_=yn[:, k, 1 + o:1 + o + nb_], identity=ident[:])
                                return ins
                            op("pe", trf, [h.bs[half * 4 + kq] for kq in range(4)] + [ident.b], [pd.b])
                            if half == 0:
                                op("act", lambda e: e.copy(out=yt[0:nb_, 0:512], in_=pd[0:nb_, :]), [pd.b], [yt.b])
                            else:
                                op("dve", lambda e: e.tensor_copy(out=yt[0:nb_, 512:1024], in_=pd[0:nb_, :]), [pd.b], [yt.b])
                        P.dma(dst[t0 + o:t0 + o + nb_, :], yt[0:nb_, :], reads=[yt.b], writes=[dstB])

                prologue(0)
                for ti in range(NTILE):
                    t0 = ti * TT
                    h = hh[ti % 2]
                    ffn_body(W, A, hn, hn.b, h, h.bs, hook=(lambda ti=ti: prologue(ti + 1)) if ti + 1 < NTILE else None,
                             hook2=(lambda ti=ti: finalize(ti - 1)) if (final and ti >= 1) else None,
                             hook3=(lambda ti=ti: finalize_b(ti - 1)) if (final and ti >= 1) else None)
                    if not final:
                        P.dma(dst.rearrange("k p t -> p k t")[:, :, t0:t0 + TT], h[:, :, 1:TT + 1], reads=list(h.bs), writes=[dstB])
                if final:
                    finalize(NTILE - 1)
                    finalize_b(NTILE - 1)
                P.barrier()

        ffn_phase(0, hs0, hs0B, hs2, hs2B, False)

        with ExitStack() as es:
            wq = C.sb(es, [128, 8, D], BF16, nb=8); wkd = C.sb(es, [128, 8, 4, 128], BF16, nb=8); wv = C.sb(es, [128, 8, 256], BF16, nb=8)
            wo = C.sb(es, [128, 8, D], BF16, nb=8)
            with ExitStack() as es2:
                stg = [C.sb(es2, [128, 1536]) for _ in range(3)]
                for k in range(8):
                    sgt = stg[k % 3]
                    P.dma(sgt[:], w_qkv[0, k * 128:(k + 1) * 128, :], writes=[sgt.b])
                    gk = gmix[:, 1, k:k + 1]
                    op("dve", lambda e, k=k, sgt=sgt, gk=gk: e.tensor_scalar(out=wq[:, k, :], in0=sgt[:, 0:1024], scalar1=gk, scalar2=None, op0=ALU.mult), [sgt.b, gmix.b], [wq.bs[k]])
                    for half in range(2):
                        op("pool", lambda e, k=k, sgt=sgt, gk=gk, half=half: e.tensor_scalar(out=wkd[:, k, :, half * 64:(half + 1) * 64], in0=sgt[:, 1024:1280].rearrange("p (h d) -> p h d", h=4),
                                                                                           scalar1=gk, scalar2=None, op0=ALU.mult), [sgt.b, gmix.b], [wkd.bs[k]])
                    op("act", lambda e, k=k, sgt=sgt, gk=gk: e.activation(out=wv[:, k, :], in_=sgt[:, 1280:1536], func=AF.Copy, scale=gk), [sgt.b, gmix.b], [wv.bs[k]])
                stg2 = [C.sb(es2, [128, 1024]) for _ in range(2)]
                for k in range(8):
                    load_cast(es2, wo[:, k, :], wo.bs[k], w_o[0, k * 128:(k + 1) * 128, :], 1024, stg=stg2)
                P.barrier()
            bias = C.sb(es, [128, 16, 384]); sink = C.sb(es, [128, 16])
            P.dma(sink[:], bc(sink_d[0:1, :], [128, 16]), writes=[sink.b])
            with ExitStack() as es2:
                di = C.sb(es2, [128, 384], I32); df = C.sb(es2, [128, 384]); mk = C.sb(es2, [128, 384])
                op("pool", lambda e: e.iota(di[:], pattern=[[-1, 384]], base=128, channel_multiplier=1), [], [di.b])
                op("dve", lambda e: e.tensor_copy(out=df[:], in_=di[:]), [di.b], [df.b])
                op("dve", lambda e: e.tensor_scalar(out=mk[:], in0=df[:], scalar1=-1.0, scalar2=None, op0=ALU.mult), [df.b], [mk.b])
                op("dve", lambda e: e.tensor_tensor(out=df[:], in0=df[:], in1=mk[:], op=ALU.max), [df.b, mk.b], [df.b])
                op("dve", lambda e: e.tensor_scalar(out=mk[:], in0=df[:], scalar1=128.0, scalar2=NEG, op0=ALU.is_gt, op1=ALU.mult), [df.b], [mk.b])
                for hh in range(16):
                    slope = 2.0 ** (-8.0 * (hh + 1) / 16.0)
                    op("dve", lambda e, hh=hh, slope=slope: e.scalar_tensor_tensor(out=bias[:, hh, :], in0=df[:], scalar=-slope, in1=mk[:], op0=ALU.mult, op1=ALU.add), [df.b, mk.b], [bias.b])
                P.barrier()
            NCK = 19
            KT2 = C.sb(es, [128, 4, NCK * 128], BF16, nb=NCK); V = C.sb(es, [128, NCK, 256], BF16, nb=NCK); QT = C.sb(es, [128, 8, 17 * 128], BF16, nb=17)
            h2c = [C.sb(es, [128, 8, 128], nb=8) for _ in range(2)]; sqc = [C.sb(es, [128, 8, 128], BF16, nb=8) for _ in range(2)]
            rsc = [C.sb(es, [128, 128]) for _ in range(2)]; hnc = [C.sb(es, [128, 8, 128], BF16) for _ in range(2)]
            Sb = [C.sb(es, [128, 385]) for _ in range(16)]; Pm = [C.sb(es, [128, 386], BF16) for _ in range(3)]; PT = [C.sb(es, [128, 384], BF16) for _ in range(3)]
            for hh in range(16):
                op("act", lambda e, hh=hh: e.copy(out=Sb[hh][:, 384:385], in_=sink[:, hh:hh + 1]), [sink.b], [Sb[hh].b])
            stat = [C.sb(es, [128, 4]) for _ in range(8)]
            otok = [C.sb(es, [128, D]) for _ in range(2)]; oT = [C.sb(es, [128, 8, 128], BF16) for _ in range(2)]
            h2b = [C.sb(es, [128, 8, 128], nb=8) for _ in range(2)]
            pmm = [C.ps(es, [128, 512]) for _ in range(2)]; pS = [C.ps(es, [128, 512]) for _ in range(2)]
            pPT = [C.ps(es, [128, 1024], BF16) for _ in range(2)]; pO = [C.ps(es, [128, 512]) for _ in range(2)]
            hs2v = hs2.rearrange("k p t -> p k t"); hs3v = hs3.rearrange("k p t -> p k t")
            nmm = [0]

            def grp(lhs_fn, rhs_fn, n, dst_fn, rd, wr, scale=None):
                pm = pmm[nmm[0] % 2]; nmm[0] += 1

                def mm(e):
                    for k in range(8):
                        ins = e.matmul(pm[:, 0:n], lhsT=lhs_fn(k), rhs=rhs_fn(k), start=(k == 0), stop=(k == 7))
                    return ins
                op("pe", mm, rd, [pm.b])
                if scale is not None:
                    op("act", lambda e: e.activation(out=dst_fn(), in_=pm[:, 0:n], func=AF.Copy, scale=scale), [pm.b], wr)
                elif nmm[0] % 2 == 0:
                    op("act", lambda e: e.copy(out=dst_fn(), in_=pm[:, 0:n]), [pm.b], wr)
                else:
                    op("dve", lambda e: e.tensor_copy(out=dst_fn(), in_=pm[:, 0:n]), [pm.b], wr)

            for sg in range(NSEG):
                s0 = sg * SEG

                def chunk_rng(i):
                    cs0 = s0 - 240 + 128 * i
                    return cs0, max(cs0, 0), min(cs0 + 128, T)

                def pa0(i):
                    cs0, lo, hi = chunk_rng(i)
                    if hi <= lo:
                        op("pool", lambda e: e.memset(KT2[:, :, i * 128:(i + 1) * 128], 0.0), [], [KT2.bs[i]])
                        op("pool", lambda e: e.memset(V[:, i, :], 0.0), [], [V.bs[i]])
                        return
                    hc = h2c[i % 2]
                    if hi - lo < 128:
                        op("dve", lambda e: e.memset(hc[:], 0.0), [], list(hc.bs))
                    P.dma(hc[:, :, lo - cs0:hi - cs0], hs2v[:, :, lo:hi], reads=[hs2B], writes=list(hc.bs))
                    rmsnorm(es, hc, hc.bs, hnc[i % 2], hnc[i % 2].b, 128, pmm[nmm[0] % 2], sqc[i % 2], rsc[i % 2]); nmm[0] += 1

                def pa1(i):
                    cs0, lo, hi = chunk_rng(i)
                    if hi <= lo:
                        return
                    hn_ = hnc[i % 2]
                    for hk in range(4):
                        grp(lambda k: wkd[:, k, hk, :], lambda k: hn_[:, k, :], 128, lambda: KT2[:, hk, i * 128:(i + 1) * 128], list(wkd.bs) + [hn_.b], [KT2.bs[i]])
                    grp(lambda k: hn_[:, k, :], lambda k: wv[:, k, :], 256, lambda: V[:, i, :], list(wv.bs) + [hn_.b], [V.bs[i]])
                    if 1 <= i <= 17:
                        for tq in range(8):
                            grp(lambda k: wq[:, k, tq * 128:(tq + 1) * 128], lambda k: hn_[:, k, :], 128, lambda: QT[:, tq, (i - 1) * 128:i * 128],
                                list(wq.bs) + [hn_.b], [QT.bs[i - 1]], scale=0.125)
                pipeline(NCK, [pa0, pa1])

                iters = [(i, hh) for i in range(1, 18) for hh in range(16)]

                def blk_info(i):
                    qs0 = s0 - 240 + 128 * i
                    ws = qs0 - 128
                    regions = []
                    if ws < s0:
                        regions.append((0, min(s0 - ws, 384), None if sg == 0 else nbcol))
                    if ws + 384 > s0 + SEG:
                        regions.append((s0 + SEG - ws, 384, None if sg == NSEG - 1 else nbcol))
                    if sg == NSEG - 1:
                        plo = max(PADLO - ws, 0); phi_ = min(T - ws, 384)
                        if plo < phi_:
                            regions.append((plo, phi_, npcol))
                    return qs0, regions

                def hd(n):
                    i, hh = iters[n]
                    hk = hh // 4; g = hh % 4
                    return i, hh, hk, hk * 2 + g // 2, 64 * (g % 2)

                def a0(n):
                    i, hh, hk, tq, base = hd(n)
                    ps_ = pS[n % 2]
                    if hh == 0:
                        qs0, _ = blk_info(i)
                        hb_ = h2b[i % 2]
                        qlo = max(qs0, 0)
                        if qlo > qs0:
                            op("dve", lambda e: e.memset(hb_[:], 0.0), [], list(hb_.bs))
                        P.dma(hb_[:, :, qlo - qs0:128], hs2v[:, :, qlo:qs0 + 128], reads=[hs2B], writes=list(hb_.bs))
                    op("pe", lambda e: e.matmul(ps_[:, 0:384], lhsT=QT[base:base + 64, tq, (i - 1) * 128:i * 128], rhs=KT2[base:base + 64, hk, (i - 1) * 128:(i + 2) * 128], start=True, stop=True),
                       [QT.bs[i - 1], KT2.bs[i - 1], KT2.bs[i], KT2.bs[i + 1]], [ps_.b])

                def a1(n):
                    i, hh, hk, tq, base = hd(n)
                    ps_ = pS[n % 2]; sb_ = Sb[hh]; st_ = stat[n % 8]
                    _, regions = blk_info(i)
                    op("dve", lambda e: e.tensor_tensor(out=sb_[:, 0:384], in0=ps_[:, 0:384], in1=bias[:, hh, :], op=ALU.add), [ps_.b, bias.b], [sb_.b])
                    for (rl, rh, sc) in regions:
                        if sc is None:
                            op("dve", lambda e: e.tensor_scalar(out=sb_[:, rl:rh], in0=sb_[:, rl:rh], scalar1=NEG, scalar2=None, op0=ALU.add), [sb_.b], [sb_.b])
                        else:
                            op("dve", lambda e: e.tensor_scalar(out=sb_[:, rl:rh], in0=sb_[:, rl:rh], scalar1=sc, scalar2=None, op0=ALU.add), [sb_.b, flg.b], [sb_.b])
                    op("dve", lambda e: e.reduce_max(out=st_[:, 1:2], in_=sb_[:, 0:385], axis=AX.X, negate=True), [sb_.b], [st_.b])

                def a2(n):
                    i, hh, hk, tq, base = hd(n)
                    sb_ = Sb[hh]; st_ = stat[n % 8]; pm_ = Pm[n % 3]
                    op("act", lambda e: e.activation(out=pm_[:, 0:385], in_=sb_[:, 0:385], func=AF.Exp, bias=st_[:, 1:2], accum_out=st_[:, 2:3]), [sb_.b, st_.b], [pm_.b, st_.b])

                def a3(n):
                    st_ = stat[n % 8]; pm_ = Pm[n % 3]; ppt = pPT[n % 2]
                    op("dve", lambda e: e.reciprocal(out=st_[:, 3:4], in_=st_[:, 2:3]), [st_.b], [st_.b])

                    def trp(e):
                        for c in range(3):
                            ins = e.transpose(out=ppt[:, c * 128:(c + 1) * 128], in_=pm_[:, c * 128:(c + 1) * 128], identity=identb[:])
                        return ins
                    op("pe", trp, [pm_.b, identb.b], [ppt.b])

                def a4(n):
                    ppt = pPT[n % 2]; pt_ = PT[n % 3]
                    op("act", lambda e: e.copy(out=pt_[:], in_=ppt[:, 0:384]), [ppt.b], [pt_.b])

                def a5(n):
                    i, hh, hk, tq, base = hd(n)
                    pt_ = PT[n % 3]; po = pO[n % 2]

                    def mo(e):
                        for c in range(3):
                            ins = e.matmul(po[:, 0:64], lhsT=pt_[:, c * 128:(c + 1) * 128], rhs=V[:, i - 1 + c, hk * 64:(hk + 1) * 64], start=(c == 0), stop=(c == 2))
                        return ins
                    op("pe", mo, [pt_.b, V.bs[i - 1], V.bs[i], V.bs[i + 1]], [po.b])

                def a6(n):
                    i, hh, hk, tq, base = hd(n)
                    po = pO[n % 2]; st_ = stat[n % 8]; ot = otok[i % 2]
                    op("act", lambda e: e.activation(out=ot[:, hh * 64:(hh + 1) * 64], in_=po[:, 0:64], func=AF.Copy, scale=st_[:, 3:4]), [po.b, st_.b], [ot.b])
                    if hh != 15:
                        return
                    qs0, _ = blk_info(i)
                    hb_ = h2b[i % 2]; oT_ = oT[i % 2]
                    for half in range(2):
                        pm = pmm[half]

                        def tro(e):
                            for kq in range(4):
                                k = half * 4 + kq
                                ins = e.transpose(out=pm[:, kq * 128:(kq + 1) * 128], in_=ot[:, k * 128:(k + 1) * 128], identity=ident[:])
                            return ins
                        op("pe", tro, [ot.b, ident.b], [pm.b])
                        if half == 0:
                            op("dve", lambda e: e.tensor_copy(out=oT_[:, 0:4, :].rearrange("p k t -> p (k t)"), in_=pm[:, :]), [pm.b], [oT_.b])
                        else:
                            op("act", lambda e: e.copy(out=oT_[:, 4:8, :].rearrange("p k t -> p (k t)"), in_=pm[:, :]), [pm.b], [oT_.b])
                    for o in range(8):
                        pm = pmm[nmm[0] % 2]; nmm[0] += 1

                        def mw(e):
                            for k in range(8):
                                ins = e.matmul(pm[:, 0:128], lhsT=wo[:, k, o * 128:(o + 1) * 128], rhs=oT_[:, k, :], start=(k == 0), stop=(k == 7))
                            return ins
                        op("pe", mw, list(wo.bs) + [oT_.b], [pm.b])
                        op("dve", lambda e: e.tensor_tensor(out=hb_[:, o, :], in0=pm[:, 0:128], in1=hb_[:, o, :], op=ALU.add), [pm.b, hb_.bs[o]], [hb_.bs[o]])
                    c_lo = 112 if i == 1 else 0
                    P.dma(hs3v[:, :, qs0 + c_lo:qs0 + 128], hb_[:, :, c_lo:128], reads=list(hb_.bs), writes=[hs3B])

                pipeline(len(iters), [a0, a1, a2, a3, a4, a5, a6])
            P.barrier()

        ffn_phase(1, hs3, hs3B, ys, ysB, True)
        P.barrier()
        block = top.enter_context(nc.Block())
        P.emit(block)
        print("plan: ops", P.nops, "waits", P.nwaits, "dmas", P.ndma, "sems", len(P.sems), flush=True)
    return nc


def core_inputs(inputs):
    meta = np.asarray(inputs["meta_tokens"], np.float32)
    xp = np.asarray(inputs["x_prompt"], np.float32)
    xsm = np.asarray(inputs["x_sample"], np.float32)
    wnames = ["norm_mix_g", "norm_ffn_g", "final_norm_g", "w_in_ab", "s5_lambda_re", "s5_lambda_im", "s5_log_dt", "s5_b_re", "s5_b_im",
              "s5_c_re", "s5_c_im", "s5_d", "w_glu", "b_glu", "lru_conv_w", "lru_conv_b", "lru_w_r", "lru_b_r", "lru_w_i", "lru_b_i",
              "lru_lambda", "w_out_ab", "w_qkv", "w_o", "attn_sink", "w_up", "ffn_conv_w", "ffn_conv_b", "w_down"]
    wts = {n: np.ascontiguousarray(np.asarray(inputs[n], np.float32)) for n in wnames}
    maps = []
    for c in range(8):
        X = np.zeros((T, D), np.float32)
        if c < 4:
            for k in range(4):
                X[k * SEG:k * SEG + 16] = meta
                X[k * SEG + 16:(k + 1) * SEG] = xp[4 * c + k]
            f = 0.0
        else:
            X[0:16] = meta
            X[16:16 + 8192] = xsm[c - 4]
            f = 1.0
        flags = np.tile(np.array([f, 1.0 - f, NEG * (1.0 - f), NEG * f], np.float32), (128, 1))
        m = {"xs": X, "flags": flags}
        m.update(wts)
        maps.append(m)
    return maps


_CACHE = {}


def kernel(**inputs):
    maps = core_inputs(inputs)
    if "nc" not in _CACHE:
        _CACHE["nc"] = build(debug=False)
    nc = _CACHE["nc"]
    res = run_bass_kernel_spmd(nc, maps, core_ids=list(range(8)))
    yp = np.empty((16, 2048, D), np.float32)
    ysm = np.empty((4, 8192, D), np.float32)
    for c in range(8):
        y = np.asarray(res.results[c]["ys"])
        if c < 4:
            for k in range(4):
                yp[4 * c + k] = y[k * SEG + 16:(k + 1) * SEG]
        else:
            ysm[c - 4] = y[16:16 + 8192]
    return (yp, ysm)
```

```python
import math
import os
import numpy as np
from contextlib import ExitStack
import concourse.bass as bass
import concourse.mybir as mybir
from concourse.bass_utils import run_bass_kernel_spmd

F32 = mybir.dt.float32
BF16 = mybir.dt.bfloat16
I32 = mybir.dt.int32
AF = mybir.ActivationFunctionType
ALU = mybir.AluOpType
AX = mybir.AxisListType

D = 1024
NSEG = 4
SEG = 2064
T = NSEG * SEG
TT = 344
NTILE = T // TT
CH = 1032
NCH = T // CH
DFF = 2816
NJ = DFF // 128
NEG = -1e30
EPS = 1e-6
PADLO = 8208
NDMASEM = 24
SEMCAP = 30000


class Buf:
    __slots__ = ("name", "w", "r")

    def __init__(self, name=""):
        self.name = name
        self.w = None
        self.r = []


class Rec:
    def __init__(self):
        self.calls = []

    def __getattr__(self, name):
        def f(*a, **k):
            self.calls.append((name, a, k))
            return self
        return f


class Plan:
    ENGS = ("pe", "dve", "act", "pool", "sp")

    def __init__(self, nc, es):
        self.nc = nc
        self.es = es
        self.items = {e: [] for e in self.ENGS}
        self.sems = {}
        self.epoch = {e: 0 for e in self.ENGS}
        self.count = {}
        for e in self.ENGS:
            self._newsem((e, 0))
        for i in range(NDMASEM):
            self._newsem(("dma", i))
        self.known = {e: {} for e in self.ENGS}
        self.ndma = 0
        self.nwaits = 0
        self.nops = 0

    def _newsem(self, key):
        self.sems[key] = self.es.enter_context(self.nc.semaphore("s%d" % len(self.sems)))
        self.count[key] = 0

    def _deps(self, eng, reads, writes):
        need = {}
        for b in reads:
            if b.w is not None:
                k, v = b.w
                if need.get(k, 0) < v:
                    need[k] = v
        for b in writes:
            if b.w is not None:
                k, v = b.w
                if need.get(k, 0) < v:
                    need[k] = v
            for k, v in b.r:
                if need.get(k, 0) < v:
                    need[k] = v
        waits = []
        kn = self.known[eng]
        for k, v in need.items():
            if kn.get(k, 0) >= v:
                continue
            kn[k] = v
            waits.append((k, v))
        self.nwaits += len(waits)
        return waits

    def _commit(self, sig, reads, writes):
        for b in reads:
            b.r.append(sig)
        for b in writes:
            b.w = sig
            b.r = []

    def op(self, eng, fn, reads=(), writes=()):
        waits = self._deps(eng, reads, writes)
        key = (eng, self.epoch[eng])
        if self.count[key] >= SEMCAP:
            self.epoch[eng] += 1
            key = (eng, self.epoch[eng])
            self._newsem(key)
        self.count[key] += 1
        val = self.count[key]
        rec = Rec()
        fn(rec)
        assert rec.calls
        self.items[eng].append((waits, rec.calls, (key, 1)))
        self._commit((key, val), reads, writes)
        self.nops += 1

    def dma(self, out, in_, reads=(), writes=(), eng="sp"):
        i = self.ndma % NDMASEM
        self.ndma += 1
        key = ("dma", i)
        waits = self._deps(eng, reads, writes)
        prev = self.count[key]
        if prev > 0 and self.known[eng].get(key, 0) < prev:
            self.known[eng][key] = prev
            waits.append((key, prev))
        self.count[key] += 16
        val = self.count[key]
        self.items[eng].append((waits, [("dma_start", (), {"out": out, "in_": in_})], (key, 16)))
        self._commit((key, val), reads, writes)

    def barrier(self):
        for eng in self.ENGS:
            waits = []
            for k, v in self.count.items():
                if v > 0 and self.known[eng].get(k, 0) < v:
                    self.known[eng][k] = v
                    waits.append((k, v))
            self.items[eng].append((waits, None, None))

    def emit(self, block):
        plan = self

        def run(engname):
            def f(e):
                for waits, fn, inc in plan.items[engname]:
                    for k, v in waits:
                        e.wait_ge(plan.sems[k], v)
                    if fn is None:
                        continue
                    for name, a, k in fn:
                        ins = getattr(e, name)(*a, **k)
                    ins.then_inc(plan.sems[inc[0]], inc[1])
            return f

        block.tensor(run("pe"))
        block.vector(run("dve"))
        block.scalar(run("act"))
        block.gpsimd(run("pool"))
        block.sync(run("sp"))


class Tl:
    def __init__(self, t, nb=1, name=""):
        self.t = t
        self.bs = [Buf(name + str(i)) for i in range(nb)]
        self.b = self.bs[0]

    def __getitem__(self, k):
        return self.t[k]


class Ctx:
    def __init__(self, nc, P):
        self.nc = nc
        self.P = P
        self.n = 0

    def sb(self, es, shape, dt=F32, nb=1):
        self.n += 1
        return Tl(es.enter_context(self.nc.sbuf_tensor("sb%d" % self.n, list(shape), dt)), nb, "sb%d_" % self.n)

    def ps(self, es, shape, dt=F32, nb=1):
        self.n += 1
        return Tl(es.enter_context(self.nc.psum_tensor("ps%d" % self.n, list(shape), dt)), nb, "ps%d_" % self.n)


def bc(ap, shape):
    return ap.to_broadcast(list(shape))


def pipeline(n, stage_fns):
    ns = len(stage_fns)
    for step in range(n + ns - 1):
        for si, f in enumerate(stage_fns):
            it = step - si
            if 0 <= it < n:
                f(it)


def build(debug=False):
    nc = bass.Bass("TRN2", target_bir_lowering=False)
    ein = lambda n, s, d=F32: nc.dram_tensor(n, list(s), d, kind="ExternalInput").ap()
    xs = ein("xs", [T, D])
    flags_d = ein("flags", [128, 4])
    g_mix = ein("norm_mix_g", [2, D]); g_ffn = ein("norm_ffn_g", [2, D]); g_fin = ein("final_norm_g", [D])
    w_in = ein("w_in_ab", [1, D, 1536])
    lam_re = ein("s5_lambda_re", [1, 2, 32, 64]); lam_im = ein("s5_lambda_im", [1, 2, 32, 64])
    log_dt = ein("s5_log_dt", [1, 2, 32])
    b_re = ein("s5_b_re", [1, 2, 32, 64, 16]); b_im = ein("s5_b_im", [1, 2, 32, 64, 16])
    c_re = ein("s5_c_re", [1, 2, 32, 16, 64]); c_im = ein("s5_c_im", [1, 2, 32, 16, 64])
    s5_d = ein("s5_d", [1, 512]); w_glu = ein("w_glu", [1, 512, 512]); b_glu = ein("b_glu", [1, 512])
    cwB = ein("lru_conv_w", [1, 4, 512]); cbB = ein("lru_conv_b", [1, 512])
    w_r = ein("lru_w_r", [1, 2, 8, 64, 64]); b_r = ein("lru_b_r", [1, 2, 512])
    w_i = ein("lru_w_i", [1, 2, 8, 64, 64]); b_i = ein("lru_b_i", [1, 2, 512])
    lru_lam = ein("lru_lambda", [1, 2, 512])
    w_out = ein("w_out_ab", [1, D, D]); w_qkv = ein("w_qkv", [1, D, 1536]); w_o = ein("w_o", [1, D, D])
    sink_d = ein("attn_sink", [1, 16])
    w_up = ein("w_up", [2, D, 2 * DFF]); cwF = ein("ffn_conv_w", [2, 3, 2 * DFF]); cbF = ein("ffn_conv_b", [2, 2 * DFF])
    w_down = ein("w_down", [2, DFF, D])
    ys = nc.dram_tensor("ys", [T, D], F32, kind="ExternalOutput").ap()
    skind = "ExternalOutput" if debug else "Internal"
    scr = lambda n, s, d=F32: nc.dram_tensor(n, list(s), d, kind=skind).ap()
    hs0 = scr("hs0", [8, 128, T]); zs = scr("zs", [12, 128, T]); mixs = scr("mixs", [8, 128, T], BF16)
    hs2 = scr("hs2", [8, 128, T]); hs3 = scr("hs3", [8, 128, T])
    hs0B = Buf("hs0"); zsB = Buf("zs"); mixB = Buf("mixs"); hs2B = Buf("hs2"); hs3B = Buf("hs3"); ysB = Buf("ys")

    with ExitStack() as top:
        P = Plan(nc, top)
        C = Ctx(nc, P)
        op = P.op
        nc_allow = top.enter_context(nc.allow_non_contiguous_dma(reason="small parameter layouts"))

        flg = C.sb(top, [128, 4])
        ident = C.sb(top, [128, 128]); identb = C.sb(top, [128, 128], BF16); onesb = C.sb(top, [128, 128], BF16)
        gmix = C.sb(top, [128, 2, 8]); gffn = C.sb(top, [128, 2, 8]); gfin = C.sb(top, [128, 8])
        P.dma(flg[:], flags_d[:, :], writes=[flg.b])
        P.dma(gmix[:], g_mix.rearrange("l (k p) -> p l k", p=128), writes=[gmix.b])
        P.dma(gffn[:], g_ffn.rearrange("l (k p) -> p l k", p=128), writes=[gffn.b])
        P.dma(gfin[:], g_fin.rearrange("(k p) -> p k", p=128), writes=[gfin.b])
        fcol = flg[:, 0:1]; nfcol = flg[:, 1:2]; nbcol = flg[:, 2:3]; npcol = flg[:, 3:4]
        with ExitStack() as es:
            it = C.sb(es, [128, 128], I32); ip = C.sb(es, [128, 1], I32); itf = C.sb(es, [128, 128]); ipf = C.sb(es, [128, 1])
            op("pool", lambda e: e.iota(it[:], pattern=[[1, 128]], base=0, channel_multiplier=0), writes=[it.b])
            op("pool", lambda e: e.iota(ip[:], pattern=[[0, 1]], base=0, channel_multiplier=1), writes=[ip.b])
            op("dve", lambda e: e.tensor_copy(out=itf[:], in_=it[:]), [it.b], [itf.b])
            op("dve", lambda e: e.tensor_copy(out=ipf[:], in_=ip[:]), [ip.b], [ipf.b])
            op("dve", lambda e: e.tensor_scalar(out=ident[:], in0=itf[:], scalar1=ipf[:, 0:1], scalar2=None, op0=ALU.is_equal),
               [itf.b, ipf.b], [ident.b])
            op("dve", lambda e: e.tensor_copy(out=identb[:], in_=ident[:]), [ident.b], [identb.b])
            op("dve", lambda e: e.memset(onesb[:], 1.0), [], [onesb.b])
            P.barrier()

        def load_cast(es_, dst, dstbuf, src_ap, ncols, scale_ap=None, scale_buf=None, eng_cycle=("dve", "act", "pool"), stg=None, k=[0]):
            s = stg[k[0] % len(stg)]
            eng = eng_cycle[k[0] % len(eng_cycle)]
            k[0] += 1
            P.dma(s[:, 0:ncols], src_ap, writes=[s.b])
            rd = [s.b] + ([scale_buf] if scale_buf is not None else [])
            if scale_ap is None:
                if eng == "act":
                    op("act", lambda e: e.copy(out=dst, in_=s[:, 0:ncols]), rd, [dstbuf])
                else:
                    op(eng, lambda e: e.tensor_copy(out=dst, in_=s[:, 0:ncols]), rd, [dstbuf])
            else:
                if eng == "act":
                    op("act", lambda e: e.activation(out=dst, in_=s[:, 0:ncols], func=AF.Copy, scale=scale_ap), rd, [dstbuf])
                else:
                    op(eng, lambda e: e.tensor_scalar(out=dst, in0=s[:, 0:ncols], scalar1=scale_ap, scalar2=None, op0=ALU.mult), rd, [dstbuf])

        def rmsnorm(es_, h, hbufs, hn, hnbuf, n, pss, sq, rstd, gscale=None):
            for k in range(8):
                eng = "act" if k % 2 == 0 else "pool"
                if eng == "act":
                    op("act", lambda e, k=k: e.activation(out=sq[:, k, 0:n], in_=h[:, k, 0:n], func=AF.Square), [hbufs[k]], [sq.bs[k]])
                else:
                    op("pool", lambda e, k=k: e.tensor_tensor(out=sq[:, k, 0:n], in0=h[:, k, 0:n], in1=h[:, k, 0:n], op=ALU.mult), [hbufs[k]], [sq.bs[k]])

            def mm(e):
                for k in range(8):
                    ins = e.matmul(pss[:, 0:n], lhsT=onesb[:], rhs=sq[:, k, 0:n], start=(k == 0), stop=(k == 7))
                return ins
            op("pe", mm, list(sq.bs) + [onesb.b], [pss.b])
            op("act", lambda e: e.activation(out=rstd[:, 0:n], in_=pss[:, 0:n], func=AF.Sqrt, scale=1.0 / D, bias=EPS), [pss.b], [rstd.b])
            op("dve", lambda e: e.reciprocal(out=rstd[:, 0:n], in_=rstd[:, 0:n]), [rstd.b], [rstd.b])
            if gscale is None:
                op("dve", lambda e: e.tensor_tensor(out=hn[:, :, 0:n], in0=h[:, :, 0:n], in1=bc(rstd[:, 0:n].unsqueeze(1), [128, 8, n]), op=ALU.mult),
                   list(hbufs) + [rstd.b], [hnbuf])
            else:
                for k in range(8):
                    op("dve", lambda e, k=k: e.scalar_tensor_tensor(out=hn[:, k, 0:n], in0=h[:, k, 0:n], scalar=gscale[:, k:k + 1], in1=rstd[:, 0:n],
                                                                  op0=ALU.mult, op1=ALU.mult), [hbufs[k], rstd.b], [hnbuf])

        def tile_cols(t0):
            lo = max(t0 - 1, 0); hi = min(t0 + TT + 1, T)
            return lo, hi, lo - (t0 - 1)

        def halo_fix(hn, hnbuf, t0):
            N = TT + 2
            if t0 == 0:
                op("dve", lambda e: e.memset(hn[:, :, 0:1], 0.0), [], [hnbuf])
            elif t0 % SEG == 0:
                op("dve", lambda e: e.tensor_scalar(out=hn[:, :, 0:1], in0=hn[:, :, 0:1], scalar1=fcol, scalar2=None, op0=ALU.mult), [hnbuf, flg.b], [hnbuf])
            if t0 + TT == T:
                op("dve", lambda e: e.memset(hn[:, :, N - 1:N], 0.0), [], [hnbuf])
                c0 = PADLO - (t0 - 1)
                op("dve", lambda e: e.tensor_scalar(out=hn[:, :, c0:N - 1], in0=hn[:, :, c0:N - 1], scalar1=nfcol, scalar2=None, op0=ALU.mult), [hnbuf, flg.b], [hnbuf])
            elif (t0 + TT) % SEG == 0:
                op("dve", lambda e: e.tensor_scalar(out=hn[:, :, N - 1:N], in0=hn[:, :, N - 1:N], scalar1=fcol, scalar2=None, op0=ALU.mult), [hnbuf, flg.b], [hnbuf])

        def ffn_setup(es, layer):
            W = {}
            W["up"] = C.sb(es, [128, 8, 2 * DFF], BF16, nb=8)
            W["dn"] = C.sb(es, [128, NJ, D], BF16, nb=NJ)
            W["cw"] = C.sb(es, [128, 3, 44]); W["cb"] = C.sb(es, [128, 44])
            P.dma(W["cw"][:], cwF[layer].rearrange("k (j p) -> p k j", p=128), writes=[W["cw"].b])
            P.dma(W["cb"][:], cbF[layer].rearrange("(j p) -> p j", p=128), writes=[W["cb"].b])
            with ExitStack() as es2:
                stg = [C.sb(es2, [128, 2816]) for _ in range(3)]
                for k in range(8):
                    for hlf in range(2):
                        load_cast(es2, W["up"][:, k, hlf * 2816:(hlf + 1) * 2816], W["up"].bs[k], w_up[layer, k * 128:(k + 1) * 128, hlf * 2816:(hlf + 1) * 2816], 2816,
                                  scale_ap=gffn[:, layer, k:k + 1], scale_buf=gffn.b, stg=stg)
                for j in range(NJ):
                    load_cast(es2, W["dn"][:, j, :], W["dn"].bs[j], w_down[layer, j * 128:(j + 1) * 128, :], 1024, stg=stg)
                P.barrier()
            return W

        def ffn_alloc(es):
            A = {}
            A["pa"] = [C.ps(es, [128, 512]) for _ in range(3)]
            A["pg"] = [C.ps(es, [128, 512]) for _ in range(3)]
            A["pd"] = [C.ps(es, [128, 512]) for _ in range(2)]
            A["ac"] = [C.sb(es, [128, TT]) for _ in range(4)]
            A["gc"] = [C.sb(es, [128, TT]) for _ in range(4)]
            A["t1"] = [C.sb(es, [128, TT]) for _ in range(3)]
            A["t2"] = [C.sb(es, [128, TT]) for _ in range(2)]
            A["m"] = C.sb(es, [128, NJ, TT], BF16, nb=NJ)
            return A

        def ffn_body(W, A, hn, hnbuf, hres, hresbufs, hook=None, hook2=None):
            N = TT + 2
            cw = W["cw"]; cb = W["cb"]

            def bufs(j):
                return (A["pa"][j % 3], A["pg"][j % 3], A["ac"][j % 4], A["gc"][j % 4], A["t1"][j % 3], A["t2"][j % 2])

            def s0(j):
                pa, pg, ac, gc, t1, t2 = bufs(j)

                def mm(e, col, pt):
                    for k in range(8):
                        ins = e.matmul(pt[:, 0:N], lhsT=W["up"][:, k, col * 128:(col + 1) * 128], rhs=hn[:, k, 0:N], start=(k == 0), stop=(k == 7))
                    return ins
                op("pe", lambda e: mm(e, j, pa), list(W["up"].bs) + [hnbuf], [pa.b])
                op("pe", lambda e: mm(e, NJ + j, pg), list(W["up"].bs) + [hnbuf], [pg.b])

            def s1(j):
                pa, pg, ac, gc, t1, t2 = bufs(j)
                for (pt, dst, col) in ((pg, gc, NJ + j), (pa, ac, j)):
                    op("act", lambda e: e.activation(out=dst[:], in_=pt[:, 1:TT + 1], func=AF.Identity, scale=cw[:, 1, col:col + 1], bias=cb[:, col:col + 1]), [pt.b, cw.b, cb.b], [dst.b])
                    op("dve", lambda e: e.scalar_tensor_tensor(out=dst[:], in0=pt[:, 0:TT], scalar=cw[:, 0, col:col + 1], in1=dst[:], op0=ALU.mult, op1=ALU.add), [pt.b, cw.b, dst.b], [dst.b])
                    op("dve", lambda e: e.scalar_tensor_tensor(out=dst[:], in0=pt[:, 2:TT + 2], scalar=cw[:, 2, col:col + 1], in1=dst[:], op0=ALU.mult, op1=ALU.add), [pt.b, cw.b, dst.b], [dst.b])

            def s2(j):
                pa, pg, ac, gc, t1, t2 = bufs(j)
                op("act", lambda e: e.activation(out=t1[:], in_=gc[:], func=AF.Gelu_apprx_tanh), [gc.b], [t1.b])

            def s3(j):
                pa, pg, ac, gc, t1, t2 = bufs(j)
                op("pool", lambda e: e.tensor_tensor(out=A["m"][:, j, :], in0=t1[:], in1=ac[:], op=ALU.mult), [t1.b, ac.b], [A["m"].bs[j]])

            stg_ = [s0, s1, s2, s3]
            for step in range(NJ + len(stg_) - 1):
                for si, f in enumerate(stg_):
                    it = step - si
                    if 0 <= it < NJ:
                        f(it)
                        if si == 0 and it == NJ - 1 and hook is not None:
                            hook()
                if step == 6 and hook2 is not None:
                    hook2()
            JH = NJ - 3
            banks4 = [A["pd"][0], A["pd"][1], A["pa"][1], A["pg"][1]]

            def mm_head(e):
                for j in range(JH):
                    for g in range(4):
                        ins = e.matmul(banks4[g][:, 0:TT], lhsT=W["dn"][:, j, g * 128:(g + 1) * 128], rhs=A["m"][:, j, :], start=(j == 0), stop=False)
                return ins
            op("pe", mm_head, list(W["dn"].bs) + list(A["m"].bs[0:JH]), [bk.b for bk in banks4])

            def mm_tail(e):
                for g in range(4):
                    for j in range(JH, NJ):
                        ins = e.matmul(banks4[g][:, 0:TT], lhsT=W["dn"][:, j, g * 128:(g + 1) * 128], rhs=A["m"][:, j, :], start=False, stop=(j == NJ - 1))
                return ins
            op("pe", mm_tail, list(W["dn"].bs) + list(A["m"].bs[JH:NJ]), [bk.b for bk in banks4])
            for g in range(4):
                op("dve", lambda e, g=g: e.tensor_tensor(out=hres[:, g, 1:TT + 1], in0=banks4[g][:, 0:TT], in1=hres[:, g, 1:TT + 1], op=ALU.add),
                   [banks4[g].b, hresbufs[g]], [hresbufs[g]])
            for o in range(4, 8):
                pd = A["pd"][o % 2]

                def mmd(e, o=o, pd=pd):
                    for j in range(NJ):
                        ins = e.matmul(pd[:, 0:TT], lhsT=W["dn"][:, j, o * 128:(o + 1) * 128], rhs=A["m"][:, j, :], start=(j == 0), stop=(j == NJ - 1))
                    return ins
                op("pe", mmd, list(W["dn"].bs) + list(A["m"].bs), [pd.b])
                op("dve", lambda e, o=o, pd=pd: e.tensor_tensor(out=hres[:, o, 1:TT + 1], in0=pd[:, 0:TT], in1=hres[:, o, 1:TT + 1], op=ALU.add),
                   [pd.b, hresbufs[o]], [hresbufs[o]])

        with ExitStack() as es:
            win = C.sb(es, [128, 8, 1536], BF16, nb=8)
            with ExitStack() as es2:
                stg = [C.sb(es2, [128, 1536]) for _ in range(3)]
                for k in range(8):
                    load_cast(es2, win[:, k, :], win.bs[k], w_in[0, k * 128:(k + 1) * 128, :], 1536, scale_ap=gmix[:, 0, k:k + 1], scale_buf=gmix.b, stg=stg)
                P.barrier()
            xtok = [C.sb(es, [128, 3, D]) for _ in range(2)]
            h0 = [C.sb(es, [128, 8, TT], nb=8) for _ in range(2)]
            sq = C.sb(es, [128, 8, TT], BF16, nb=8); rstd = C.sb(es, [128, TT]); hnl = [C.sb(es, [128, 8, TT], BF16) for _ in range(2)]
            zt = [C.sb(es, [128, 12, TT], nb=12) for _ in range(2)]
            ptr = [C.ps(es, [128, 512]) for _ in range(2)]; pss = C.ps(es, [128, 512]); pz = [C.ps(es, [128, 512]) for _ in range(3)]
            blks = [(0, 128), (128, 128), (256, TT - 256)]

            def xload(ti):
                t0 = ti * TT; xt = xtok[ti % 2]
                for bi, (o, nb_) in enumerate(blks):
                    P.dma(xt[0:nb_, bi, :], xs[t0 + o:t0 + o + nb_, :], writes=[xt.b])

            def R0(ti):
                t0 = ti * TT; s = ti % 2
                xt = xtok[s]; h = h0[s]
                if ti == 0:
                    xload(0)
                if ti + 1 < NTILE:
                    xload(ti + 1)
                for k in range(8):
                    pt = ptr[k % 2]

                    def tr(e):
                        for bi, (o, nb_) in enumerate(blks):
                            ins = e.transpose(out=pt[:, o:o + nb_], in_=xt[0:nb_, bi, k * 128:(k + 1) * 128], identity=ident[0:nb_, 0:nb_])
                        return ins
                    op("pe", tr, [xt.b, ident.b], [pt.b])
                    if k % 2 == 0:
                        op("act", lambda e: e.copy(out=h[:, k, :], in_=pt[:, 0:TT]), [pt.b], [h.bs[k]])
                    else:
                        op("dve", lambda e: e.tensor_copy(out=h[:, k, :], in_=pt[:, 0:TT]), [pt.b], [h.bs[k]])
                P.dma(hs0.rearrange("k p t -> p k t")[:, :, t0:t0 + TT], h[:], reads=list(h.bs), writes=[hs0B])

            def R1(ti):
                h = h0[ti % 2]; hn = hnl[ti % 2]
                rmsnorm(es, h, h.bs, hn, hn.b, TT, pss, sq, rstd)

            def R2(ti):
                t0 = ti * TT; z = zt[ti % 2]; hn = hnl[ti % 2]
                for o in range(12):
                    pzz = pz[o % 3]

                    def mm(e):
                        for k in range(8):
                            ins = e.matmul(pzz[:, 0:TT], lhsT=win[:, k, o * 128:(o + 1) * 128], rhs=hn[:, k, :], start=(k == 0), stop=(k == 7))
                        return ins
                    op("pe", mm, list(win.bs) + [hn.b], [pzz.b])
                    if o % 2 == 0:
                        op("act", lambda e: e.copy(out=z[:, o, :], in_=pzz[:, 0:TT]), [pzz.b], [z.bs[o]])
                    else:
                        op("dve", lambda e: e.tensor_copy(out=z[:, o, :], in_=pzz[:, 0:TT]), [pzz.b], [z.bs[o]])
                P.dma(zs.rearrange("k p t -> p k t")[:, :, t0:t0 + TT], z[:], reads=list(z.bs), writes=[zsB])

            pipeline(NTILE, [R0, R1, R2])
            P.barrier()

        with ExitStack() as es:
            cw = C.sb(es, [128, 4, 4]); cbt = C.sb(es, [128, 4]); ncw = C.sb(es, [128, 4, 4])
            br = C.sb(es, [128, 2, 4]); bi_ = C.sb(es, [128, 2, 4]); cp = C.sb(es, [128, 2, 4])
            P.dma(cw[:], cwB[0].rearrange("k (q p) -> p k q", p=128), writes=[cw.b])
            P.dma(cbt[:], cbB[0].rearrange("(q p) -> p q", p=128), writes=[cbt.b])
            P.dma(br[:], b_r[0].rearrange("d (q p) -> p d q", p=128), writes=[br.b])
            P.dma(bi_[:], b_i[0].rearrange("d (q p) -> p d q", p=128), writes=[bi_.b])
            P.dma(cp[:], lru_lam[0].rearrange("d (q p) -> p d q", p=128), writes=[cp.b])
            op("act", lambda e: e.activation(out=cp[:], in_=cp[:], func=AF.Exp, scale=-1.0), [cp.b], [cp.b])
            op("act", lambda e: e.activation(out=cp[:], in_=cp[:], func=AF.Ln, bias=1.0), [cp.b], [cp.b])
            op("dve", lambda e: e.tensor_scalar(out=cp[:], in0=cp[:], scalar1=-8.0, scalar2=None, op0=ALU.mult), [cp.b], [cp.b])
            op("dve", lambda e: e.tensor_scalar(out=ncw[:], in0=cw[:], scalar1=nfcol, scalar2=-1.0, op0=ALU.mult, op1=ALU.mult), [cw.b, flg.b], [ncw.b])
            wg = C.sb(es, [128, 2, 2, 4, 128], BF16)
            with ExitStack() as es2:
                wst = C.sb(es2, [128, 2, 2, 4, 128])
                op("dve", lambda e: e.memset(wst[:], 0.0), [], [wst.b])
                for gi, wsrc in enumerate((w_r, w_i)):
                    for d in range(2):
                        for q in range(4):
                            for hh in range(2):
                                P.dma(wst[64 * hh:64 * hh + 64, gi, d, q, 64 * hh:64 * hh + 64], wsrc[0, d, 2 * q + hh, :, :], writes=[wst.b])
                op("dve", lambda e: e.tensor_copy(out=wg[:], in_=wst[:]), [wst.b], [wg.b])
                P.barrier()
            xb = C.sb(es, [128, T]); gb = C.sb(es, [128, T]); xc = C.sb(es, [128, T]); xcb = C.sb(es, [128, T], BF16); hsum = C.sb(es, [128, T])
            rr = [C.sb(es, [128, CH]) for _ in range(2)]; ii = [C.sb(es, [128, CH]) for _ in range(3)]
            aa = [C.sb(es, [128, CH]) for _ in range(2)]; ss_ = [C.sb(es, [128, CH]) for _ in range(2)]
            hb = [C.sb(es, [128, CH]) for _ in range(2)]
            carry = C.sb(es, [128, 1]); ybc = [C.sb(es, [128, CH], BF16) for _ in range(2)]
            pgr = [C.ps(es, [128, 3, 512]) for _ in range(2)]
            for q in range(4):
                if q == 0:
                    P.dma(xb[:], zs[4 + q, :, :], reads=[zsB], writes=[xb.b])
                P.dma(gb[:], zs[8 + q, :, :], reads=[zsB], writes=[gb.b])
                op("act", lambda e, q=q: e.activation(out=xc[:], in_=xb[:], func=AF.Identity, scale=cw[:, 2, q:q + 1], bias=cbt[:, q:q + 1]), [xb.b, cw.b, cbt.b], [xc.b])
                op("dve", lambda e, q=q: e.scalar_tensor_tensor(out=xc[:, 2:T], in0=xb[:, 0:T - 2], scalar=cw[:, 0, q:q + 1], in1=xc[:, 2:T], op0=ALU.mult, op1=ALU.add), [xb.b, xc.b, cw.b], [xc.b])
                op("dve", lambda e, q=q: e.scalar_tensor_tensor(out=xc[:, 1:T], in0=xb[:, 0:T - 1], scalar=cw[:, 1, q:q + 1], in1=xc[:, 1:T], op0=ALU.mult, op1=ALU.add), [xb.b, xc.b, cw.b], [xc.b])
                op("dve", lambda e, q=q: e.scalar_tensor_tensor(out=xc[:, 0:T - 1], in0=xb[:, 1:T], scalar=cw[:, 3, q:q + 1], in1=xc[:, 0:T - 1], op0=ALU.mult, op1=ALU.add), [xb.b, xc.b, cw.b], [xc.b])
                for sgi in range(1, NSEG):
                    B_ = sgi * SEG
                    for (to, fo, kk) in ((B_ - 1, B_, 3), (B_, B_ - 1, 1), (B_, B_ - 2, 0), (B_ + 1, B_ - 1, 0)):
                        op("dve", lambda e, to=to, fo=fo, kk=kk, q=q: e.scalar_tensor_tensor(out=xc[:, to:to + 1], in0=xb[:, fo:fo + 1], scalar=ncw[:, kk, q:q + 1], in1=xc[:, to:to + 1],
                                                                                       op0=ALU.mult, op1=ALU.add), [xb.b, xc.b, ncw.b], [xc.b])
                op("pool", lambda e: e.tensor_copy(out=xcb[:], in_=xc[:]), [xc.b], [xcb.b])
                if q + 1 < 4:
                    P.dma(xb[:], zs[4 + q + 1, :, :], reads=[zsB], writes=[xb.b])
                for d in range(2):
                    order = list(range(NCH)) if d == 0 else list(range(NCH - 1, -1, -1))

                    def L0(ci, d=d, q=q, order=order):
                        c = order[ci]; c0 = c * CH
                        pg = pgr[ci % 2]
                        for gi, dst, bias_ in ((0, rr[ci % 2], br), (1, ii[ci % 3], bi_)):
                            def mm(e):
                                for u in range(3):
                                    ins = e.matmul(pg[:, u, 0:TT], lhsT=wg[:, gi, d, q, :], rhs=xcb[:, c0 + u * TT:c0 + (u + 1) * TT], start=True, stop=True)
                                return ins
                            op("pe", mm, [wg.b, xcb.b], [pg.b])
                            op("act", lambda e: e.activation(out=dst[:].rearrange("p (u t) -> p u t", u=3), in_=pg[:, :, 0:TT], func=AF.Sigmoid, bias=bias_[:, d, q:q + 1]), [pg.b, bias_.b], [dst.b])

                    def L1(ci, d=d, q=q, order=order):
                        c = order[ci]; c0 = c * CH
                        r_ = rr[ci % 2]; i_ = ii[ci % 3]; a_ = aa[ci % 2]; s_ = ss_[ci % 2]
                        op("act", lambda e: e.activation(out=a_[:], in_=r_[:], func=AF.Exp, scale=cp[:, d, q:q + 1]), [r_.b, cp.b], [a_.b])
                        op("pool", lambda e: e.tensor_tensor(out=s_[:], in0=a_[:], in1=a_[:], op=ALU.mult), [a_.b], [s_.b])
                        op("act", lambda e: e.activation(out=s_[:], in_=s_[:], func=AF.Sqrt, scale=-1.0, bias=1.0), [s_.b], [s_.b])
                        op("pool", lambda e: e.tensor_tensor(out=i_[:], in0=i_[:], in1=xc[:, c0:c0 + CH], op=ALU.mult), [i_.b, xc.b], [i_.b])
                        op("dve", lambda e: e.tensor_tensor(out=i_[:], in0=i_[:], in1=s_[:], op=ALU.mult), [i_.b, s_.b], [i_.b])
                        if c == NCH - 1:
                            pc = PADLO - c0
                            op("dve", lambda e: e.tensor_scalar(out=i_[:, pc:CH], in0=i_[:, pc:CH], scalar1=nfcol, scalar2=None, op0=ALU.mult), [i_.b, flg.b], [i_.b])

                    def L2(ci, d=d, q=q, order=order):
                        c = order[ci]; c0 = c * CH
                        i_ = ii[ci % 3]; a_ = aa[ci % 2]
                        if ci == 0:
                            init = 0.0; rd = []
                        else:
                            init = carry[:, 0:1]; rd = [carry.b]
                        if d == 0:
                            op("dve", lambda e: e.tensor_tensor_scan(out=hsum[:, c0:c0 + CH], data0=a_[:], data1=i_[:], initial=init, op0=ALU.mult, op1=ALU.add), [a_.b, i_.b] + rd, [hsum.b])
                            last = hsum[:, c0 + CH - 1:c0 + CH]; lastb = hsum.b
                        else:
                            hbt = hb[ci % 2]
                            op("dve", lambda e: e.tensor_tensor_scan(out=hbt[:, ::-1], data0=a_[:, ::-1], data1=i_[:, ::-1], initial=init, op0=ALU.mult, op1=ALU.add), [a_.b, i_.b] + rd, [hbt.b])
                            last = hbt[:, 0:1]; lastb = hbt.b
                        crossing = (c % 2 == 1) if d == 0 else (c % 2 == 0)
                        if ci < NCH - 1:
                            if crossing:
                                op("act", lambda e: e.activation(out=carry[:], in_=last, func=AF.Copy, scale=fcol), [lastb, flg.b], [carry.b])
                            else:
                                op("act", lambda e: e.copy(out=carry[:], in_=last), [lastb], [carry.b])
                        if d == 1:
                            op("pool", lambda e: e.tensor_tensor(out=hsum[:, c0:c0 + CH], in0=hsum[:, c0:c0 + CH], in1=hbt[:], op=ALU.add), [hbt.b, hsum.b], [hsum.b])

                    pipeline(NCH, [L0, L1, L2])
                for c in range(NCH):
                    c0 = c * CH; s = c % 2
                    t1 = rr[s]; yb = ybc[s]
                    op("act", lambda e: e.activation(out=t1[:], in_=gb[:, c0:c0 + CH], func=AF.Gelu_apprx_tanh), [gb.b], [t1.b])
                    op("dve", lambda e: e.tensor_tensor(out=yb[:], in0=t1[:], in1=hsum[:, c0:c0 + CH], op=ALU.mult), [t1.b, hsum.b], [yb.b])
                    P.dma(mixs[4 + q, :, c0:c0 + CH], yb[:], reads=[yb.b], writes=[mixB])
            P.barrier()

        with ExitStack() as es:
            rdec = C.sb(es, [128, 32]); phi = C.sb(es, [128, 32], I32)
            WB = C.sb(es, [128, 32, 2, 128], BF16)
            WC = C.sb(es, [128, 32, 3, 128], BF16)
            dsk = C.sb(es, [128, 4])
            P.dma(dsk[:], s5_d[0].rearrange("(q p) -> p q", p=128), writes=[dsk.b])
            with ExitStack() as es2:
                lre = C.sb(es2, [128, 32]); lim = C.sb(es2, [128, 32]); ldt = C.sb(es2, [128, 32])
                Bre = C.sb(es2, [128, 32, 16]); Bim = C.sb(es2, [128, 32, 16])
                for gl in range(2):
                    sl = slice(64 * gl, 64 * gl + 64)
                    P.dma(lre[sl, :].rearrange("p (d j) -> p d j", d=2), lam_re[0].rearrange("d (j g) n -> g n d j", g=2)[gl], writes=[lre.b])
                    P.dma(lim[sl, :].rearrange("p (d j) -> p d j", d=2), lam_im[0].rearrange("d (j g) n -> g n d j", g=2)[gl], writes=[lim.b])
                    P.dma(ldt[sl, :].rearrange("p (d j) -> p d j", d=2), bc(log_dt[0].rearrange("d (j g) -> g d j", g=2)[gl:gl + 1], [64, 2, 16]), writes=[ldt.b])
                    for d in range(2):
                        P.dma(Bre[sl, d * 16:(d + 1) * 16, :], b_re[0, d].rearrange("(j g) n c -> g n j c", g=2)[gl], writes=[Bre.b])
                        P.dma(Bim[sl, d * 16:(d + 1) * 16, :], b_im[0, d].rearrange("(j g) n c -> g n j c", g=2)[gl], writes=[Bim.b])
                dtt = C.sb(es2, [128, 32]); xr = C.sb(es2, [128, 32]); xi = C.sb(es2, [128, 32]); er = C.sb(es2, [128, 32])
                ki = C.sb(es2, [128, 32], I32); kf = C.sb(es2, [128, 32]); fr = C.sb(es2, [128, 32]); pi_ = C.sb(es2, [128, 32], I32); pc_ = C.sb(es2, [128, 32], I32)
                cs = C.sb(es2, [128, 32]); sn = C.sb(es2, [128, 32]); q30 = C.sb(es2, [128, 32], I32)
                nr = C.sb(es2, [128, 32]); ni = C.sb(es2, [128, 32]); den = C.sb(es2, [128, 32]); cr = C.sb(es2, [128, 32]); ci_ = C.sb(es2, [128, 32]); tmp = C.sb(es2, [128, 32]); tmp2 = C.sb(es2, [128, 32])
                A1 = lambda eng, fn, r, w: op(eng, fn, [x.b for x in r], [x.b for x in w])
                A1("dve", lambda e: e.tensor_scalar(out=lre[:], in0=lre[:], scalar1=-1e-4, scalar2=None, op0=ALU.min), [lre], [lre])
                A1("act", lambda e: e.activation(out=dtt[:], in_=ldt[:], func=AF.Exp), [ldt], [dtt])
                A1("dve", lambda e: e.tensor_tensor(out=xr[:], in0=lre[:], in1=dtt[:], op=ALU.mult), [lre, dtt], [xr])
                A1("dve", lambda e: e.tensor_tensor(out=xi[:], in0=lim[:], in1=dtt[:], op=ALU.mult), [lim, dtt], [xi])
                A1("act", lambda e: e.activation(out=rdec[:], in_=xr[:], func=AF.Exp), [xr], [rdec])
                A1("dve", lambda e: e.tensor_scalar(out=fr[:], in0=xi[:], scalar1=float(1.0 / (2 * math.pi)), scalar2=None, op0=ALU.mult), [xi], [fr])
                A1("dve", lambda e: e.tensor_copy(out=ki[:], in_=fr[:]), [fr], [ki])
                A1("dve", lambda e: e.tensor_copy(out=kf[:], in_=ki[:]), [ki], [kf])
                A1("dve", lambda e: e.tensor_tensor(out=fr[:], in0=fr[:], in1=kf[:], op=ALU.subtract), [fr, kf], [fr])
                A1("dve", lambda e: e.tensor_scalar(out=fr[:], in0=fr[:], scalar1=float(2 ** 31), scalar2=None, op0=ALU.mult), [fr], [fr])
                A1("dve", lambda e: e.tensor_copy(out=pi_[:], in_=fr[:]), [fr], [pi_])
                A1("pool", lambda e: e.tensor_tensor(out=pi_[:], in0=pi_[:], in1=pi_[:], op=ALU.add), [pi_], [pi_])
                A1("pool", lambda e: e.iota(q30[:], pattern=[[0, 32]], base=2 ** 30, channel_multiplier=0), [], [q30])
                A1("pool", lambda e: e.tensor_tensor(out=pc_[:], in0=pi_[:], in1=q30[:], op=ALU.add), [pi_, q30], [pc_])
                A1("act", lambda e: e.activation(out=sn[:], in_=pi_[:], func=AF.Sin, scale=float(2 * math.pi / 2 ** 32)), [pi_], [sn])
                A1("act", lambda e: e.activation(out=cs[:], in_=pc_[:], func=AF.Sin, scale=float(2 * math.pi / 2 ** 32)), [pc_], [cs])
                A1("dve", lambda e: e.tensor_tensor(out=nr[:], in0=rdec[:], in1=cs[:], op=ALU.mult), [rdec, cs], [nr])
                A1("dve", lambda e: e.tensor_scalar(out=nr[:], in0=nr[:], scalar1=-1.0, scalar2=None, op0=ALU.add), [nr], [nr])
                A1("dve", lambda e: e.tensor_tensor(out=ni[:], in0=rdec[:], in1=sn[:], op=ALU.mult), [rdec, sn], [ni])
                A1("dve", lambda e: e.tensor_tensor(out=den[:], in0=lre[:], in1=lre[:], op=ALU.mult), [lre], [den])
                A1("dve", lambda e: e.tensor_tensor(out=tmp[:], in0=lim[:], in1=lim[:], op=ALU.mult), [lim], [tmp])
                A1("dve", lambda e: e.tensor_tensor(out=den[:], in0=den[:], in1=tmp[:], op=ALU.add), [den, tmp], [den])
                A1("dve", lambda e: e.reciprocal(out=den[:], in_=den[:]), [den], [den])
                A1("dve", lambda e: e.tensor_tensor(out=cr[:], in0=nr[:], in1=lre[:], op=ALU.mult), [nr, lre], [cr])
                A1("dve", lambda e: e.tensor_tensor(out=tmp[:], in0=ni[:], in1=lim[:], op=ALU.mult), [ni, lim], [tmp])
                A1("dve", lambda e: e.tensor_tensor(out=cr[:], in0=cr[:], in1=tmp[:], op=ALU.add), [cr, tmp], [cr])
                A1("dve", lambda e: e.tensor_tensor(out=cr[:], in0=cr[:], in1=den[:], op=ALU.mult), [cr, den], [cr])
                A1("dve", lambda e: e.tensor_tensor(out=ci_[:], in0=ni[:], in1=lre[:], op=ALU.mult), [ni, lre], [ci_])
                A1("dve", lambda e: e.tensor_tensor(out=tmp2[:], in0=nr[:], in1=lim[:], op=ALU.mult), [nr, lim], [tmp2])
                A1("dve", lambda e: e.tensor_tensor(out=ci_[:], in0=ci_[:], in1=tmp2[:], op=ALU.subtract), [ci_, tmp2], [ci_])
                A1("dve", lambda e: e.tensor_tensor(out=ci_[:], in0=ci_[:], in1=den[:], op=ALU.mult), [ci_, den], [ci_])
                A1("pool", lambda e: e.tensor_copy(out=phi[:], in_=pi_[:]), [pi_], [phi])
                XR = C.sb(es2, [128, 32, 128]); XI = C.sb(es2, [128, 32, 128]); Tm = C.sb(es2, [128, 32, 16]); Tm2 = C.sb(es2, [128, 32, 16])
                A1("pool", lambda e: e.memset(XR[:], 0.0), [], [XR])
                A1("pool", lambda e: e.memset(XI[:], 0.0), [], [XI])
                crb = lambda: bc(cr[:].unsqueeze(2), [128, 32, 16]); cib = lambda: bc(ci_[:].unsqueeze(2), [128, 32, 16])
                A1("dve", lambda e: e.tensor_tensor(out=Tm[:], in0=Bre[:], in1=crb(), op=ALU.mult), [Bre, cr], [Tm])
                A1("dve", lambda e: e.tensor_tensor(out=Tm2[:], in0=Bim[:], in1=cib(), op=ALU.mult), [Bim, ci_], [Tm2])
                A1("dve", lambda e: e.tensor_tensor(out=Tm[:], in0=Tm[:], in1=Tm2[:], op=ALU.subtract), [Tm, Tm2], [Tm])
                A1("dve", lambda e: e.tensor_tensor(out=Tm2[:], in0=Bre[:], in1=cib(), op=ALU.mult), [Bre, ci_], [Tm2])
                A1("dve", lambda e: e.tensor_tensor(out=Bre[:], in0=Bim[:], in1=crb(), op=ALU.mult), [Bim, cr, Bre], [Bre])
                A1("dve", lambda e: e.tensor_tensor(out=Tm2[:], in0=Tm2[:], in1=Bre[:], op=ALU.add), [Tm2, Bre], [Tm2])
                for st in range(32):
                    j = st % 16
                    for gl in range(2):
                        col = ((2 * j + gl) % 8) * 16
                        sl = slice(64 * gl, 64 * gl + 64)
                        A1("dve", lambda e, st=st, sl=sl, col=col: e.tensor_copy(out=XR[sl, st, col:col + 16], in_=Tm[sl, st, :]), [Tm], [XR])
                        A1("pool", lambda e, st=st, sl=sl, col=col: e.tensor_copy(out=XI[sl, st, col:col + 16], in_=Tm2[sl, st, :]), [Tm2], [XI])
                ptb = [C.ps(es2, [128, 512]) for _ in range(2)]
                for st in range(32):
                    for ri, X in enumerate((XR, XI)):
                        pt = ptb[(2 * st + ri) % 2]
                        op("pe", lambda e, st=st, X=X, pt=pt: e.transpose(out=pt[:, 0:128], in_=X[:, st, :], identity=ident[:]), [X.b, ident.b], [pt.b])
                        op("act", lambda e, st=st, ri=ri, pt=pt: e.copy(out=WB[:, st, ri, :], in_=pt[:, 0:128]), [pt.b], [WB.b])
                A1("pool", lambda e: e.memset(XR[:], 0.0), [WB], [XR])
                A1("pool", lambda e: e.memset(XI[:], 0.0), [WB], [XI])
                for st in range(32):
                    d = st // 16; j = st % 16
                    for gl in range(2):
                        g = 2 * j + gl; col = (g % 8) * 16
                        sl = slice(64 * gl, 64 * gl + 64)
                        P.dma(XR[sl, st, col:col + 16], c_re[0, d, g].rearrange("c n -> n c"), writes=[XR.b])
                        P.dma(XI[sl, st, col:col + 16], c_im[0, d, g].rearrange("c n -> n c"), writes=[XI.b])
                A1("dve", lambda e: e.tensor_copy(out=WC[:, :, 0, :], in_=XR[:]), [XR], [WC])
                A1("dve", lambda e: e.tensor_scalar(out=WC[:, :, 1, :], in0=XR[:], scalar1=-1.0, scalar2=None, op0=ALU.mult), [XR], [WC])
                A1("dve", lambda e: e.tensor_scalar(out=WC[:, :, 2, :], in0=XI[:], scalar1=-1.0, scalar2=None, op0=ALU.mult), [XI], [WC])
                P.barrier()
            CH1 = CH + 1
            c30 = C.sb(es, [128, CH1], I32)
            op("pool", lambda e: e.iota(c30[:], pattern=[[0, CH1]], base=2 ** 30, channel_multiplier=0), [], [c30.b])
            ub = C.sb(es, [128, T], BF16); yacc = C.sb(es, [128, T])
            an = C.sb(es, [128, CH1], I32); anc = C.sb(es, [128, CH1], I32)
            snT = [C.sb(es, [128, CH1]) for _ in range(2)]; csT = [C.sb(es, [128, CH1]) for _ in range(2)]
            rotc = [C.sb(es, [128, 8]) for _ in range(2)]
            NB2 = 2
            mk = lambda dt=F32: [C.sb(es, [128, CH], dt) for _ in range(NB2)]
            bre = [C.sb(es, [128, CH]) for _ in range(3)]; bim = [C.sb(es, [128, CH]) for _ in range(3)]
            ta = mk(); tc = mk(); wr_ = mk(); wi_ = mk(); gr = mk(); gi_ = mk()
            pa_ = mk(BF16); pb2 = mk(BF16); pc2 = mk(BF16); pd2 = mk(BF16)
            car = C.sb(es, [128, 4])
            pbu = [C.ps(es, [128, 512]) for _ in range(2)]; pyy = [C.ps(es, [128, 3, 512]) for _ in range(2)]
            SC = float(2 * math.pi / 2 ** 32)
            v3 = lambda ap2: ap2.rearrange("p (u t) -> p u t", u=3)
            for ctile in range(4):
                for c in range(NCH):
                    c0 = c * CH; uf = wr_[c % NB2]
                    P.dma(uf[:], zs[ctile, :, c0:c0 + CH], reads=[zsB], writes=[uf.b])
                    op("pool", lambda e: e.tensor_copy(out=ub[:, c0:c0 + CH], in_=uf[:]), [uf.b], [ub.b])
                    op("act", lambda e: e.activation(out=yacc[:, c0:c0 + CH], in_=uf[:], func=AF.Copy, scale=dsk[:, ctile:ctile + 1]), [uf.b, dsk.b], [yacc.b])
                its = []
                for jj in range(4):
                    for d in range(2):
                        st = d * 16 + ctile * 4 + jj
                        order = list(range(NCH)) if d == 0 else list(range(NCH - 1, -1, -1))
                        for ci, c in enumerate(order):
                            its.append((st, d, ci, c))

                def tables(st, par):
                    sn = snT[par]; cs = csT[par]; rc = rotc[par]
                    op("pool", lambda e: e.iota(an[:], pattern=[[1, CH1]], base=0, channel_multiplier=0), [], [an.b])
                    op("pool", lambda e: e.tensor_tensor(out=an[:], in0=an[:], in1=bc(phi[:, st:st + 1], [128, CH1]), op=ALU.mult), [an.b, phi.b], [an.b])
                    op("pool", lambda e: e.tensor_tensor(out=anc[:], in0=an[:], in1=c30[:], op=ALU.add), [an.b, c30.b], [anc.b])
                    op("act", lambda e: e.activation(out=sn[:], in_=an[:], func=AF.Sin, scale=SC), [an.b], [sn.b])
                    op("act", lambda e: e.activation(out=cs[:], in_=anc[:], func=AF.Sin, scale=SC), [anc.b], [cs.b])
                    op("act", lambda e: e.copy(out=rc[:, 0:1], in_=cs[:, CH:CH1]), [cs.b], [rc.b])
                    op("act", lambda e: e.copy(out=rc[:, 1:2], in_=sn[:, CH:CH1]), [sn.b], [rc.b])
                    op("act", lambda e: e.mul(out=rc[:, 2:3], in_=sn[:, CH:CH1], mul=-1.0), [sn.b], [rc.b])
                    op("act", lambda e: e.activation(out=rc[:, 3:6], in_=rc[:, 0:3], func=AF.Copy, scale=fcol), [rc.b, flg.b], [rc.b])

                def B0(k):
                    st, d, ci, c = its[k]
                    c0 = c * CH; s = k % 3
                    if ci == 0:
                        tables(st, (k // NCH) % 2)
                    for u in range(3):
                        for ri, dstt in ((0, bre[s]), (1, bim[s])):
                            pb_ = pbu[ri]
                            op("pe", lambda e: e.matmul(pb_[:, 0:TT], lhsT=WB[:, st, ri, :], rhs=ub[:, c0 + u * TT:c0 + (u + 1) * TT], start=True, stop=True), [WB.b, ub.b], [pb_.b])
                            op("act", lambda e: e.copy(out=dstt[:, u * TT:(u + 1) * TT], in_=pb_[:, 0:TT]), [pb_.b], [dstt.b])

                def B12(k):
                    st, d, ci, c = its[k]
                    s = k % NB2; par = (k // NCH) % 2
                    sn = snT[par]; cs = csT[par]; rc = rotc[par]
                    br_ = bre[k % 3]; bi2 = bim[k % 3]; t_a = ta[s]; t_c = tc[s]; wr = wr_[s]; wi = wi_[s]; g_r = gr[s]; g_i = gi_[s]
                    cs2 = cs[:, 0:CH]; sn2 = sn[:, 0:CH]
                    brv = br_[:] if d == 0 else br_[:, ::-1]
                    biv = bi2[:] if d == 0 else bi2[:, ::-1]
                    op("dve", lambda e: e.tensor_tensor(out=wr[:], in0=brv, in1=cs2, op=ALU.mult), [br_.b, cs.b], [wr.b])
                    op("dve", lambda e: e.tensor_tensor(out=t_a[:], in0=biv, in1=sn2, op=ALU.mult), [bi2.b, sn.b], [t_a.b])
                    op("dve", lambda e: e.tensor_tensor(out=wr[:], in0=wr[:], in1=t_a[:], op=ALU.add), [wr.b, t_a.b], [wr.b])
                    op("dve", lambda e: e.tensor_tensor(out=wi[:], in0=biv, in1=cs2, op=ALU.mult), [bi2.b, cs.b], [wi.b])
                    op("dve", lambda e: e.tensor_tensor(out=t_c[:], in0=brv, in1=sn2, op=ALU.mult), [br_.b, sn.b], [t_c.b])
                    op("dve", lambda e: e.tensor_tensor(out=wi[:], in0=wi[:], in1=t_c[:], op=ALU.subtract), [wi.b, t_c.b], [wi.b])
                    dec = bc(rdec[:, st:st + 1], [128, CH])
                    for ri, (src, dst) in enumerate(((wr, g_r), (wi, g_i))):
                        if ci == 0:
                            init = 0.0; rd = []
                        else:
                            init = car[:, ri:ri + 1]; rd = [car.b]
                        op("dve", lambda e: e.tensor_tensor_scan(out=dst[:], data0=dec, data1=src[:], initial=init, op0=ALU.mult, op1=ALU.add), [src.b, rdec.b] + rd, [dst.b])
                    if ci < NCH - 1:
                        crossing = (c % 2 == 1) if d == 0 else (c % 2 == 0)
                        o3 = 3 if crossing else 0
                        lr = g_r[:, CH - 1:CH]; li = g_i[:, CH - 1:CH]
                        op("act", lambda e: e.activation(out=car[:, 2:3], in_=li, func=AF.Copy, scale=rc[:, o3 + 2:o3 + 3]), [g_i.b, rc.b], [car.b])
                        op("act", lambda e: e.activation(out=car[:, 0:1], in_=lr, func=AF.Identity, scale=rc[:, o3 + 0:o3 + 1], bias=car[:, 2:3]), [g_r.b, rc.b, car.b], [car.b])
                        op("act", lambda e: e.activation(out=car[:, 3:4], in_=li, func=AF.Copy, scale=rc[:, o3 + 0:o3 + 1]), [g_i.b, rc.b], [car.b])
                        op("act", lambda e: e.activation(out=car[:, 1:2], in_=lr, func=AF.Identity, scale=rc[:, o3 + 1:o3 + 2], bias=car[:, 3:4]), [g_r.b, rc.b, car.b], [car.b])

                def B3p(k, eng):
                    st, d, ci, c = its[k]
                    s = k % NB2; par = (k // NCH) % 2
                    sn = snT[par]; cs = csT[par]
                    g_r = gr[s]; g_i = gi_[s]
                    cs2 = cs[:, 0:CH]; sn2 = sn[:, 0:CH]
                    if eng == "pool":
                        lst = ((pa_[s], g_r, cs2, cs), (pb2[s], g_i, sn2, sn))
                        extra = [wi_[(k + 1) % NB2].b] if k + 1 < n_it else []
                    elif eng == "dve2":
                        eng = "dve"
                        lst = ((pa_[s], g_r, cs2, cs), (pb2[s], g_i, sn2, sn))
                        extra = []
                    else:
                        lst = ((pc2[s], g_r, sn2, sn), (pd2[s], g_i, cs2, cs))
                        extra = []
                    for (dst, src, tab, tabb) in lst:
                        dv = dst[:] if d == 0 else dst[:, ::-1]
                        op(eng, lambda e: e.tensor_tensor(out=dv, in0=src[:], in1=tab, op=ALU.mult), [src.b, tabb.b] + extra, [dst.b])

                def B4a(k):
                    st, d, ci, c = its[k]
                    s = k % NB2
                    py = pyy[k % 2]
                    terms = ((0, pa_[s]), (1, pb2[s]), (2, pc2[s]), (2, pd2[s]))

                    c0 = c * CH

                    def mmy(e):
                        for u in range(3):
                            e.matmul(py[:, u, 0:TT], lhsT=ident[:], rhs=yacc[:, c0 + u * TT:c0 + (u + 1) * TT], start=True, stop=False)
                            for ti_, (slot, src) in enumerate(terms):
                                ins = e.matmul(py[:, u, 0:TT], lhsT=WC[:, st, slot, :], rhs=src[:, u * TT:(u + 1) * TT], start=False, stop=(ti_ == 3))
                        return ins
                    op("pe", mmy, [WC.b, ident.b, yacc.b] + [t_[1].b for t_ in terms], [py.b])

                def B4b(k):
                    st, d, ci, c = its[k]
                    c0 = c * CH
                    py = pyy[k % 2]
                    op("act", lambda e: e.copy(out=v3(yacc[:, c0:c0 + CH]), in_=py[:, :, 0:TT]), [py.b], [yacc.b])

                n_it = len(its)
                B0(0)
                if n_it > 1:
                    B0(1)
                B12(0)
                for k in range(n_it):
                    if k + 2 < n_it:
                        B0(k + 2)
                    if k + 1 < n_it:
                        B12(k + 1)
                    if k >= 1:
                        B4b(k - 1)
                    B3p(k, "dve2")
                    B3p(k, "dve")
                    B4a(k)
                B4b(n_it - 1)
                for c in range(NCH):
                    c0 = c * CH; s = c % 2
                    yab = pa_[s]
                    op("act", lambda e: e.activation(out=yab[:], in_=yacc[:, c0:c0 + CH], func=AF.Gelu_apprx_tanh), [yacc.b], [yab.b])
                    P.dma(mixs[ctile, :, c0:c0 + CH], yab[:], reads=[yab.b], writes=[mixB])
            P.barrier()

        with ExitStack() as es:
            wout = C.sb(es, [128, 8, D], BF16, nb=8); wglu = C.sb(es, [128, 4, 512], BF16, nb=4); bglu = C.sb(es, [128, 4])
            P.dma(bglu[:], b_glu[0].rearrange("(q p) -> p q", p=128), writes=[bglu.b])
            with ExitStack() as es2:
                stg = [C.sb(es2, [128, 1024]) for _ in range(3)]
                for k in range(8):
                    load_cast(es2, wout[:, k, :], wout.bs[k], w_out[0, k * 128:(k + 1) * 128, :], 1024, stg=stg)
                for k in range(4):
                    load_cast(es2, wglu[:, k, :], wglu.bs[k], w_glu[0, k * 128:(k + 1) * 128, :], 512, stg=stg)
                P.barrier()
            h0t = [C.sb(es, [128, 8, TT], nb=8) for _ in range(3)]; mx = [C.sb(es, [128, 8, TT], BF16) for _ in range(3)]
            ya2l = [C.sb(es, [128, 4, TT], BF16, nb=4) for _ in range(2)]; sg_ = [C.sb(es, [128, TT]) for _ in range(2)]
            pgl = [C.ps(es, [128, 512]) for _ in range(2)]; pwo = [C.ps(es, [128, 512]) for _ in range(2)]
            hs0v = hs0.rearrange("k p t -> p k t"); mixv = mixs.rearrange("k p t -> p k t")
            hs0T = [Buf("hs0t%d" % i) for i in range(NTILE)]

            def Q0(ti):
                t0 = ti * TT; h = h0t[ti % 3]; m_ = mx[ti % 3]
                P.dma(h[:], hs0v[:, :, t0:t0 + TT], reads=[hs0T[ti]], writes=list(h.bs))
                P.dma(m_[:], mixv[:, :, t0:t0 + TT], reads=[mixB], writes=[m_.b])

            def Q1(ti):
                m_ = mx[ti % 3]; ya2 = ya2l[ti % 2]
                for o in range(4):
                    pg_ = pgl[o % 2]; sgt = sg_[o % 2]

                    def mm(e):
                        for k in range(4):
                            ins = e.matmul(pg_[:, 0:TT], lhsT=wglu[:, k, o * 128:(o + 1) * 128], rhs=m_[:, k, :], start=(k == 0), stop=(k == 3))
                        return ins
                    op("pe", mm, list(wglu.bs) + [m_.b], [pg_.b])
                    op("act", lambda e: e.activation(out=sgt[:], in_=pg_[:, 0:TT], func=AF.Sigmoid, bias=bglu[:, o:o + 1]), [pg_.b, bglu.b], [sgt.b])
                    op("dve", lambda e: e.tensor_tensor(out=ya2[:, o, :], in0=m_[:, o, :], in1=sgt[:], op=ALU.mult), [m_.b, sgt.b], [ya2.bs[o]])

            def Q2(ti):
                t0 = ti * TT; h = h0t[ti % 3]; m_ = mx[ti % 3]; ya2 = ya2l[ti % 2]
                for o in range(8):
                    pw = pwo[o % 2]

                    def mm2(e):
                        for k in range(8):
                            rhs = ya2[:, k, :] if k < 4 else m_[:, k, :]
                            ins = e.matmul(pw[:, 0:TT], lhsT=wout[:, k, o * 128:(o + 1) * 128], rhs=rhs, start=(k == 0), stop=(k == 7))
                        return ins
                    op("pe", mm2, list(wout.bs) + list(ya2.bs) + [m_.b], [pw.b])
                    op("dve", lambda e: e.tensor_tensor(out=h[:, o, :], in0=pw[:, 0:TT], in1=h[:, o, :], op=ALU.add), [pw.b, h.bs[o]], [h.bs[o]])
                P.dma(hs0v[:, :, t0:t0 + TT], h[:], reads=list(h.bs), writes=[hs0T[ti]])

            pipeline(NTILE, [Q0, Q1, Q2])
            P.barrier()

        def ffn_phase(layer, src, srcB, dst, dstB, final):
            N = TT + 2
            with ExitStack() as es:
                W = ffn_setup(es, layer); A = ffn_alloc(es)
                hh = [C.sb(es, [128, 8, N], nb=8) for _ in range(2)]; sq = C.sb(es, [128, 8, N], BF16, nb=8); rstd = C.sb(es, [128, N]); hn = C.sb(es, [128, 8, N], BF16)
                pss = A["pd"][0]
                if final:
                    ytok = [C.sb(es, [128, D])] * 2
                srcv = src.rearrange("k p t -> p k t")
                blks = [(0, 128), (128, 128), (256, TT - 256)]

                def prologue(ti):
                    t0 = ti * TT
                    h = hh[ti % 2]
                    lo, hi, off = tile_cols(t0)
                    if off > 0:
                        op("dve", lambda e: e.memset(h[:, :, 0:1], 0.0), [], list(h.bs))
                    if hi - lo + off < N:
                        op("dve", lambda e: e.memset(h[:, :, N - 1:N], 0.0), [], list(h.bs))
                    P.dma(h[:, :, off:off + hi - lo], srcv[:, :, lo:hi], reads=[srcB], writes=list(h.bs))
                    rmsnorm(es, h, h.bs, hn, hn.b, N, pss, sq, rstd)
                    halo_fix(hn, hn.b, t0)

                def finalize(ti):
                    t0 = ti * TT
                    h = hh[ti % 2]
                    yn = h
                    for k in range(8):
                        if k % 2 == 0:
                            op("act", lambda e: e.activation(out=sq[:, k, 0:TT], in_=h[:, k, 1:TT + 1], func=AF.Square), [h.bs[k]], [sq.bs[k]])
                        else:
                            op("pool", lambda e: e.tensor_tensor(out=sq[:, k, 0:TT], in0=h[:, k, 1:TT + 1], in1=h[:, k, 1:TT + 1], op=ALU.mult), [h.bs[k]], [sq.bs[k]])

                    def mmn(e):
                        for k in range(8):
                            ins = e.matmul(pss[:, 0:TT], lhsT=onesb[:], rhs=sq[:, k, 0:TT], start=(k == 0), stop=(k == 7))
                        return ins
                    op("pe", mmn, list(sq.bs) + [onesb.b], [pss.b])
                    op("act", lambda e: e.activation(out=rstd[:, 0:TT], in_=pss[:, 0:TT], func=AF.Sqrt, scale=1.0 / D, bias=EPS), [pss.b], [rstd.b])
                    op("dve", lambda e: e.reciprocal(out=rstd[:, 0:TT], in_=rstd[:, 0:TT]), [rstd.b], [rstd.b])
                    for k in range(8):
                        op("dve", lambda e: e.scalar_tensor_tensor(out=yn[:, k, 1:TT + 1], in0=h[:, k, 1:TT + 1], scalar=gfin[:, k:k + 1], in1=rstd[:, 0:TT], op0=ALU.mult, op1=ALU.mult),
                           [h.bs[k], rstd.b, gfin.b], [h.bs[k]])
                    for bi, (o, nb_) in enumerate(blks):
                        yt = ytok[bi % 2]
                        for half in range(2):
                            pd = A["pd"][half]

                            def trf(e):
                                for kq in range(4):
                                    k = half * 4 + kq
                                    ins = e.transpose(out=pd[0:nb_, kq * 128:(kq + 1) * 128], in_=yn[:, k, 1 + o:1 + o + nb_], identity=ident[:])
                                return ins
                            op("pe", trf, [h.bs[half * 4 + kq] for kq in range(4)] + [ident.b], [pd.b])
                            if half == 0:
                                op("act", lambda e: e.copy(out=yt[0:nb_, 0:512], in_=pd[0:nb_, :]), [pd.b], [yt.b])
                            else:
                                op("dve", lambda e: e.tensor_copy(out=yt[0:nb_, 512:1024], in_=pd[0:nb_, :]), [pd.b], [yt.b])
                        P.dma(dst[t0 + o:t0 + o + nb_, :], yt[0:nb_, :], reads=[yt.b], writes=[dstB])

                prologue(0)
                for ti in range(NTILE):
                    t0 = ti * TT
                    h = hh[ti % 2]
                    ffn_body(W, A, hn, hn.b, h, h.bs, hook=(lambda ti=ti: prologue(ti + 1)) if ti + 1 < NTILE else None,
                             hook2=(lambda ti=ti: finalize(ti - 1)) if (final and ti >= 1) else None)
                    if not final:
                        P.dma(dst.rearrange("k p t -> p k t")[:, :, t0:t0 + TT], h[:, :, 1:TT + 1], reads=list(h.bs), writes=[dstB])
                if final:
                    finalize(NTILE - 1)
                P.barrier()

        ffn_phase(0, hs0, hs0B, hs2, hs2B, False)

        with ExitStack() as es:
            wq = C.sb(es, [128, 8, D], BF16, nb=8); wkd = C.sb(es, [128, 8, 4, 128], BF16, nb=8); wv = C.sb(es, [128, 8, 256], BF16, nb=8)
            wo = C.sb(es, [128, 8, D], BF16, nb=8)
            with ExitStack() as es2:
                stg = [C.sb(es2, [128, 1536]) for _ in range(3)]
                for k in range(8):
                    sgt = stg[k % 3]
                    P.dma(sgt[:], w_qkv[0, k * 128:(k + 1) * 128, :], writes=[sgt.b])
                    gk = gmix[:, 1, k:k + 1]
                    op("dve", lambda e, k=k, sgt=sgt, gk=gk: e.tensor_scalar(out=wq[:, k, :], in0=sgt[:, 0:1024], scalar1=gk, scalar2=None, op0=ALU.mult), [sgt.b, gmix.b], [wq.bs[k]])
                    for half in range(2):
                        op("pool", lambda e, k=k, sgt=sgt, gk=gk, half=half: e.tensor_scalar(out=wkd[:, k, :, half * 64:(half + 1) * 64], in0=sgt[:, 1024:1280].rearrange("p (h d) -> p h d", h=4),
                                                                                           scalar1=gk, scalar2=None, op0=ALU.mult), [sgt.b, gmix.b], [wkd.bs[k]])
                    op("act", lambda e, k=k, sgt=sgt, gk=gk: e.activation(out=wv[:, k, :], in_=sgt[:, 1280:1536], func=AF.Copy, scale=gk), [sgt.b, gmix.b], [wv.bs[k]])
                stg2 = [C.sb(es2, [128, 1024]) for _ in range(2)]
                for k in range(8):
                    load_cast(es2, wo[:, k, :], wo.bs[k], w_o[0, k * 128:(k + 1) * 128, :], 1024, stg=stg2)
                P.barrier()
            bias = C.sb(es, [128, 16, 384]); sink = C.sb(es, [128, 16])
            P.dma(sink[:], bc(sink_d[0:1, :], [128, 16]), writes=[sink.b])
            with ExitStack() as es2:
                di = C.sb(es2, [128, 384], I32); df = C.sb(es2, [128, 384]); mk = C.sb(es2, [128, 384])
                op("pool", lambda e: e.iota(di[:], pattern=[[-1, 384]], base=128, channel_multiplier=1), [], [di.b])
                op("dve", lambda e: e.tensor_copy(out=df[:], in_=di[:]), [di.b], [df.b])
                op("dve", lambda e: e.tensor_scalar(out=mk[:], in0=df[:], scalar1=-1.0, scalar2=None, op0=ALU.mult), [df.b], [mk.b])
                op("dve", lambda e: e.tensor_tensor(out=df[:], in0=df[:], in1=mk[:], op=ALU.max), [df.b, mk.b], [df.b])
                op("dve", lambda e: e.tensor_scalar(out=mk[:], in0=df[:], scalar1=128.0, scalar2=NEG, op0=ALU.is_gt, op1=ALU.mult), [df.b], [mk.b])
                for hh in range(16):
                    slope = 2.0 ** (-8.0 * (hh + 1) / 16.0)
                    op("dve", lambda e, hh=hh, slope=slope: e.scalar_tensor_tensor(out=bias[:, hh, :], in0=df[:], scalar=-slope, in1=mk[:], op0=ALU.mult, op1=ALU.add), [df.b, mk.b], [bias.b])
                P.barrier()
            NCK = 19
            KT2 = C.sb(es, [128, 4, NCK * 128], BF16, nb=NCK); V = C.sb(es, [128, NCK, 256], BF16, nb=NCK); QT = C.sb(es, [128, 8, 17 * 128], BF16, nb=17)
            h2c = [C.sb(es, [128, 8, 128], nb=8) for _ in range(2)]; sqc = [C.sb(es, [128, 8, 128], BF16, nb=8) for _ in range(2)]
            rsc = [C.sb(es, [128, 128]) for _ in range(2)]; hnc = [C.sb(es, [128, 8, 128], BF16) for _ in range(2)]
            Sb = [C.sb(es, [128, 385]) for _ in range(16)]; Pm = [C.sb(es, [128, 386], BF16) for _ in range(3)]; PT = [C.sb(es, [128, 384], BF16) for _ in range(3)]
            for hh in range(16):
                op("act", lambda e, hh=hh: e.copy(out=Sb[hh][:, 384:385], in_=sink[:, hh:hh + 1]), [sink.b], [Sb[hh].b])
            stat = [C.sb(es, [128, 4]) for _ in range(8)]
            otok = [C.sb(es, [128, D]) for _ in range(2)]; oT = [C.sb(es, [128, 8, 128], BF16) for _ in range(2)]
            h2b = [C.sb(es, [128, 8, 128], nb=8) for _ in range(2)]
            pmm = [C.ps(es, [128, 512]) for _ in range(2)]; pS = [C.ps(es, [128, 512]) for _ in range(2)]
            pPT = [C.ps(es, [128, 1024], BF16) for _ in range(2)]; pO = [C.ps(es, [128, 512]) for _ in range(2)]
            hs2v = hs2.rearrange("k p t -> p k t"); hs3v = hs3.rearrange("k p t -> p k t")
            nmm = [0]

            def grp(lhs_fn, rhs_fn, n, dst_fn, rd, wr, scale=None):
                pm = pmm[nmm[0] % 2]; nmm[0] += 1

                def mm(e):
                    for k in range(8):
                        ins = e.matmul(pm[:, 0:n], lhsT=lhs_fn(k), rhs=rhs_fn(k), start=(k == 0), stop=(k == 7))
                    return ins
                op("pe", mm, rd, [pm.b])
                if scale is not None:
                    op("act", lambda e: e.activation(out=dst_fn(), in_=pm[:, 0:n], func=AF.Copy, scale=scale), [pm.b], wr)
                elif nmm[0] % 2 == 0:
                    op("act", lambda e: e.copy(out=dst_fn(), in_=pm[:, 0:n]), [pm.b], wr)
                else:
                    op("dve", lambda e: e.tensor_copy(out=dst_fn(), in_=pm[:, 0:n]), [pm.b], wr)

            for sg in range(NSEG):
                s0 = sg * SEG

                def chunk_rng(i):
                    cs0 = s0 - 240 + 128 * i
                    return cs0, max(cs0, 0), min(cs0 + 128, T)

                def pa0(i):
                    cs0, lo, hi = chunk_rng(i)
                    if hi <= lo:
                        op("pool", lambda e: e.memset(KT2[:, :, i * 128:(i + 1) * 128], 0.0), [], [KT2.bs[i]])
                        op("pool", lambda e: e.memset(V[:, i, :], 0.0), [], [V.bs[i]])
                        return
                    hc = h2c[i % 2]
                    if hi - lo < 128:
                        op("dve", lambda e: e.memset(hc[:], 0.0), [], list(hc.bs))
                    P.dma(hc[:, :, lo - cs0:hi - cs0], hs2v[:, :, lo:hi], reads=[hs2B], writes=list(hc.bs))
                    rmsnorm(es, hc, hc.bs, hnc[i % 2], hnc[i % 2].b, 128, pmm[nmm[0] % 2], sqc[i % 2], rsc[i % 2]); nmm[0] += 1

                def pa1(i):
                    cs0, lo, hi = chunk_rng(i)
                    if hi <= lo:
                        return
                    hn_ = hnc[i % 2]
                    for hk in range(4):
                        grp(lambda k: wkd[:, k, hk, :], lambda k: hn_[:, k, :], 128, lambda: KT2[:, hk, i * 128:(i + 1) * 128], list(wkd.bs) + [hn_.b], [KT2.bs[i]])
                    grp(lambda k: hn_[:, k, :], lambda k: wv[:, k, :], 256, lambda: V[:, i, :], list(wv.bs) + [hn_.b], [V.bs[i]])
                    if 1 <= i <= 17:
                        for tq in range(8):
                            grp(lambda k: wq[:, k, tq * 128:(tq + 1) * 128], lambda k: hn_[:, k, :], 128, lambda: QT[:, tq, (i - 1) * 128:i * 128],
                                list(wq.bs) + [hn_.b], [QT.bs[i - 1]], scale=0.125)
                pipeline(NCK, [pa0, pa1])

                iters = [(i, hh) for i in range(1, 18) for hh in range(16)]

                def blk_info(i):
                    qs0 = s0 - 240 + 128 * i
                    ws = qs0 - 128
                    regions = []
                    if ws < s0:
                        regions.append((0, min(s0 - ws, 384), None if sg == 0 else nbcol))
                    if ws + 384 > s0 + SEG:
                        regions.append((s0 + SEG - ws, 384, None if sg == NSEG - 1 else nbcol))
                    if sg == NSEG - 1:
                        plo = max(PADLO - ws, 0); phi_ = min(T - ws, 384)
                        if plo < phi_:
                            regions.append((plo, phi_, npcol))
                    return qs0, regions

                def hd(n):
                    i, hh = iters[n]
                    hk = hh // 4; g = hh % 4
                    return i, hh, hk, hk * 2 + g // 2, 64 * (g % 2)

                def a0(n):
                    i, hh, hk, tq, base = hd(n)
                    ps_ = pS[n % 2]
                    if hh == 0:
                        qs0, _ = blk_info(i)
                        hb_ = h2b[i % 2]
                        qlo = max(qs0, 0)
                        if qlo > qs0:
                            op("dve", lambda e: e.memset(hb_[:], 0.0), [], list(hb_.bs))
                        P.dma(hb_[:, :, qlo - qs0:128], hs2v[:, :, qlo:qs0 + 128], reads=[hs2B], writes=list(hb_.bs))
                    op("pe", lambda e: e.matmul(ps_[:, 0:384], lhsT=QT[base:base + 64, tq, (i - 1) * 128:i * 128], rhs=KT2[base:base + 64, hk, (i - 1) * 128:(i + 2) * 128], start=True, stop=True),
                       [QT.bs[i - 1], KT2.bs[i - 1], KT2.bs[i], KT2.bs[i + 1]], [ps_.b])

                def a1(n):
                    i, hh, hk, tq, base = hd(n)
                    ps_ = pS[n % 2]; sb_ = Sb[hh]; st_ = stat[n % 8]
                    _, regions = blk_info(i)
                    op("dve", lambda e: e.tensor_tensor(out=sb_[:, 0:384], in0=ps_[:, 0:384], in1=bias[:, hh, :], op=ALU.add), [ps_.b, bias.b], [sb_.b])
                    for (rl, rh, sc) in regions:
                        if sc is None:
                            op("dve", lambda e: e.tensor_scalar(out=sb_[:, rl:rh], in0=sb_[:, rl:rh], scalar1=NEG, scalar2=None, op0=ALU.add), [sb_.b], [sb_.b])
                        else:
                            op("dve", lambda e: e.tensor_scalar(out=sb_[:, rl:rh], in0=sb_[:, rl:rh], scalar1=sc, scalar2=None, op0=ALU.add), [sb_.b, flg.b], [sb_.b])
                    op("dve", lambda e: e.reduce_max(out=st_[:, 1:2], in_=sb_[:, 0:385], axis=AX.X, negate=True), [sb_.b], [st_.b])

                def a2(n):
                    i, hh, hk, tq, base = hd(n)
                    sb_ = Sb[hh]; st_ = stat[n % 8]; pm_ = Pm[n % 3]
                    op("act", lambda e: e.activation(out=pm_[:, 0:385], in_=sb_[:, 0:385], func=AF.Exp, bias=st_[:, 1:2], accum_out=st_[:, 2:3]), [sb_.b, st_.b], [pm_.b, st_.b])

                def a3(n):
                    st_ = stat[n % 8]; pm_ = Pm[n % 3]; ppt = pPT[n % 2]
                    op("dve", lambda e: e.reciprocal(out=st_[:, 3:4], in_=st_[:, 2:3]), [st_.b], [st_.b])

                    def trp(e):
                        for c in range(3):
                            ins = e.transpose(out=ppt[:, c * 128:(c + 1) * 128], in_=pm_[:, c * 128:(c + 1) * 128], identity=identb[:])
                        return ins
                    op("pe", trp, [pm_.b, identb.b], [ppt.b])

                def a4(n):
                    ppt = pPT[n % 2]; pt_ = PT[n % 3]
                    op("act", lambda e: e.copy(out=pt_[:], in_=ppt[:, 0:384]), [ppt.b], [pt_.b])

                def a5(n):
                    i, hh, hk, tq, base = hd(n)
                    pt_ = PT[n % 3]; po = pO[n % 2]

                    def mo(e):
                        for c in range(3):
                            ins = e.matmul(po[:, 0:64], lhsT=pt_[:, c * 128:(c + 1) * 128], rhs=V[:, i - 1 + c, hk * 64:(hk + 1) * 64], start=(c == 0), stop=(c == 2))
                        return ins
                    op("pe", mo, [pt_.b, V.bs[i - 1], V.bs[i], V.bs[i + 1]], [po.b])

                def a6(n):
                    i, hh, hk, tq, base = hd(n)
                    po = pO[n % 2]; st_ = stat[n % 8]; ot = otok[i % 2]
                    op("act", lambda e: e.activation(out=ot[:, hh * 64:(hh + 1) * 64], in_=po[:, 0:64], func=AF.Copy, scale=st_[:, 3:4]), [po.b, st_.b], [ot.b])
                    if hh == 3 and i > 1:
                        epilogue(i - 1)
                    if n == len(iters) - 1:
                        epilogue(i)

                def epilogue(i):
                    ot = otok[i % 2]
                    qs0, _ = blk_info(i)
                    hb_ = h2b[i % 2]; oT_ = oT[i % 2]
                    for half in range(2):
                        pm = pmm[half]

                        def tro(e):
                            for kq in range(4):
                                k = half * 4 + kq
                                ins = e.transpose(out=pm[:, kq * 128:(kq + 1) * 128], in_=ot[:, k * 128:(k + 1) * 128], identity=ident[:])
                            return ins
                        op("pe", tro, [ot.b, ident.b], [pm.b])
                        if half == 0:
                            op("dve", lambda e: e.tensor_copy(out=oT_[:, 0:4, :].rearrange("p k t -> p (k t)"), in_=pm[:, :]), [pm.b], [oT_.b])
                        else:
                            op("act", lambda e: e.copy(out=oT_[:, 4:8, :].rearrange("p k t -> p (k t)"), in_=pm[:, :]), [pm.b], [oT_.b])
                    for o in range(8):
                        pm = pmm[nmm[0] % 2]; nmm[0] += 1

                        def mw(e):
                            for k in range(8):
                                ins = e.matmul(pm[:, 0:128], lhsT=wo[:, k, o * 128:(o + 1) * 128], rhs=oT_[:, k, :], start=(k == 0), stop=(k == 7))
                            return ins
                        op("pe", mw, list(wo.bs) + [oT_.b], [pm.b])
                        op("dve", lambda e: e.tensor_tensor(out=hb_[:, o, :], in0=pm[:, 0:128], in1=hb_[:, o, :], op=ALU.add), [pm.b, hb_.bs[o]], [hb_.bs[o]])
                    c_lo = 112 if i == 1 else 0
                    P.dma(hs3v[:, :, qs0 + c_lo:qs0 + 128], hb_[:, :, c_lo:128], reads=list(hb_.bs), writes=[hs3B])

                pipeline(len(iters), [a0, a1, a2, a3, a4, a5, a6])
            P.barrier()

        ffn_phase(1, hs3, hs3B, ys, ysB, True)
        P.barrier()
        block = top.enter_context(nc.Block())
        P.emit(block)
        print("plan: ops", P.nops, "waits", P.nwaits, "dmas", P.ndma, "sems", len(P.sems), flush=True)
    return nc


def core_inputs(inputs):
    meta = np.asarray(inputs["meta_tokens"], np.float32)
    xp = np.asarray(inputs["x_prompt"], np.float32)
    xsm = np.asarray(inputs["x_sample"], np.float32)
    wnames = ["norm_mix_g", "norm_ffn_g", "final_norm_g", "w_in_ab", "s5_lambda_re", "s5_lambda_im", "s5_log_dt", "s5_b_re", "s5_b_im",
              "s5_c_re", "s5_c_im", "s5_d", "w_glu", "b_glu", "lru_conv_w", "lru_conv_b", "lru_w_r", "lru_b_r", "lru_w_i", "lru_b_i",
              "lru_lambda", "w_out_ab", "w_qkv", "w_o", "attn_sink", "w_up", "ffn_conv_w", "ffn_conv_b", "w_down"]
    wts = {n: np.ascontiguousarray(np.asarray(inputs[n], np.float32)) for n in wnames}
    maps = []
    for c in range(8):
        X = np.zeros((T, D), np.float32)
        if c < 4:
            for k in range(4):
                X[k * SEG:k * SEG + 16] = meta
                X[k * SEG + 16:(k + 1) * SEG] = xp[4 * c + k]
            f = 0.0
        else:
            X[0:16] = meta
            X[16:16 + 8192] = xsm[c - 4]
            f = 1.0
        flags = np.tile(np.array([f, 1.0 - f, NEG * (1.0 - f), NEG * f], np.float32), (128, 1))
        m = {"xs": X, "flags": flags}
        m.update(wts)
        maps.append(m)
    return maps


_CACHE = {}


def kernel(**inputs):
    maps = core_inputs(inputs)
    if "nc" not in _CACHE:
        _CACHE["nc"] = build(debug=False)
    nc = _CACHE["nc"]
    res = run_bass_kernel_spmd(nc, maps, core_ids=list(range(8)))
    yp = np.empty((16, 2048, D), np.float32)
    ysm = np.empty((4, 8192, D), np.float32)
    for c in range(8):
        y = np.asarray(res.results[c]["ys"])
        if c < 4:
            for k in range(4):
                yp[4 * c + k] = y[k * SEG + 16:(k + 1) * SEG]
        else:
            ysm[c - 4] = y[16:16 + 8192]
    return (yp, ysm)
```
